# Optimizing a Trainium2 kernel written in Bass

```python
import jax, jax.numpy as jnp
from jax import lax
import numpy as np

D_MODEL = 1024
BATCH = 8
SEQ = 2048
DEPTH = 1
DEC_BATCH = 128
DEC_SEQ = 1
PAST_LEN = 16384
PAGE_SIZE = 128

D_A = D_MODEL
CHUNK = 128
N_GROUPS_A = D_A // CHUNK
GD_A = D_A // N_GROUPS_A
D_B = D_MODEL
POOL_WINDOWS = (2, 4, 8, 16)
N_GROUPS_B = len(POOL_WINDOWS)
GD_B = D_B // N_GROUPS_B
POOL_BUF = max(POOL_WINDOWS) - 1
PLE_DIM = 256
EPS = 1e-6
SPLITS = (D_A, 2 * D_A, 3 * D_A, 3 * D_A + D_B, 3 * D_A + 2 * D_B, 3 * D_A + 2 * D_B + D_MODEL)
D_IN = 3 * D_A + 2 * D_B + 2 * D_MODEL

kernel_name = "gated_chunk_gmlp_pool_hybrid_step"


def _rmsnorm(x, g):
    x32 = x.astype(jnp.float32)
    r = x32 * lax.rsqrt(jnp.mean(x32 * x32, axis=-1, keepdims=True) + EPS)
    return (r * g.astype(jnp.float32)).astype(x.dtype)


def _layernorm(x, g, b):
    x32 = x.astype(jnp.float32)
    mu = jnp.mean(x32, axis=-1, keepdims=True)
    xc = x32 - mu
    r = xc * lax.rsqrt(jnp.mean(xc * xc, axis=-1, keepdims=True) + EPS)
    return (r * g.astype(jnp.float32) + b.astype(jnp.float32)).astype(x.dtype)


def _chunk_spatial_mix(vn, w_s, b_s):
    bn, L, _ = vn.shape
    pad = (-L) % CHUNK
    vp = jnp.pad(vn, ((0, 0), (0, pad), (0, 0)))
    nc = (L + pad) // CHUNK
    vp = vp.reshape(bn, nc, CHUNK, N_GROUPS_A, GD_A)
    w = jnp.tril(w_s)
    s = jnp.einsum('hts,bcshd->bcthd', w, vp) + jnp.transpose(b_s)[None, None, :, :, None]
    return s.reshape(bn, nc * CHUNK, D_A)[:, :L]


def _pool_branch(xb, prev, pos0, w_lin, scale):
    bn, L, _ = xb.shape
    ext = jnp.concatenate([prev, xb], axis=1)
    e32 = ext.astype(jnp.float32)
    cs = jnp.concatenate([jnp.zeros_like(e32[:, :1]), jnp.cumsum(e32, axis=1)], axis=1)
    pos = pos0 + jnp.arange(L, dtype=jnp.int32)
    x32 = xb.astype(jnp.float32)
    outs = []
    for g, w in enumerate(POOL_WINDOWS):
        sl = slice(g * GD_B, (g + 1) * GD_B)
        win_sum = (cs[:, POOL_BUF + 1:POOL_BUF + 1 + L, sl]
                   - cs[:, POOL_BUF + 1 - w:POOL_BUF + 1 - w + L, sl])
        cnt = jnp.minimum(pos + 1, w).astype(jnp.float32)[None, :, None]
        outs.append(win_sum / cnt - x32[..., sl])
    pooled = jnp.stack(outs, axis=2).astype(xb.dtype)
    mixed = jnp.einsum('blgc,gcd->blgd', pooled, w_lin).reshape(bn, L, D_B) * scale
    return mixed, ext[:, -POOL_BUF:]


def _layer(h, p, prev_pool, pos0, pre_g, w_in, ln_g, ln_b, w_s, b_s, w_pool, pool_scale,
           w_pa, w_pb, w_out, post_g, w_ple, w_pg, ple_in_g, ple_out_g):
    L = h.shape[1]
    xn = _rmsnorm(h, pre_g)
    proj = jnp.einsum('bld,de->ble', xn, w_in)
    u, v, z_a, x_b, z_b, g_a, g_b = jnp.split(proj, SPLITS, axis=-1)
    u = jax.nn.gelu(u)
    vn = _layernorm(jax.nn.gelu(v), ln_g, ln_b)
    y_a = u * _chunk_spatial_mix(vn, w_s, b_s) * jax.nn.silu(z_a)
    pooled, new_pool = _pool_branch(x_b, prev_pool, pos0, w_pool, pool_scale)
    y_b = pooled * jax.nn.silu(z_b)
    m = (jax.nn.sigmoid(g_a) * jnp.einsum('ble,ed->bld', y_a, w_pa)
         + jax.nn.sigmoid(g_b) * jnp.einsum('ble,ed->bld', y_b, w_pb))
    h = h + _rmsnorm(jnp.einsum('bld,de->ble', m, w_out), post_g)
    e = jnp.einsum('blp,pd->bld', p, w_ple)
    gate = jax.nn.sigmoid(jnp.einsum('bld,de->ble', _rmsnorm(h, ple_in_g), w_pg))
    h = h + _rmsnorm(gate * e, ple_out_g)
    start = ((L - 1) // CHUNK) * CHUNK
    return h, vn[:, start:], new_pool


def setup_inputs(seed: int = 0) -> dict:
    key = jax.random.key(seed)
    ks = jax.random.split(key, 32)
    f32 = jnp.float32
    nrm = lambda k, shape, s: jax.random.normal(k, shape, f32) * s
    return {
        "x_prompt": nrm(ks[0], (BATCH, SEQ, D_MODEL), 1.0),
        "x_sample": nrm(ks[1], (DEC_BATCH, DEC_SEQ, D_MODEL), 1.0),
        "state_pool": nrm(ks[2], (DEPTH, DEC_BATCH, POOL_BUF, D_B), 1.0),
        "p_prompt": nrm(ks[3], (DEPTH, BATCH, SEQ, PLE_DIM), 1.0),
        "p_sample": nrm(ks[4], (DEPTH, DEC_BATCH, DEC_SEQ, PLE_DIM), 1.0),
        "pre_g": 1.0 + nrm(ks[5], (DEPTH, D_MODEL), 0.02),
        "w_in": nrm(ks[6], (DEPTH, D_MODEL, D_IN), D_MODEL ** -0.5),
        "ln_g": 1.0 + nrm(ks[7], (DEPTH, D_A), 0.02),
        "ln_b": nrm(ks[8], (DEPTH, D_A), 0.02),
        "w_s": nrm(ks[9], (DEPTH, N_GROUPS_A, CHUNK, CHUNK), CHUNK ** -0.5),
        "b_s": 1.0 + nrm(ks[10], (DEPTH, N_GROUPS_A, CHUNK), 0.02),
        "w_pool": nrm(ks[11], (DEPTH, N_GROUPS_B, GD_B, GD_B), GD_B ** -0.5),
        "pool_scale": 1.0 + nrm(ks[12], (DEPTH, D_B), 0.02),
        "w_pa": nrm(ks[13], (DEPTH, D_A, D_MODEL), D_A ** -0.5),
        "w_pb": nrm(ks[14], (DEPTH, D_B, D_MODEL), D_B ** -0.5),
        "w_out": nrm(ks[15], (DEPTH, D_MODEL, D_MODEL), D_MODEL ** -0.5),
        "post_g": 1.0 + nrm(ks[16], (DEPTH, D_MODEL), 0.02),
        "w_ple": nrm(ks[17], (DEPTH, PLE_DIM, D_MODEL), PLE_DIM ** -0.5),
        "w_pg": nrm(ks[18], (DEPTH, D_MODEL, D_MODEL), D_MODEL ** -0.5),
        "ple_in_g": 1.0 + nrm(ks[19], (DEPTH, D_MODEL), 0.02),
        "ple_out_g": 1.0 + nrm(ks[20], (DEPTH, D_MODEL), 0.02),
    }


def reference(x_prompt, x_sample, state_pool, p_prompt, p_sample, pre_g, w_in, ln_g, ln_b,
              w_s, b_s, w_pool, pool_scale, w_pa, w_pb, w_out, post_g, w_ple, w_pg,
              ple_in_g, ple_out_g):
    hp, hs = x_prompt, x_sample
    pv_list, pp_list, sv_list, sp_list = [], [], [], []
    for i in range(DEPTH):
        lw = (pre_g[i], w_in[i], ln_g[i], ln_b[i], w_s[i], b_s[i], w_pool[i], pool_scale[i],
              w_pa[i], w_pb[i], w_out[i], post_g[i], w_ple[i], w_pg[i], ple_in_g[i], ple_out_g[i])
        zero_prev = jnp.zeros((hp.shape[0], POOL_BUF, D_B), hp.dtype)
        hp, pv, pp = _layer(hp, p_prompt[i], zero_prev, 0, *lw)
        hs, sv, sp = _layer(hs, p_sample[i], state_pool[i], PAST_LEN, *lw)
        pv_list.append(pv); pp_list.append(pp); sv_list.append(sv); sp_list.append(sp)
    prompt_chunk_v = jnp.stack(pv_list)
    prompt_pool = jnp.stack(pp_list)
    sample_chunk_v = jnp.stack(sv_list)
    sample_pool = jnp.stack(sp_list)
    return (hp, hs, prompt_chunk_v, prompt_pool, sample_chunk_v, sample_pool)
```

```python
import contextlib
import numpy as np
import concourse.bass as bass
import concourse.mybir as mybir
from concourse.bass_utils import run_bass_kernel_spmd

F32 = mybir.dt.float32
BF16 = mybir.dt.bfloat16
I32 = mybir.dt.int32
AF = mybir.ActivationFunctionType
ALU = mybir.AluOpType

NCORES = 8
D = 1024
SEQ = 2048
NS_TOK = 16
POOL_BUF = 15
WINDOWS = (2, 4, 8, 16)
EPS = 1e-6
MAGIC = 0x5F3759DF
NPASS = 4
SAMPLE_PASS = 0
NRING = 4
USE_SCRATCH = False
NEWTON_ITERS = 2
POOL_ADDS = True

ENGS = ("pe", "act", "dve", "pool", "sp")


class Prog:
    def __init__(self, nc):
        self.nc = nc
        self.ops = {e: [] for e in ENGS}
        self.cnt = {e: 0 for e in ENGS}
        self.buf = {}
        self.waited = {e: {} for e in ENGS}
        self.dcount = {}
        self.final_tokens = []
        self.stage = "setup"
        self.pe_labels = []

    def _deps(self, eng, reads, writes):
        need = {}

        def add(tok, raw):
            if tok is None:
                return
            s, v = tok
            if s == eng and eng == "pe":
                return
            if need.get(s, 0) < v:
                need[s] = v

        for k in reads:
            b = self.buf.get(k)
            if b is not None:
                add(b[0], True)
                if k.startswith("ps"):
                    for t in b[1]:
                        if t[0] != eng:
                            add(t, False)
        for k in writes:
            b = self.buf.get(k)
            if b is not None:
                add(b[0], False)
                for t in b[1]:
                    add(t, False)
        waits = []
        wd = self.waited[eng]
        for s, v in need.items():
            if wd.get(s, 0) >= v:
                continue
            wd[s] = v
            waits.append((s, v))
        return waits

    def _commit(self, tok, reads, writes):
        for k in reads:
            self.buf.setdefault(k, [None, []])[1].append(tok)
        for k in writes:
            self.buf[k] = [tok, []]

    def op(self, eng, fn, reads=(), writes=(), sig=True):
        waits = self._deps(eng, reads, writes)
        if sig:
            self.cnt[eng] += 1
            tok = (eng, self.cnt[eng])
        else:
            tok = (eng, self.cnt[eng] + 1)
        self.ops[eng].append((fn, waits, eng if sig else None, 1))
        if eng == "pe":
            self.pe_labels.append(self.stage)
        self._commit(tok, reads, writes)
        return tok

    def dma(self, eng, sem, fn, reads=(), writes=()):
        waits = self._deps(eng, reads, writes)
        self.dcount[sem] = self.dcount.get(sem, 0) + 16
        tok = (sem, self.dcount[sem])
        self.ops[eng].append((fn, waits, sem, 16))
        self._commit(tok, reads, writes)
        return tok

    def finish(self, eng, toks):
        self.final_tokens.append((eng, list(toks)))

    def emit(self):
        nc = self.nc
        names = list(ENGS) + sorted(self.dcount.keys())
        with contextlib.ExitStack() as st:
            sems = {n: st.enter_context(nc.semaphore("s_" + n)) for n in names}
            finals = {e: {} for e in ENGS}
            for e, toks in self.final_tokens:
                for (s, v) in toks:
                    finals[e][s] = max(finals[e].get(s, 0), v)

            def run(e):
                def body(engine):
                    for fn, waits, sg, inc in self.ops[e]:
                        for (s, v) in waits:
                            engine.wait_ge(sems[s], v)
                        ins = fn(engine)
                        if sg is not None:
                            ins.then_inc(sems[sg], inc)
                    for s, v in finals[e].items():
                        engine.wait_ge(sems[s], v)
                return body

            with nc.Block() as block:
                block.sync(run("sp"))
                block.scalar(run("act"))
                block.vector(run("dve"))
                block.gpsimd(run("pool"))
                block.tensor(run("pe"))


def _bf16_round(a):
    a = np.ascontiguousarray(a, dtype=np.float32)
    u = a.view(np.uint32).astype(np.uint64)
    r = ((u + 0x7FFF + ((u >> 16) & 1)) & 0xFFFF0000).astype(np.uint32)
    return r.view(np.float32)


def _constants():
    c = {}
    c["ident"] = np.eye(128, dtype=np.float32)
    t = np.arange(128)
    c["trilmask"] = (t[None, :] <= t[:, None]).astype(np.float32)
    pcur = np.zeros((128, 4, 128), np.float32)
    pfirst = np.zeros((128, 4, 128), np.float32)
    pprev = np.zeros((128, 4, 16), np.float32)
    for g, w in enumerate(WINDOWS):
        for tt in range(128):
            for j in range(w):
                s = tt - j
                if s >= 0:
                    pcur[s, g, tt] += 1.0 / w
                    pfirst[s, g, tt] += 1.0 / min(tt + 1, w)
                elif tt < 16:
                    pprev[128 + s, g, tt] += 1.0 / w
            pcur[tt, g, tt] -= 1.0
            pfirst[tt, g, tt] -= 1.0
    c["pcur"] = pcur
    hi = _bf16_round(pfirst)
    c["pfirst_hi"] = hi
    c["pfirst_lo"] = (pfirst - hi).astype(np.float32)
    c["pprev"] = pprev
    pss = np.zeros((128, 2, 4, 16), np.float32)
    psx = np.zeros((16, 4, 16), np.float32)
    for g, w in enumerate(WINDOWS):
        for b in range(16):
            psx[b, g, b] = 1.0 / w - 1.0
            for r in range(POOL_BUF):
                if r >= 16 - w:
                    row = b * POOL_BUF + r
                    pss[row % 128, row // 128, g, b] = 1.0 / w
    c["pss"] = pss
    c["psx"] = psx
    return c


CONST_SHAPES = {
    "ident": [128, 128], "trilmask": [128, 128], "pcur": [128, 4, 128], "pfirst_hi": [128, 4, 128],
    "pfirst_lo": [128, 4, 128], "pprev": [128, 4, 16], "pss": [128, 2, 4, 16], "psx": [16, 4, 16],
}


class Grp:
    pass


def build_program(stop=None):
    nc = bass.Bass("TRN2", target_bir_lowering=False)

    def din(name, shape):
        return nc.dram_tensor(name, list(shape), F32, kind="ExternalInput").ap()

    def dout(name, shape):
        return nc.dram_tensor(name, list(shape), F32, kind="ExternalOutput").ap()

    x_d = din("x", [SEQ, D]); xs_d = din("xs", [NS_TOK, D]); sp_d = din("sp", [NS_TOK, POOL_BUF, D])
    pp_d = din("pp", [SEQ, 256]); psm_d = din("psm", [NS_TOK, 256])
    pre_g_d = din("pre_g", [1, D]); w_in_d = din("w_in", [D, 7 * D]); ln_g_d = din("ln_g", [1, D]); ln_b_d = din("ln_b", [1, D])
    w_s_d = din("w_s", [8, 128, 128]); b_s_d = din("b_s", [1, D]); w_pool_d = din("w_pool", [4, 256, 256])
    pool_scale_d = din("pool_scale", [1, D]); w_pa_d = din("w_pa", [D, D]); w_pb_d = din("w_pb", [D, D])
    w_out_d = din("w_out", [D, D]); post_g_d = din("post_g", [1, D]); w_ple_d = din("w_ple", [256, D])
    w_pg_d = din("w_pg", [D, D]); ple_in_g_d = din("ple_in_g", [1, D]); ple_out_g_d = din("ple_out_g", [1, D])
    cst = {k: din("c_" + k, v) for k, v in CONST_SHAPES.items()}

    y_d = dout("y", [SEQ, D]); ys_d = dout("ys", [NS_TOK, D]); pcv_d = dout("pcv", [128, D]); ppool_d = dout("ppool", [POOL_BUF, D])
    scv_d = dout("scv", [NS_TOK, D]); spool_d = dout("spool", [NS_TOK, POOL_BUF, D])

    with contextlib.ExitStack() as stack:
        def sb(name, shape, dt):
            return stack.enter_context(nc.sbuf_tensor(name, list(shape), dt))

        ring = [sb(f"ring{i}", [128, 8, 512], BF16) for i in range(NRING)]
        wpool_sb = sb("wpool_sb", [128, 4, 2, 256], BF16)
        wple_sb = sb("wple_sb", [128, 2, 1024], BF16)
        trilWT = sb("trilWT", [128, 8, 128], BF16)
        identb = sb("identb", [128, 128], BF16)
        pcur_sb = sb("pcur_sb", [128, 4, 128], BF16)
        pfh_sb = sb("pfh_sb", [128, 4, 128], BF16)
        pfl_sb = sb("pfl_sb", [128, 4, 128], BF16)
        pprev_sb = sb("pprev_sb", [128, 4, 16], BF16)
        pss_sb = sb("pss_sb", [128, 2, 4, 16], BF16)
        psx_sb = sb("psx_sb", [16, 4, 16], BF16)
        wd_sb = sb("wd_sb", [16, 8, 16], BF16)
        ws00 = sb("ws00", [16, 8], F32)
        bs2 = sb("bs2", [2, 1024], BF16)
        ones2 = sb("ones2", [2, 128], BF16)
        ln_g_t = sb("ln_g_t", [128, D], F32); ln_b_t = sb("ln_b_t", [128, D], F32)
        post_g_t = sb("post_g_t", [128, D], F32); ple_out_g_t = sb("ple_out_g_t", [128, D], F32)
        gvec = sb("gvec", [24, 128], F32)
        identf24 = sb("identf24", [24, 24], F32)
        gT3 = sb("gT3", [128, 24], F32)
        pre_gT, ple_in_gT, pscaleT = gT3[:, 0:8], gT3[:, 8:16], gT3[:, 16:24]
        st_bf = sb("st_bf", [128, 2, 1024], BF16)
        NSTAT = 12
        stats = [sb(f"stat{i}", [128, 16], F32) for i in range(NSTAT)]
        xin = [sb(f"xin{i}", [128, D], F32) for i in range(2)]
        hres = [sb(f"hres{i}", [128, D], F32) for i in range(5)]
        tmp = [sb(f"tmp{i}", [128, D], F32) for i in range(5)]
        xn = [sb(f"xn{i}", [128, D], BF16) for i in range(5)]
        szt = [sb(f"szt{i}", [128, 512], BF16) for i in range(2)]
        m2t = [sb(f"m2t{i}", [128, 512], F32) for i in range(2)]
        junk = m2t[0].bitcast(BF16)
        xn2T = [sb(f"xn2T{i}", [128, 8, 128], BF16) for i in range(2)]
        pbf = [sb(f"pbf{i}", [128, 256], BF16) for i in range(5)]
        pT = [sb(f"pT{i}", [128, 2, 128], BF16) for i in range(5)]

        def mkgrp(name, T, nch):
            G = Grp()
            G.name, G.T, G.nch, G.TT = name, T, nch, T * nch
            G.xT = sb(name + "_xT", [128, 8, G.TT], BF16)
            G.u = sb(name + "_u", [128, 8, G.TT], BF16)
            G.szb = sb(name + "_szb", [128, 8, G.TT], BF16)
            G.sgb = sb(name + "_sgb", [128, 8, G.TT], BF16)
            G.qT = sb(name + "_qT", [128, 8, G.TT], BF16)
            G.sga = G.qT
            G.vn = sb(name + "_vn", [128, nch, D], BF16)
            G.xo = 1 if nch > 1 else 0
            G.xb = sb(name + "_xb", [128, nch + G.xo, D], BF16)
            return G

        GP = mkgrp("P", 128, 4)
        GS = mkgrp("S", NS_TOK, 1)

        psd = [stack.enter_context(nc.psum_tensor(f"psd{i}", [128, 1024], F32)) for i in range(4)]
        psb = [psd[i // 2][:, (i % 2) * 512:(i % 2 + 1) * 512] for i in range(8)]
        hist = [sb(f"hist{i}", [128, 16], F32) for i in range(12)]
        pro = sb("pro", [128, 32], F32)
        lnst = sb("lnst", [128, 72], F32)

        p = Prog(nc)
        rot = {}

        def nxt(name, n):
            i = rot.get(name, 0)
            rot[name] = i + 1
            return i % n

        def ps_next(pool=None):
            if pool is None:
                i = nxt("ps", 8)
            else:
                lo, n = pool
                i = lo + nxt(f"ps_{lo}_{n}", n)
            return psb[i], f"ps{i}"

        def stat_next():
            i = nxt("stat", NSTAT)
            return stats[i], f"stat{i}"

        def pool_next(name, lst):
            i = nxt(name, len(lst))
            return lst[i], f"{name}{i}"

        def mm(out, lhsT, rhs, start, stop, reads, writes, sig):
            p.op("pe", lambda e: e.matmul(out, lhsT=lhsT, rhs=rhs, start=start, stop=stop), reads, writes, sig)

        def tr(out, in_, ident, reads, writes, sig):
            p.op("pe", lambda e: e.transpose(out=out, in_=in_, identity=ident), reads, writes, sig)

        def act(out, in_, func, reads, writes, scale=None, accum=None):
            def fn(e):
                kw = {}
                if scale is not None:
                    kw["scale"] = scale
                if accum is not None:
                    kw["accum_out"] = accum
                return e.activation(out=out, in_=in_, func=func, **kw)
            p.op("act", fn, reads, writes)

        def ve(eng, name, reads, writes, **kw):
            p.op(eng, lambda e: getattr(e, name)(**kw), reads, writes)

        def dve(name, reads, writes, **kw):
            ve("dve", name, reads, writes, **kw)

        def dma(eng, semkey, out, in_, reads, writes, slow=False):
            if slow:
                return p.dma(eng, semkey, lambda e: e.dma_start(out=out, in_=in_, allow_slow_non_contiguous=True), reads, writes)
            return p.dma(eng, semkey, lambda e: e.dma_start(out=out, in_=in_), reads, writes)

        SQ_SCALE = 1.0 / 32.0

        def newton(stt, sk, T, xcol):
            X = stt[:T, xcol:xcol + 1]
            Y = stt[:T, 4:5]
            HX = stt[:T, 5:6]
            B = stt[:T, 6:7]
            dve("tensor_scalar", [sk], [sk], out=Y.bitcast(I32), in0=X.bitcast(I32), scalar1=-0.5, scalar2=float(MAGIC),
                op0=ALU.mult, op1=ALU.add)
            dve("tensor_scalar", [sk], [sk], out=HX, in0=X, scalar1=EPS, scalar2=-0.5, op0=ALU.add, op1=ALU.mult)
            for _ in range(NEWTON_ITERS):
                dve("scalar_tensor_tensor", [sk], [sk], out=B, in0=Y, scalar=HX, in1=Y, op0=ALU.mult, op1=ALU.mult)
                dve("scalar_tensor_tensor", [sk], [sk], out=Y, in0=B, scalar=1.5, in1=Y, op0=ALU.add, op1=ALU.mult)
            return Y

        def rms_rstd(stt, sk, T, nparts):
            assert nparts == 1
            return newton(stt, sk, T, 0)

        def wsrc(w_ap, c0):
            return w_ap.rearrange("(k p) n -> p k n", p=128)[:, :, c0:c0 + 512]

        pass_blocks = []
        for c0 in (3072, 3584, 1024, 1536, 0, 512, 2048, 2560, 4096, 4608, 5120, 5632):
            pass_blocks.append(wsrc(w_in_d, c0))
        pass_blocks += [wsrc(w_pa_d, 0), wsrc(w_pa_d, 512)]
        pass_blocks += [wsrc(w_in_d, 6144), wsrc(w_in_d, 6656)]
        pass_blocks += [wsrc(w_pb_d, 0), wsrc(w_pb_d, 512), wsrc(w_out_d, 0), wsrc(w_out_d, 512), wsrc(w_pg_d, 0), wsrc(w_pg_d, 512)]
        NBLK = len(pass_blocks)
        n_loads = NBLK * NPASS
        wscr = nc.dram_tensor("wscr", [NBLK, 128, 8 * 512], BF16).ap()
        rs = {"issued": 0, "acq": 0, "done": 0}

        def ring_pump():
            while rs["issued"] < n_loads and rs["issued"] - rs["done"] < NRING:
                n = rs["issued"]
                s = n % NRING
                if n < NBLK or not USE_SCRATCH:
                    dma("pool", f"d_ring{s}", ring[s][:], pass_blocks[n % NBLK], [], [f"ring{s}"])
                else:
                    dma("pool", f"d_ring{s}", ring[s][:].rearrange("p k n -> p (k n)"), wscr[n % NBLK], [f"scr{n % NBLK}"], [f"ring{s}"])
                rs["issued"] += 1

        def ring_acquire():
            n = rs["acq"]
            assert n < rs["issued"], "ring underflow"
            rs["acq"] += 1
            s = n % NRING
            return ring[s], f"ring{s}"

        def ring_release(k=1):
            for _ in range(k):
                n = rs["done"]
                s_ = n % NRING
                if USE_SCRATCH and n < NBLK and NPASS > 1:
                    dma("pool", f"s_scr{s_}", wscr[n], ring[s_][:].rearrange("p k n -> p (k n)"), [f"ring{s_}"], [f"scr{n}"])
                rs["done"] += 1
            ring_pump()

        def cdma(eng, key, out, in_, slow=False):
            return dma(eng, "d_c_" + key, out, in_, [], [key], slow=slow)

        def setup_early():
            cdma("pool", "identb", identb[:], cst["ident"])
            dma("act", "d_c_gvec", gvec[0:8, :], pre_g_d.rearrange("o (k p) -> (o k) p", p=128), [], ["gvec"])
            dma("act", "d_c_gvec", gvec[8:16, :], ple_in_g_d.rearrange("o (k p) -> (o k) p", p=128), [], ["gvec"])
            dma("act", "d_c_gvec", gvec[16:24, :], pool_scale_d.rearrange("o (k p) -> (o k) p", p=128), [], ["gvec"])
            cdma("act", "identf24", identf24[:], cst["ident"][0:24, 0:24])
            cdma("pool", "pcur", pcur_sb[:], cst["pcur"])
            cdma("pool", "pfh", pfh_sb[:], cst["pfirst_hi"])
            cdma("pool", "pfl", pfl_sb[:], cst["pfirst_lo"])
            cdma("pool", "pprev", pprev_sb[:], cst["pprev"])
            cdma("pool", "pss", pss_sb[:], cst["pss"])
            cdma("pool", "psx", psx_sb[:], cst["psx"])
            spf = sp_d.rearrange("b r c -> (b r) c")
            dma("pool", "d_c_st_bf", st_bf[:, 0, :], spf[0:128, :], [], ["st_bf"])
            dma("pool", "d_c_st_bf", st_bf[0:112, 1, :], spf[128:240, :], [], ["st_bf"])
            ring_pump()
            bk0, bk0k = ps_next()
            tr(bk0[:, 0:24], gvec[:, :], identf24[:, :], ["gvec", "identf24"], [bk0k], True)
            dve("tensor_copy", [bk0k], ["pre_gT", "ple_in_gT", "pscaleT"], out=gT3[:, :], in_=bk0[:, 0:24])
            for i in range(12):
                dve("memset", [], [f"hist{i}"], ap=hist[i][:], constant=1.0)
            dve("memset", [], ["lnst"], ap=lnst[:], constant=1.0)

        def setup_ln():
            cdma("sp", "ln_g", ln_g_t[:], ln_g_d.broadcast_to([128, D]))
            cdma("sp", "ln_b", ln_b_t[:], ln_b_d.broadcast_to([128, D]))

        def setup_late():
            cdma("sp", "ws00", ws00[:], bass.AP(w_s_d.tensor, 0, [[0, 16], [128 * 128, 8]]), slow=True)

        def setup_late2():
            cdma("pool", "wpool", wpool_sb[:], w_pool_d.rearrange("g (kk p) d -> p g kk d", p=128))

        def setup_late3():
            cdma("sp", "post_g", post_g_t[:], post_g_d.broadcast_to([128, D]))
            cdma("sp", "ple_out_g", ple_out_g_t[:], ple_out_g_d.broadcast_to([128, D]))
            cdma("pool", "wple", wple_sb[:], w_ple_d.rearrange("(kk p) d -> p kk d", p=128))

        def spool_copy():
            store_toks.append(dma("sp", "s_spool", spool_d[:, 0:POOL_BUF - 1, :], sp_d[:, 1:POOL_BUF, :], [], []))

        def setup_stage_dma():
            dma("sp", "s_stg0", tmp[0][:].rearrange("p (h s) -> p h s", h=8), w_s_d.rearrange("h t s -> t h s"), [], ["tmp0"])
            dma("sp", "s_stg1", tmp[1][:, 0:128], cst["trilmask"], [], ["tmp1"])
            dma("sp", "s_stg1", tmp[1][:, 128:256], cst["ident"], [], ["tmp1"])
            dma("sp", "s_stg2", tmp[2][0:2, :], b_s_d.broadcast_to([2, D]), [], ["tmp2"])

        def setup_compute():
            wsb = xn[0]
            dve("tensor_tensor", ["tmp0", "tmp1"], ["xn0"], out=wsb[:].rearrange("p (h s) -> p h s", h=8),
                in0=tmp[0][:].rearrange("p (h s) -> p h s", h=8), in1=tmp[1][:, 0:128].unsqueeze(1).broadcast_to([128, 8, 128]), op=ALU.mult)
            bk, bkk = ps_next()
            bkb = bk.bitcast(BF16)
            for h in range(8):
                tr(bkb[:, h * 128:(h + 1) * 128], wsb[:, h * 128:(h + 1) * 128], identb[:], ["xn0", "identb"], [bkk], h == 7)
            dve("tensor_copy", [bkk], ["trilWT"], out=trilWT[:].rearrange("p h t -> p (h t)"), in_=bkb[:, :])
            for h in range(8):
                dve("tensor_scalar", ["tmp1", "ws00"], ["wd"], out=wd_sb[:, h, :], in0=tmp[1][0:16, 128:144], scalar1=ws00[:, h:h + 1],
                    scalar2=None, op0=ALU.mult)
            bsf, bsh, bshf = tmp[2][0:2, :], xn[1][0:2, :], hres[0][0:2, :]
            dve("tensor_copy", ["tmp2"], ["xn1"], out=bsh, in_=bsf)
            dve("tensor_copy", ["xn1"], ["hres0"], out=bshf, in_=bsh)
            dve("tensor_tensor", ["tmp2", "hres0"], ["hres0"], out=bshf, in0=bsf, in1=bshf, op=ALU.subtract)
            dve("tensor_copy", ["hres0"], ["bs2"], out=bs2[:], in_=bshf)
            dve("tensor_copy", ["xn1", "bs2"], ["bs2"], out=bs2[0:1, :], in_=bsh[0:1, :])
            dve("memset", [], ["ones2"], ap=ones2[:], constant=1.0)

        store_toks = []

        def gk(G, nm, i=None):
            if nm == "sga":
                nm = "qT"
            return f"{G.name}:{nm}" if i is None else f"{G.name}:{nm}.{i}"

        def x_rows(G, pi, c):
            if G.name == "P":
                r0 = (pi * 4 + c) * 128
                return x_d[r0:r0 + 128, :]
            return xs_d[:, :]

        def p_rows(G, pi, c):
            if G.name == "P":
                r0 = (pi * 4 + c) * 128
                return pp_d[r0:r0 + 128, :]
            return psm_d[:, :]

        def y_rows(G, pi, c):
            if G.name == "P":
                r0 = (pi * 4 + c) * 128
                return y_d[r0:r0 + 128, :]
            return ys_d[:, :]

        xnl = {}
        ptl = {}

        def _stage_A1(G, pi):
            T = G.T
            for c in range(G.nch):
                xt, xk = pool_next("xin", xin)
                dma("sp", "d_" + xk, xt[:T, :], x_rows(G, pi, c), [], [xk])
                stt, sk = stat_next()
                act(junk[:T, :], xt[:T, :], AF.Square, [xk], ["m2t0", sk], accum=stt[:T, 0:1], scale=SQ_SCALE)
                r = rms_rstd(stt, sk, T, 1)
                xnt, xnk = pool_next("xn", xn)
                dve("tensor_scalar", [xk, sk], [xnk], out=xnt[:T, :], in0=xt[:T, :], scalar1=r, scalar2=None, op0=ALU.mult)
                xnl[(G.name, pi, c)] = (xnt, xnk)

        def _stage_A1_batched(groups, pi):
            items = [(G, c) for G in groups for c in range(G.nch)]
            assert len(items) <= len(hres)
            dve("memset", [], ["pro"], ap=pro[:], constant=1.0)
            for i, (G, c) in enumerate(items):
                T = G.T
                xt, xk = hres[i], f"hres{i}"
                dma("sp", "d_" + xk, xt[:T, :], x_rows(G, pi, c), [], [xk])
                act(junk[:T, :], xt[:T, :], AF.Square, [xk], ["m2t0", "pro"], accum=pro[:T, i:i + 1], scale=SQ_SCALE)
            nI = len(items)
            X, Y, HX, B1 = pro[:, 0:nI], pro[:, 8:8 + nI], pro[:, 16:16 + nI], pro[:, 24:24 + nI]
            ks = ["pro"]
            dve("tensor_scalar", ks, ks, out=Y.bitcast(I32), in0=X.bitcast(I32), scalar1=-0.5, scalar2=float(MAGIC), op0=ALU.mult, op1=ALU.add)
            dve("tensor_scalar", ks, ks, out=HX, in0=X, scalar1=EPS, scalar2=-0.5, op0=ALU.add, op1=ALU.mult)
            for _ in range(NEWTON_ITERS):
                dve("tensor_tensor", ks, ks, out=B1, in0=Y, in1=Y, op=ALU.mult)
                dve("tensor_tensor", ks, ks, out=B1, in0=B1, in1=HX, op=ALU.mult)
                dve("scalar_tensor_tensor", ks, ks, out=Y, in0=B1, scalar=1.5, in1=Y, op0=ALU.add, op1=ALU.mult)
            for i, (G, c) in enumerate(items):
                T = G.T
                xnt, xnk = pool_next("xn", xn)
                dve("tensor_scalar", [f"hres{i}", "pro"], [xnk], out=xnt[:T, :], in0=hres[i][:T, :], scalar1=pro[:T, 8 + i:9 + i], scalar2=None,
                    op0=ALU.mult)
                xnl[(G.name, pi, c)] = (xnt, xnk)

        def _stage_A2(G, pi, chunks=None, pspool=None):
            T = G.T
            for c in (range(G.nch) if chunks is None else chunks):
                xnt, xnk = xnl.pop((G.name, pi, c))
                bk, bkk = ps_next(pspool)
                bkb = bk.bitcast(BF16)
                for k in range(8):
                    tr(bkb[:, k * T:(k + 1) * T], xnt[:T, k * 128:(k + 1) * 128], identb[:T, :T], [xnk, "identb"], [bkk], k == 7)
                dve("tensor_tensor", [bkk, "pre_gT"], [gk(G, "xT", c)], out=G.xT[:, :, c * T:(c + 1) * T],
                    in0=bkb[:, 0:8 * T].rearrange("p (k t) -> p k t", k=8), in1=pre_gT.unsqueeze(2).broadcast_to([128, 8, T]),
                    op=ALU.mult)

        def _stage_P1(G, pi):
            T = G.T
            for c in range(G.nch):
                pb, pbk = pool_next("pbf", pbf)
                dma("pool", "d_" + pbk, pb[:T, :], p_rows(G, pi, c), [], [pbk])
                ptl[(G.name, pi, c)] = (pb, pbk)

        def _stage_P2(G, pi, chunks=None, pspool=None):
            T = G.T
            for c in (range(G.nch) if chunks is None else chunks):
                pb, pbk = ptl.pop((G.name, pi, c))
                bk2, bk2k = ps_next(pspool)
                bk2b = bk2.bitcast(BF16)
                for kk in range(2):
                    tr(bk2b[:, kk * T:(kk + 1) * T], pb[:T, kk * 128:(kk + 1) * 128], identb[:T, :T], [pbk, "identb"], [bk2k], kk == 1)
                ptt, ptk = pool_next("pT", pT)
                dve("tensor_copy", [bk2k], [ptk], out=ptt[:, :, :T], in_=bk2b[:, 0:2 * T].rearrange("p (k t) -> p k t", k=2))
                ptl[(G.name, pi, c, "T")] = (ptt, ptk)

        def is_out_chunk(G, pi, c, kind="v"):
            return G.name == "S" or (pi == NPASS - 1 and c == G.nch - 1)

        bvs = {}

        def bv_begin():
            bvs["s"] = (ring_acquire(), ring_acquire())

        def bv_chunk(G, c, pi, pspool=None):
            (s0, s0k), (s1, s1k) = bvs["s"]
            T = G.T
            if True:
                if True:
                    vf, vk = pool_next("tmp", tmp)
                    stt, sk = stat_next()
                    for half, (sl, slk) in enumerate(((s0, s0k), (s1, s1k))):
                        bk, bkk = ps_next(pspool)
                        for k in range(8):
                            mm(bk[:T, :], G.xT[:, k, c * T:(c + 1) * T], sl[:, k, :], k == 0, k == 7, [gk(G, "xT", c), slk], [bkk], k == 7)
                        act(vf[:T, half * 512:(half + 1) * 512], bk[:T, :], AF.Gelu_apprx_tanh, [bkk], [vk, sk], accum=stt[:T, half:half + 1])
                    act(junk[:T, :], vf[:T, :], AF.Square, [vk], ["m2t0", sk], accum=stt[:T, 2:3], scale=SQ_SCALE)
                    dve("tensor_tensor", [sk], [sk], out=stt[:T, 7:8], in0=stt[:T, 0:1], in1=stt[:T, 1:2], op=ALU.add)
                    dve("tensor_scalar", [sk], [sk], out=stt[:T, 8:9], in0=stt[:T, 7:8], scalar1=1.0 / D, scalar2=None, op0=ALU.mult)
                    dve("tensor_tensor", [sk], [sk], out=stt[:T, 9:10], in0=stt[:T, 8:9], in1=stt[:T, 8:9], op=ALU.mult)
                    dve("tensor_tensor", [sk], [sk], out=stt[:T, 3:4], in0=stt[:T, 2:3], in1=stt[:T, 9:10], op=ALU.subtract)
                    r = newton(stt, sk, T, 3)
                    dve("scalar_tensor_tensor", [vk, sk, "ln_g"], [vk], out=vf[:T, :], in0=vf[:T, :], scalar=stt[:T, 8:9], in1=ln_g_t[:T, :],
                        op0=ALU.subtract, op1=ALU.mult)
                    if is_out_chunk(G, pi, c):
                        dve("scalar_tensor_tensor", [vk, sk, "ln_b"], [vk], out=vf[:T, :], in0=vf[:T, :], scalar=r, in1=ln_b_t[:T, :],
                            op0=ALU.mult, op1=ALU.add)
                        act(G.vn[:T, c, :], vf[:T, :], AF.Copy, [vk], [gk(G, "vn", c)])
                        dst = scv_d[:, :] if G.name == "S" else pcv_d[:, :]
                        store_toks.append(dma("sp", "s_" + vk, dst, vf[:T, :], [vk], []))
                    else:
                        dve("scalar_tensor_tensor", [vk, sk, "ln_b"], [gk(G, "vn", c)], out=G.vn[:T, c, :], in0=vf[:T, :], scalar=r,
                            in1=ln_b_t[:T, :], op0=ALU.mult, op1=ALU.add)

        def bv_all(groups, pi):
            (s0, s0k), (s1, s1k) = ring_acquire(), ring_acquire()
            items = [(G, c) for G in groups for c in range(G.nch)]
            n = len(items)
            assert n <= len(tmp) and n <= 8
            A0, A1, Q, MEAN, MSQ, X, Y, HX, B1 = [lnst[:, 8 * j:8 * j + n] for j in range(9)]
            lk = ["lnst"]
            vfs = []
            for i, (G, c) in enumerate(items):
                T = G.T
                vf, vk = pool_next("tmp", tmp)
                vfs.append((vf, vk))
                for half, (sl, slk) in enumerate(((s0, s0k), (s1, s1k))):
                    bk, bkk = ps_next()
                    for k in range(8):
                        mm(bk[:T, :], G.xT[:, k, c * T:(c + 1) * T], sl[:, k, :], k == 0, k == 7, [gk(G, "xT", c), slk], [bkk], k == 7)
                    act(vf[:T, half * 512:(half + 1) * 512], bk[:T, :], AF.Gelu_apprx_tanh, [bkk], [vk] + lk,
                        accum=lnst[:T, 8 * half + i:8 * half + i + 1])
                act(junk[:T, :], vf[:T, :], AF.Square, [vk], ["m2t0"] + lk, accum=lnst[:T, 16 + i:17 + i], scale=SQ_SCALE)
            ring_release(2)
            dve("tensor_tensor", lk, lk, out=MEAN, in0=A0, in1=A1, op=ALU.add)
            dve("tensor_scalar", lk, lk, out=MEAN, in0=MEAN, scalar1=1.0 / D, scalar2=None, op0=ALU.mult)
            dve("tensor_tensor", lk, lk, out=MSQ, in0=MEAN, in1=MEAN, op=ALU.mult)
            dve("tensor_tensor", lk, lk, out=X, in0=Q, in1=MSQ, op=ALU.subtract)
            dve("tensor_scalar", lk, lk, out=Y.bitcast(I32), in0=X.bitcast(I32), scalar1=-0.5, scalar2=float(MAGIC), op0=ALU.mult, op1=ALU.add)
            dve("tensor_scalar", lk, lk, out=HX, in0=X, scalar1=EPS, scalar2=-0.5, op0=ALU.add, op1=ALU.mult)
            for _ in range(NEWTON_ITERS):
                dve("tensor_tensor", lk, lk, out=B1, in0=Y, in1=Y, op=ALU.mult)
                dve("tensor_tensor", lk, lk, out=B1, in0=B1, in1=HX, op=ALU.mult)
                dve("scalar_tensor_tensor", lk, lk, out=Y, in0=B1, scalar=1.5, in1=Y, op0=ALU.add, op1=ALU.mult)
            for i, (G, c) in enumerate(items):
                T = G.T
                vf, vk = vfs[i]
                mean_i = lnst[:T, 24 + i:25 + i]
                r_i = lnst[:T, 48 + i:49 + i]
                dve("scalar_tensor_tensor", [vk, "ln_g"] + lk, [vk], out=vf[:T, :], in0=vf[:T, :], scalar=mean_i, in1=ln_g_t[:T, :],
                    op0=ALU.subtract, op1=ALU.mult)
                if is_out_chunk(G, pi, c):
                    dve("scalar_tensor_tensor", [vk, "ln_b"] + lk, [vk], out=vf[:T, :], in0=vf[:T, :], scalar=r_i, in1=ln_b_t[:T, :],
                        op0=ALU.mult, op1=ALU.add)
                    act(G.vn[:T, c, :], vf[:T, :], AF.Copy, [vk], [gk(G, "vn", c)])
                    dst = scv_d[:, :] if G.name == "S" else pcv_d[:, :]
                    store_toks.append(dma("sp", "s_" + vk, dst, vf[:T, :], [vk], []))
                else:
                    dve("scalar_tensor_tensor", [vk, "ln_b"] + lk, [gk(G, "vn", c)], out=G.vn[:T, c, :], in0=vf[:T, :], scalar=r_i,
                        in1=ln_b_t[:T, :], op0=ALU.mult, op1=ALU.add)

        def bv_end():
            ring_release(2)

        bxs = {}

        def bx_begin(groups, pi):
            bxs["s"] = (ring_acquire(), ring_acquire())
            for G in groups:
                if G.name == "P" and pi > 0:
                    dve("tensor_copy", [gk(G, "xb", 4)], [gk(G, "xb", 0)], out=G.xb[:, 0, :], in_=G.xb[:, 4, :])

        def bx_chunk(G, c, pi, pspool=None):
            (s0, s0k), (s1, s1k) = bxs["s"]
            T = G.T
            outc = is_out_chunk(G, pi, c, "x")
            if outc:
                xf, xfk = pool_next("tmp", tmp)
            for half, (sl, slk) in enumerate(((s0, s0k), (s1, s1k))):
                bk, bkk = ps_next(pspool)
                for k in range(8):
                    mm(bk[:T, :], G.xT[:, k, c * T:(c + 1) * T], sl[:, k, :], k == 0, k == 7, [gk(G, "xT", c), slk], [bkk], k == 7)
                act(G.xb[:T, G.xo + c, half * 512:(half + 1) * 512], bk[:T, :], AF.Copy, [bkk], [gk(G, "xb", 1 + c)])
                if outc:
                    dve("tensor_copy", [bkk], [xfk], out=xf[:T, half * 512:(half + 1) * 512], in_=bk[:T, :])
            if outc:
                if G.name == "S":
                    store_toks.append(dma("sp", "s_" + xfk, spool_d[:, POOL_BUF - 1, :], xf[:T, :], [xfk], []))
                else:
                    store_toks.append(dma("sp", "s_" + xfk, ppool_d[:, :], xf[128 - POOL_BUF:128, :], [xfk], []))

        def bx_end():
            ring_release(2)

        def fm_proj(groups, func, dst_fn, post_fn=None):
            for half in range(2):
                sl, slk = ring_acquire()
                for j in range(4):
                    jj = half * 4 + j
                    for G in groups:
                        bk, bkk = ps_next()
                        for k in range(8):
                            mm(bk[:, :G.TT], sl[:, k, j * 128:(j + 1) * 128], G.xT[:, k, :], k == 0, k == 7,
                               [slk] + [gk(G, "xT", c) for c in range(G.nch)], [bkk], k == 7)
                        out, okeys = dst_fn(G, jj)
                        act(out, bk[:, :G.TT], func, [bkk], okeys)
                        if post_fn is not None:
                            post_fn(G, jj)
                ring_release(1)

        def _stage_C(groups, mid_hook=None, after_za=None, after_zb=None):
            fm_proj(groups, AF.Gelu_apprx_tanh, lambda G, jj: (G.u[:, jj, :], [gk(G, "u", jj)]))
            if mid_hook is not None:
                mid_hook()
            zt = {}

            def dst_za(G, jj):
                t, tk = pool_next("szt", szt)
                zt[(G.name, jj)] = (t, tk)
                return t[:, :G.TT], [tk]

            def post_za(G, jj):
                t, tk = zt[(G.name, jj)]
                dve("tensor_tensor", [tk, gk(G, "u", jj)], [gk(G, "u", jj)], out=G.u[:, jj, :], in0=G.u[:, jj, :], in1=t[:, :G.TT], op=ALU.mult)

            fm_proj(groups, AF.Silu, dst_za, post_za)
            if after_za is not None:
                after_za()
            fm_proj(groups, AF.Silu, lambda G, jj: (G.szb[:, jj, :], [gk(G, "szb", jj)]))
            if after_zb is not None:
                after_zb()

        def _stage_D(groups, pi):
            for G in groups:
                T = G.T
                for jj in range(8):
                    g = jj // 2
                    cs = slice(jj * 128, (jj + 1) * 128)
                    bk, bkk = ps_next()
                    if G.name == "P":
                        for c in range(4):
                            o = bk[:, c * 128:(c + 1) * 128]
                            last = (c == 3)
                            if pi == 0 and c == 0:
                                mm(o, G.xb[:, 1, cs], pfh_sb[:, g, :], True, False, [gk(G, "xb", 1), "pfh"], [bkk], False)
                                mm(o, G.xb[:, 1, cs], pfl_sb[:, g, :], False, True, [gk(G, "xb", 1), "pfl"], [bkk], last)
                            else:
                                mm(o, G.xb[:, 1 + c, cs], pcur_sb[:, g, :], True, False, [gk(G, "xb", 1 + c), "pcur"], [bkk], False)
                                mm(bk[:, c * 128:c * 128 + 16], G.xb[64:128, c, cs], pprev_sb[64:128, g, :], False, True,
                                   [gk(G, "xb", c), "pprev"], [bkk], last)
                    else:
                        o = bk[:, 0:16]
                        mm(o, st_bf[:, 0, cs], pss_sb[:, 0, g, :], True, False, ["st_bf", "pss"], [bkk], False)
                        mm(o, st_bf[0:112, 1, cs], pss_sb[0:112, 1, g, :], False, False, ["st_bf", "pss"], [bkk], False)
                        mm(o, G.xb[0:16, G.xo, cs], psx_sb[:, g, :], False, True, [gk(G, "xb", 1), "psx"], [bkk], True)
                    dve("tensor_copy", [bkk], [gk(G, "qT", jj)], out=G.qT[:, jj, :], in_=bk[:, :G.TT])

        def _stage_E(groups):
            for G in groups:
                T = G.T
                for h in range(8):
                    bk, bkk = ps_next()
                    if G.name == "P":
                        brhs = bass.AP(bs2, h * 128, [[1024, 2], [0, 4], [1, 128]])
                    else:
                        brhs = bass.AP(bs2, h * 128, [[1024, 2], [0, 16]])
                    mm(bk[:, :G.TT], ones2[:, :], brhs, True, False, ["ones2", "bs2"], [bkk], False)
                    for c in range(G.nch):
                        rhs = trilWT[:, h, :] if G.name == "P" else wd_sb[:, h, :]
                        mm(bk[:, c * T:(c + 1) * T], G.vn[:T, c, h * 128:(h + 1) * 128], rhs, False, c == G.nch - 1,
                           [gk(G, "vn", c), "trilWT", "wd"], [bkk], c == G.nch - 1)
                    dve("tensor_tensor", [bkk, gk(G, "u", h)], [gk(G, "u", h)], out=G.u[:, h, :], in0=bk[:, :G.TT], in1=G.u[:, h, :], op=ALU.mult)

        def _stage_F(groups):
            for G in groups:
                for jd in range(8):
                    g, dd = jd // 2, jd % 2
                    bk, bkk = ps_next()
                    for kk in range(2):
                        mm(bk[:, :G.TT], wpool_sb[:, g, kk, dd * 128:(dd + 1) * 128], G.qT[:, 2 * g + kk, :], kk == 0, kk == 1,
                           ["wpool", gk(G, "qT", 2 * g + kk)], [bkk], kk == 1)
                    dve("scalar_tensor_tensor", [bkk, "pscaleT", gk(G, "szb", jd)], [gk(G, "szb", jd)], out=G.szb[:, jd, :], in0=bk[:, :G.TT],
                        scalar=pscaleT[:, jd:jd + 1], in1=G.szb[:, jd, :], op0=ALU.mult, op1=ALU.mult)

        def fm_mix(groups, src_fn, evac_fn):
            for half in range(2):
                sl, slk = ring_acquire()
                for j in range(4):
                    jj = half * 4 + j
                    for G in groups:
                        bk, bkk = ps_next()
                        for k in range(8):
                            src, skey = src_fn(G, k)
                            mm(bk[:, :G.TT], sl[:, k, j * 128:(j + 1) * 128], src, k == 0, k == 7, [slk, skey], [bkk], k == 7)
                        evac_fn(G, jj, bk, bkk)
                ring_release(1)

        def _stage_G(groups):
            fm_proj(groups, AF.Sigmoid, lambda G, jj: (G.sga[:, jj, :], [gk(G, "sga", jj)]))

            def evac_a(G, jj, bk, bkk):
                dve("tensor_tensor", [bkk, gk(G, "sga", jj)], [gk(G, "sga", jj)], out=G.sga[:, jj, :], in0=bk[:, :G.TT], in1=G.sga[:, jj, :], op=ALU.mult)

            fm_mix(groups, lambda G, k: (G.u[:, k, :], gk(G, "u", k)), evac_a)
            fm_proj(groups, AF.Sigmoid, lambda G, jj: (G.sgb[:, jj, :], [gk(G, "sgb", jj)]))

            def evac_b(G, jj, bk, bkk):
                t, tk = pool_next("m2t", m2t)
                dve("tensor_tensor", [bkk, gk(G, "sgb", jj)], [tk], out=t[:, :G.TT], in0=bk[:, :G.TT], in1=G.sgb[:, jj, :], op=ALU.mult)
                ve("pool" if (POOL_ADDS and jj < 6) else "dve", "tensor_tensor", [tk, gk(G, "sga", jj)], [gk(G, "sgb", jj)], out=G.sgb[:, jj, :],
                   in0=t[:, :G.TT], in1=G.sga[:, jj, :], op=ALU.add)

            fm_mix(groups, lambda G, k: (G.szb[:, k, :], gk(G, "szb", k)), evac_b)

        def _stage_HI(groups, pi, fill_hook=None, drain_hook=None):
            (s0, s0k), (s1, s1k) = ring_acquire(), ring_acquire()
            (g0, g0k), (g1, g1k) = ring_acquire(), ring_acquire()
            items = [(G, c) for G in groups for c in range(G.nch)]
            n = len(items)
            S = [dict() for _ in range(n)]
            PS_P, PS_I = (4, 1), (5, 3)
            addeng = "pool" if POOL_ADDS else "dve"
            nsteps = n + 5
            HB = [(hist[i], f"hist{i}") for i in range(nsteps + 1)]

            def chain(t):
                bt, bk_ = HB[t]
                X, Y, HX, B1 = bt[:, 0:3], bt[:, 4:7], bt[:, 8:11], bt[:, 12:15]
                dve("tensor_scalar", [bk_], [bk_], out=Y.bitcast(I32), in0=X.bitcast(I32), scalar1=-0.5, scalar2=float(MAGIC),
                    op0=ALU.mult, op1=ALU.add)
                dve("tensor_scalar", [bk_], [bk_], out=HX, in0=X, scalar1=EPS, scalar2=-0.5, op0=ALU.add, op1=ALU.mult)
                for _ in range(NEWTON_ITERS):
                    dve("tensor_tensor", [bk_], [bk_], out=B1, in0=Y, in1=Y, op=ALU.mult)
                    dve("tensor_tensor", [bk_], [bk_], out=B1, in0=B1, in1=HX, op=ALU.mult)
                    dve("scalar_tensor_tensor", [bk_], [bk_], out=Y, in0=B1, scalar=1.5, in1=Y, op0=ALU.add, op1=ALU.mult)

            def h_pe(i, t):
                G, c = items[i]; T = G.T; st = S[i]
                d = nxt("psH", 2)
                pd = psd[d]
                keys = [f"ps{2 * d}", f"ps{2 * d + 1}"]
                for half, (sl, slk) in enumerate(((s0, s0k), (s1, s1k))):
                    for k in range(8):
                        mm(pd[:T, half * 512:(half + 1) * 512], G.sgb[:, k, c * T:(c + 1) * T], sl[:, k, :], k == 0, k == 7,
                           [gk(G, "sgb", k), slk], [keys[half]], k == 7)
                nb, nbk = HB[t + 1]
                act(junk[:T, :], pd[:T, :], AF.Square, keys, ["m2t0", nbk], accum=nb[:T, 0:1], scale=SQ_SCALE)
                st["pd"], st["pkeys"] = pd, keys

            def h_d1(i, t):
                G, c = items[i]; T = G.T; st = S[i]
                bt, btk = HB[t]
                ht, hk = pool_next("hres", hres)
                st["ht"], st["hk"] = ht, hk
                dma("sp", "d_" + hk, ht[:T, :], x_rows(G, pi, c), [], [hk])
                t1, t1k = pool_next("tmp", tmp)
                dve("scalar_tensor_tensor", st["pkeys"] + [btk, "post_g"], [t1k], out=t1[:T, :], in0=st["pd"][:T, :], scalar=bt[:T, 4:5],
                    in1=post_g_t[:T, :], op0=ALU.mult, op1=ALU.mult)
                dve("tensor_tensor", [t1k, hk], [hk], out=ht[:T, :], in0=ht[:T, :], in1=t1[:T, :], op=ALU.add)

            def h_d1b(i, t):
                G, c = items[i]; T = G.T; st = S[i]
                ht, hk = st["ht"], st["hk"]
                nb, nbk = HB[t + 1]
                act(junk[:T, :], ht[:T, :], AF.Square, [hk], ["m2t0", nbk], accum=nb[:T, 1:2], scale=SQ_SCALE)

            def h_d2(i, t):
                G, c = items[i]; T = G.T; st = S[i]
                bt, btk = HB[t]
                xnt, xnk = pool_next("xn", xn)
                act(xnt[:T, :], st["ht"][:T, :], AF.Copy, [st["hk"], btk], [xnk], scale=bt[:T, 5:6])
                st["xnt"], st["xnk"] = xnt, xnk

            def i_prep(i, t):
                G, c = items[i]; T = G.T; st = S[i]
                xnt, xnk = st["xnt"], st["xnk"]
                bk, bkk = ps_next(PS_P)
                bkb = bk.bitcast(BF16)
                for k in range(8):
                    tr(bkb[:, k * T:(k + 1) * T], xnt[:T, k * 128:(k + 1) * 128], identb[:T, :T], [xnk, "identb"], [bkk], k == 7)
                x2, x2k = pool_next("xn2T", xn2T)
                dve("tensor_tensor", [bkk, "ple_in_gT"], [x2k], out=x2[:, :, :T], in0=bkb[:, 0:8 * T].rearrange("p (k t) -> p k t", k=8),
                    in1=ple_in_gT.unsqueeze(2).broadcast_to([128, 8, T]), op=ALU.mult)
                st["x2"], st["x2k"] = x2, x2k

            def i_half(i, half):
                G, c = items[i]; T = G.T; st = S[i]
                x2, x2k, gt, gtk = st["x2"], st["x2k"], st["gt"], st["gtk"]
                ptt, ptk = st["ptt"], st["ptk"]
                sl, slk = ((g0, g0k), (g1, g1k))[half]
                hs = slice(half * 512, (half + 1) * 512)
                bg, bgk = ps_next(PS_I)
                for k in range(8):
                    mm(bg[:T, :], x2[:, k, :T], sl[:, k, :], k == 0, k == 7, [x2k, slk], [bgk], k == 7)
                act(gt[:T, hs], bg[:T, :], AF.Sigmoid, [bgk], [gtk])
                be, bek = ps_next(PS_I)
                for kk in range(2):
                    mm(be[:T, :], ptt[:, kk, :T], wple_sb[:, kk, hs], kk == 0, kk == 1, [ptk, "wple"], [bek], kk == 1)
                dve("tensor_tensor", [bek, gtk], [gtk], out=gt[:T, hs], in0=be[:T, :], in1=gt[:T, hs], op=ALU.mult)

            def i_pe_a(i, t):
                G, c = items[i]; st = S[i]
                st["ptt"], st["ptk"] = ptl.pop((G.name, pi, c, "T"))
                st["gt"], st["gtk"] = pool_next("tmp", tmp)
                i_half(i, 0)

            def i_pe_b(i, t):
                G, c = items[i]; T = G.T; st = S[i]
                i_half(i, 1)
                nb, nbk = HB[t + 1]
                act(junk[:T, :], st["gt"][:T, :], AF.Square, [st["gtk"]], ["m2t0", nbk], accum=nb[:T, 2:3], scale=SQ_SCALE)

            def i_fin(i, t):
                G, c = items[i]; T = G.T; st = S[i]
                bt, btk = HB[t]
                gt, gtk = st["gt"], st["gtk"]
                last = (pi == NPASS - 1 and i == n - 1)
                if last:
                    dve("scalar_tensor_tensor", [gtk, btk, "ple_out_g"], [gtk], out=gt[:T, :], in0=gt[:T, :], scalar=bt[:T, 6:7],
                        in1=ple_out_g_t[:T, :], op0=ALU.mult, op1=ALU.mult)
                    dve("tensor_tensor", [gtk, st["hk"]], [gtk], out=gt[:T, :], in0=gt[:T, :], in1=st["ht"][:T, :], op=ALU.add)
                elif POOL_ADDS:
                    act(gt[:T, :], gt[:T, :], AF.Copy, [gtk, btk], [gtk], scale=bt[:T, 6:7])
                    ve("pool", "tensor_tensor", [gtk, "ple_out_g"], [gtk], out=gt[:T, :], in0=gt[:T, :], in1=ple_out_g_t[:T, :], op=ALU.mult)
                else:
                    dve("scalar_tensor_tensor", [gtk, btk, "ple_out_g"], [gtk], out=gt[:T, :], in0=gt[:T, :], scalar=bt[:T, 6:7],
                        in1=ple_out_g_t[:T, :], op0=ALU.mult, op1=ALU.mult)
                if not last:
                    ve(addeng, "tensor_tensor", [gtk, st["hk"]], [gtk], out=gt[:T, :], in0=gt[:T, :], in1=st["ht"][:T, :], op=ALU.add)
                store_toks.append(dma("sp", "s_" + gtk, y_rows(G, pi, c), gt[:T, :], [gtk], []))

            seq = [(h_pe, 0), (None, None), (h_d1, 1), (h_d2, 2), (i_fin, 5), (i_prep, 3), (i_pe_a, 4), (h_d1b, 1), (i_pe_b, 4)]
            uses_chain = (1, 2, 5)
            for t in range(nsteps):
                for fn, lag in seq:
                    if fn is None:
                        if any(0 <= t - sj < n for sj in uses_chain):
                            chain(t)
                        continue
                    i = t - lag
                    if 0 <= i < n:
                        fn(i, t)
                if t < n and fill_hook is not None:
                    fill_hook(t)
                if t >= n and drain_hook is not None:
                    drain_hook(t - n)
                if t == n - 1:
                    ring_release(2)
                if t == n + 3:
                    ring_release(2)
            if drain_hook is not None:
                drain_hook(None)

        def _wrap(fn, label):
            def w(*a, **k):
                old = p.stage
                p.stage = label
                try:
                    return fn(*a, **k)
                finally:
                    p.stage = old
            return w

        stage_A1 = _wrap(_stage_A1, "A1")
        stage_A2 = _wrap(_stage_A2, "A2")
        stage_P1 = _wrap(_stage_P1, "P1")
        stage_P2 = _wrap(_stage_P2, "P2")
        stage_C = _wrap(_stage_C, "C")
        stage_D = _wrap(_stage_D, "D")
        stage_E = _wrap(_stage_E, "E")
        stage_F = _wrap(_stage_F, "F")
        stage_G = _wrap(_stage_G, "G")
        stage_HI = _wrap(_stage_HI, "HI")

        def groups_of(pi):
            return [GP] + ([GS] if pi == SAMPLE_PASS else [])

        def schedule():
            setup_early()
            p.stage = "A1"
            _stage_A1_batched(groups_of(0), 0)
            setup_ln()
            for G in groups_of(0):
                stage_A2(G, 0)
            p.stage = "B"
            bx_begin(groups_of(0), 0)
            for G in groups_of(0):
                for c in range(G.nch):
                    bx_chunk(G, c, 0)
            bx_end()
            setup_late()
            for pi in range(NPASS):
                groups = groups_of(pi)
                for G in groups:
                    stage_P1(G, pi)
                stage_D(groups, pi)
                p.stage = "Bv"
                bv_all(groups, pi)
                if pi == 0:
                    setup_stage_dma()

                def mid(pi=pi):
                    if pi == 0:
                        setup_late2()
                        setup_compute()
                    if pi + 1 < NPASS:
                        for G in groups_of(pi + 1):
                            stage_A1(G, pi + 1)

                def after_za(groups=groups, pi=pi):
                    stage_E(groups)

                def after_zb(groups=groups, pi=pi):
                    if pi == 0:
                        setup_late3()
                    stage_F(groups)

                stage_C(groups, mid, after_za=after_za, after_zb=after_zb)
                stage_G(groups)
                if pi == NPASS - 1:
                    spool_copy()
                nxtg = groups_of(pi + 1) if pi + 1 < NPASS else []
                items_next = [(G, c) for G in nxtg for c in range(G.nch)]
                hi_items = [(G, c) for G in groups for c in range(G.nch)]
                st = {"k": 0, "begun": False}

                def fill(t, pi=pi, hi_items=hi_items, items_next=items_next):
                    G, c = hi_items[t]
                    stage_P2(G, pi, [c], (5, 3))
                    if t < len(items_next):
                        Gn, cn = items_next[t]
                        stage_A2(Gn, pi + 1, [cn], (5, 3))

                def drain(j, pi=pi, nxtg=nxtg, items_next=items_next, st=st):
                    if not items_next:
                        return
                    old = p.stage
                    p.stage = "B"
                    if j is not None and j < 1:
                        p.stage = old
                        return
                    if not st["begun"]:
                        bx_begin(nxtg, pi + 1)
                        st["begun"] = True
                    todo = len(items_next) - st["k"] if j is None else min(1, len(items_next) - st["k"])
                    for _ in range(todo):
                        Gn, cn = items_next[st["k"]]
                        bx_chunk(Gn, cn, pi + 1, (0, 4))
                        st["k"] += 1
                    if j is None:
                        bx_end()
                    p.stage = old

                stage_HI(groups, pi, fill, drain)

        schedule()
        if stop is not None:
            for sname, cnt in list(p.dcount.items()):
                store_toks.append((sname, cnt))
        p.finish("sp", store_toks)
        import os
        if os.environ.get("KDUMP"):
            import json
            json.dump(p.pe_labels, open(os.environ["KDUMP"], "w"))
        p.emit()
    return nc


_NC_CACHE = {}


def kernel(x_prompt, x_sample, state_pool, p_prompt, p_sample, pre_g, w_in, ln_g, ln_b, w_s, b_s, w_pool, pool_scale,
           w_pa, w_pb, w_out, post_g, w_ple, w_pg, ple_in_g, ple_out_g):
    f = lambda a: np.ascontiguousarray(np.asarray(a), dtype=np.float32)
    if "nc" not in _NC_CACHE:
        _NC_CACHE["nc"] = build_program()
    nc = _NC_CACHE["nc"]
    consts = {"c_" + k: v for k, v in _constants().items()}
    shared = {
        "pre_g": f(pre_g[0]).reshape(1, D), "w_in": f(w_in[0]), "ln_g": f(ln_g[0]).reshape(1, D), "ln_b": f(ln_b[0]).reshape(1, D),
        "w_s": f(w_s[0]), "b_s": f(b_s[0]).reshape(1, D), "w_pool": f(w_pool[0]), "pool_scale": f(pool_scale[0]).reshape(1, D),
        "w_pa": f(w_pa[0]), "w_pb": f(w_pb[0]), "w_out": f(w_out[0]), "post_g": f(post_g[0]).reshape(1, D), "w_ple": f(w_ple[0]),
        "w_pg": f(w_pg[0]), "ple_in_g": f(ple_in_g[0]).reshape(1, D), "ple_out_g": f(ple_out_g[0]).reshape(1, D),
    }
    shared.update(consts)
    xp = f(x_prompt); xs = f(x_sample); spl = f(state_pool); pp = f(p_prompt); psm = f(p_sample)
    in_maps = []
    for i in range(NCORES):
        m = dict(shared)
        m["x"] = xp[i]
        m["xs"] = xs[i * NS_TOK:(i + 1) * NS_TOK, 0, :]
        m["sp"] = spl[0, i * NS_TOK:(i + 1) * NS_TOK]
        m["pp"] = pp[0, i]
        m["psm"] = psm[0, i * NS_TOK:(i + 1) * NS_TOK, 0, :]
        in_maps.append({k: np.ascontiguousarray(v) for k, v in m.items()})
    res = run_bass_kernel_spmd(nc, in_maps, core_ids=list(range(NCORES)))
    R = res.results
    y_prompt = np.stack([R[i]["y"] for i in range(NCORES)], axis=0)
    y_sample = np.concatenate([R[i]["ys"] for i in range(NCORES)], axis=0)[:, None, :]
    pcv = np.stack([R[i]["pcv"] for i in range(NCORES)], axis=0)[None]
    ppool = np.stack([R[i]["ppool"] for i in range(NCORES)], axis=0)[None]
    scv = np.concatenate([R[i]["scv"] for i in range(NCORES)], axis=0)[None, :, None, :]
    spool = np.concatenate([R[i]["spool"] for i in range(NCORES)], axis=0)[None]
    return (y_prompt.astype(np.float32), y_sample.astype(np.float32), pcv.astype(np.float32), ppool.astype(np.float32),
            scv.astype(np.float32), spool.astype(np.float32))
```

```python
import contextlib
import numpy as np
import concourse.bass as bass
import concourse.mybir as mybir
from concourse.bass_utils import run_bass_kernel_spmd

F32 = mybir.dt.float32
BF16 = mybir.dt.bfloat16
I32 = mybir.dt.int32
AF = mybir.ActivationFunctionType
ALU = mybir.AluOpType

NCORES = 8
D = 1024
SEQ = 2048
NS_TOK = 16
POOL_BUF = 15
WINDOWS = (2, 4, 8, 16)
EPS = 1e-6
MAGIC = 0x5F3759DF
NPASS = 4
SAMPLE_PASS = 0
NRING = 4
USE_SCRATCH = False
NEWTON_ITERS = 2
POOL_ADDS = True

ENGS = ("pe", "act", "dve", "pool", "sp")


class Prog:
    def __init__(self, nc):
        self.nc = nc
        self.ops = {e: [] for e in ENGS}
        self.cnt = {e: 0 for e in ENGS}
        self.buf = {}
        self.waited = {e: {} for e in ENGS}
        self.dcount = {}
        self.final_tokens = []
        self.stage = "setup"
        self.pe_labels = []

    def _deps(self, eng, reads, writes):
        need = {}

        def add(tok, raw):
            if tok is None:
                return
            s, v = tok
            if s == eng and eng == "pe":
                return
            if need.get(s, 0) < v:
                need[s] = v

        for k in reads:
            b = self.buf.get(k)
            if b is not None:
                add(b[0], True)
                if k.startswith("ps"):
                    for t in b[1]:
                        if t[0] != eng:
                            add(t, False)
        for k in writes:
            b = self.buf.get(k)
            if b is not None:
                add(b[0], False)
                for t in b[1]:
                    add(t, False)
        waits = []
        wd = self.waited[eng]
        for s, v in need.items():
            if wd.get(s, 0) >= v:
                continue
            wd[s] = v
            waits.append((s, v))
        return waits

    def _commit(self, tok, reads, writes):
        for k in reads:
            self.buf.setdefault(k, [None, []])[1].append(tok)
        for k in writes:
            self.buf[k] = [tok, []]

    def op(self, eng, fn, reads=(), writes=(), sig=True):
        waits = self._deps(eng, reads, writes)
        if sig:
            self.cnt[eng] += 1
            tok = (eng, self.cnt[eng])
        else:
            tok = (eng, self.cnt[eng] + 1)
        self.ops[eng].append((fn, waits, eng if sig else None, 1))
        if eng == "pe":
            self.pe_labels.append(self.stage)
        self._commit(tok, reads, writes)
        return tok

    def dma(self, eng, sem, fn, reads=(), writes=()):
        waits = self._deps(eng, reads, writes)
        self.dcount[sem] = self.dcount.get(sem, 0) + 16
        tok = (sem, self.dcount[sem])
        self.ops[eng].append((fn, waits, sem, 16))
        self._commit(tok, reads, writes)
        return tok

    def finish(self, eng, toks):
        self.final_tokens.append((eng, list(toks)))

    def emit(self):
        nc = self.nc
        names = list(ENGS) + sorted(self.dcount.keys())
        with contextlib.ExitStack() as st:
            sems = {n: st.enter_context(nc.semaphore("s_" + n)) for n in names}
            finals = {e: {} for e in ENGS}
            for e, toks in self.final_tokens:
                for (s, v) in toks:
                    finals[e][s] = max(finals[e].get(s, 0), v)

            def run(e):
                def body(engine):
                    for fn, waits, sg, inc in self.ops[e]:
                        for (s, v) in waits:
                            engine.wait_ge(sems[s], v)
                        ins = fn(engine)
                        if sg is not None:
                            ins.then_inc(sems[sg], inc)
                    for s, v in finals[e].items():
                        engine.wait_ge(sems[s], v)
                return body

            with nc.Block() as block:
                block.sync(run("sp"))
                block.scalar(run("act"))
                block.vector(run("dve"))
                block.gpsimd(run("pool"))
                block.tensor(run("pe"))


def _bf16_round(a):
    a = np.ascontiguousarray(a, dtype=np.float32)
    u = a.view(np.uint32).astype(np.uint64)
    r = ((u + 0x7FFF + ((u >> 16) & 1)) & 0xFFFF0000).astype(np.uint32)
    return r.view(np.float32)


def _constants():
    c = {}
    c["ident"] = np.eye(128, dtype=np.float32)
    t = np.arange(128)
    c["trilmask"] = (t[None, :] <= t[:, None]).astype(np.float32)
    pcur = np.zeros((128, 4, 128), np.float32)
    pfirst = np.zeros((128, 4, 128), np.float32)
    pprev = np.zeros((128, 4, 16), np.float32)
    for g, w in enumerate(WINDOWS):
        for tt in range(128):
            for j in range(w):
                s = tt - j
                if s >= 0:
                    pcur[s, g, tt] += 1.0 / w
                    pfirst[s, g, tt] += 1.0 / min(tt + 1, w)
                elif tt < 16:
                    pprev[128 + s, g, tt] += 1.0 / w
            pcur[tt, g, tt] -= 1.0
            pfirst[tt, g, tt] -= 1.0
    c["pcur"] = pcur
    hi = _bf16_round(pfirst)
    c["pfirst_hi"] = hi
    c["pfirst_lo"] = (pfirst - hi).astype(np.float32)
    c["pprev"] = pprev
    pss = np.zeros((128, 2, 4, 16), np.float32)
    psx = np.zeros((16, 4, 16), np.float32)
    for g, w in enumerate(WINDOWS):
        for b in range(16):
            psx[b, g, b] = 1.0 / w - 1.0
            for r in range(POOL_BUF):
                if r >= 16 - w:
                    row = b * POOL_BUF + r
                    pss[row % 128, row // 128, g, b] = 1.0 / w
    c["pss"] = pss
    c["psx"] = psx
    return c


CONST_SHAPES = {
    "ident": [128, 128], "trilmask": [128, 128], "pcur": [128, 4, 128], "pfirst_hi": [128, 4, 128],
    "pfirst_lo": [128, 4, 128], "pprev": [128, 4, 16], "pss": [128, 2, 4, 16], "psx": [16, 4, 16],
}


class Grp:
    pass


def build_program(stop=None):
    nc = bass.Bass("TRN2", target_bir_lowering=False)

    def din(name, shape):
        return nc.dram_tensor(name, list(shape), F32, kind="ExternalInput").ap()

    def dout(name, shape):
        return nc.dram_tensor(name, list(shape), F32, kind="ExternalOutput").ap()

    x_d = din("x", [SEQ, D]); xs_d = din("xs", [NS_TOK, D]); sp_d = din("sp", [NS_TOK, POOL_BUF, D])
    pp_d = din("pp", [SEQ, 256]); psm_d = din("psm", [NS_TOK, 256])
    pre_g_d = din("pre_g", [1, D]); w_in_d = din("w_in", [D, 7 * D]); ln_g_d = din("ln_g", [1, D]); ln_b_d = din("ln_b", [1, D])
    w_s_d = din("w_s", [8, 128, 128]); b_s_d = din("b_s", [1, D]); w_pool_d = din("w_pool", [4, 256, 256])
    pool_scale_d = din("pool_scale", [1, D]); w_pa_d = din("w_pa", [D, D]); w_pb_d = din("w_pb", [D, D])
    w_out_d = din("w_out", [D, D]); post_g_d = din("post_g", [1, D]); w_ple_d = din("w_ple", [256, D])
    w_pg_d = din("w_pg", [D, D]); ple_in_g_d = din("ple_in_g", [1, D]); ple_out_g_d = din("ple_out_g", [1, D])
    cst = {k: din("c_" + k, v) for k, v in CONST_SHAPES.items()}

    y_d = dout("y", [SEQ, D]); ys_d = dout("ys", [NS_TOK, D]); pcv_d = dout("pcv", [128, D]); ppool_d = dout("ppool", [POOL_BUF, D])
    scv_d = dout("scv", [NS_TOK, D]); spool_d = dout("spool", [NS_TOK, POOL_BUF, D])

    with contextlib.ExitStack() as stack:
        def sb(name, shape, dt):
            return stack.enter_context(nc.sbuf_tensor(name, list(shape), dt))

        ring = [sb(f"ring{i}", [128, 8, 512], BF16) for i in range(NRING)]
        wpool_sb = sb("wpool_sb", [128, 4, 2, 256], BF16)
        wple_sb = sb("wple_sb", [128, 2, 1024], BF16)
        trilWT = sb("trilWT", [128, 8, 128], BF16)
        identb = sb("identb", [128, 128], BF16)
        pcur_sb = sb("pcur_sb", [128, 4, 128], BF16)
        pfh_sb = sb("pfh_sb", [128, 4, 128], BF16)
        pfl_sb = sb("pfl_sb", [128, 4, 128], BF16)
        pprev_sb = sb("pprev_sb", [128, 4, 16], BF16)
        pss_sb = sb("pss_sb", [128, 2, 4, 16], BF16)
        psx_sb = sb("psx_sb", [16, 4, 16], BF16)
        wd_sb = sb("wd_sb", [16, 8, 16], BF16)
        ws00 = sb("ws00", [16, 8], F32)
        bs2 = sb("bs2", [2, 1024], BF16)
        ones2 = sb("ones2", [2, 128], BF16)
        ln_g_t = sb("ln_g_t", [128, D], F32); ln_b_t = sb("ln_b_t", [128, D], F32)
        post_g_t = sb("post_g_t", [128, D], F32); ple_out_g_t = sb("ple_out_g_t", [128, D], F32)
        gvec = sb("gvec", [24, 128], F32)
        identf24 = sb("identf24", [24, 24], F32)
        gT3 = sb("gT3", [128, 24], F32)
        pre_gT, ple_in_gT, pscaleT = gT3[:, 0:8], gT3[:, 8:16], gT3[:, 16:24]
        st_bf = sb("st_bf", [128, 2, 1024], BF16)
        NSTAT = 12
        stats = [sb(f"stat{i}", [128, 16], F32) for i in range(NSTAT)]
        xin = [sb(f"xin{i}", [128, D], F32) for i in range(2)]
        hres = [sb(f"hres{i}", [128, D], F32) for i in range(5)]
        tmp = [sb(f"tmp{i}", [128, D], F32) for i in range(5)]
        xn = [sb(f"xn{i}", [128, D], BF16) for i in range(5)]
        szt = [sb(f"szt{i}", [128, 512], BF16) for i in range(2)]
        m2t = [sb(f"m2t{i}", [128, 512], F32) for i in range(2)]
        junk = m2t[0].bitcast(BF16)
        xn2T = [sb(f"xn2T{i}", [128, 8, 128], BF16) for i in range(2)]
        pbf = [sb(f"pbf{i}", [128, 256], BF16) for i in range(5)]
        pT = [sb(f"pT{i}", [128, 2, 128], BF16) for i in range(5)]

        def mkgrp(name, T, nch):
            G = Grp()
            G.name, G.T, G.nch, G.TT = name, T, nch, T * nch
            G.xT = sb(name + "_xT", [128, 8, G.TT], BF16)
            G.u = sb(name + "_u", [128, 8, G.TT], BF16)
            G.szb = sb(name + "_szb", [128, 8, G.TT], BF16)
            G.sgb = sb(name + "_sgb", [128, 8, G.TT], BF16)
            G.qT = sb(name + "_qT", [128, 8, G.TT], BF16)
            G.sga = G.qT
            G.vn = sb(name + "_vn", [128, nch, D], BF16)
            G.xo = 1 if nch > 1 else 0
            G.xb = sb(name + "_xb", [128, nch + G.xo, D], BF16)
            return G

        GP = mkgrp("P", 128, 4)
        GS = mkgrp("S", NS_TOK, 1)

        psd = [stack.enter_context(nc.psum_tensor(f"psd{i}", [128, 1024], F32)) for i in range(4)]
        psb = [psd[i // 2][:, (i % 2) * 512:(i % 2 + 1) * 512] for i in range(8)]
        hist = [sb(f"hist{i}", [128, 16], F32) for i in range(12)]
        pro = sb("pro", [128, 32], F32)
        lnst = sb("lnst", [128, 72], F32)

        p = Prog(nc)
        rot = {}

        def nxt(name, n):
            i = rot.get(name, 0)
            rot[name] = i + 1
            return i % n

        def ps_next(pool=None):
            if pool is None:
                i = nxt("ps", 8)
            else:
                lo, n = pool
                i = lo + nxt(f"ps_{lo}_{n}", n)
            return psb[i], f"ps{i}"

        def stat_next():
            i = nxt("stat", NSTAT)
            return stats[i], f"stat{i}"

        def pool_next(name, lst):
            i = nxt(name, len(lst))
            return lst[i], f"{name}{i}"

        def mm(out, lhsT, rhs, start, stop, reads, writes, sig):
            p.op("pe", lambda e: e.matmul(out, lhsT=lhsT, rhs=rhs, start=start, stop=stop), reads, writes, sig)

        def tr(out, in_, ident, reads, writes, sig):
            p.op("pe", lambda e: e.transpose(out=out, in_=in_, identity=ident), reads, writes, sig)

        def act(out, in_, func, reads, writes, scale=None, accum=None):
            def fn(e):
                kw = {}
                if scale is not None:
                    kw["scale"] = scale
                if accum is not None:
                    kw["accum_out"] = accum
                return e.activation(out=out, in_=in_, func=func, **kw)
            p.op("act", fn, reads, writes)

        def ve(eng, name, reads, writes, **kw):
            p.op(eng, lambda e: getattr(e, name)(**kw), reads, writes)

        def dve(name, reads, writes, **kw):
            ve("dve", name, reads, writes, **kw)

        def dma(eng, semkey, out, in_, reads, writes, slow=False):
            if slow:
                return p.dma(eng, semkey, lambda e: e.dma_start(out=out, in_=in_, allow_slow_non_contiguous=True), reads, writes)
            return p.dma(eng, semkey, lambda e: e.dma_start(out=out, in_=in_), reads, writes)

        SQ_SCALE = 1.0 / 32.0

        def newton(stt, sk, T, xcol):
            X = stt[:T, xcol:xcol + 1]
            Y = stt[:T, 4:5]
            HX = stt[:T, 5:6]
            B = stt[:T, 6:7]
            dve("tensor_scalar", [sk], [sk], out=Y.bitcast(I32), in0=X.bitcast(I32), scalar1=-0.5, scalar2=float(MAGIC),
                op0=ALU.mult, op1=ALU.add)
            dve("tensor_scalar", [sk], [sk], out=HX, in0=X, scalar1=EPS, scalar2=-0.5, op0=ALU.add, op1=ALU.mult)
            for _ in range(NEWTON_ITERS):
                dve("scalar_tensor_tensor", [sk], [sk], out=B, in0=Y, scalar=HX, in1=Y, op0=ALU.mult, op1=ALU.mult)
                dve("scalar_tensor_tensor", [sk], [sk], out=Y, in0=B, scalar=1.5, in1=Y, op0=ALU.add, op1=ALU.mult)
            return Y

        def rms_rstd(stt, sk, T, nparts):
            assert nparts == 1
            return newton(stt, sk, T, 0)

        def wsrc(w_ap, c0):
            return w_ap.rearrange("(k p) n -> p k n", p=128)[:, :, c0:c0 + 512]

        pass_blocks = []
        for c0 in (3072, 3584, 1024, 1536, 0, 512, 2048, 2560, 4096, 4608, 5120, 5632):
            pass_blocks.append(wsrc(w_in_d, c0))
        pass_blocks += [wsrc(w_pa_d, 0), wsrc(w_pa_d, 512)]
        pass_blocks += [wsrc(w_in_d, 6144), wsrc(w_in_d, 6656)]
        pass_blocks += [wsrc(w_pb_d, 0), wsrc(w_pb_d, 512), wsrc(w_out_d, 0), wsrc(w_out_d, 512), wsrc(w_pg_d, 0), wsrc(w_pg_d, 512)]
        NBLK = len(pass_blocks)
        n_loads = NBLK * NPASS
        wscr = nc.dram_tensor("wscr", [NBLK, 128, 8 * 512], BF16).ap()
        rs = {"issued": 0, "acq": 0, "done": 0}

        def ring_pump():
            while rs["issued"] < n_loads and rs["issued"] - rs["done"] < NRING:
                n = rs["issued"]
                s = n % NRING
                if n < NBLK or not USE_SCRATCH:
                    dma("pool", f"d_ring{s}", ring[s][:], pass_blocks[n % NBLK], [], [f"ring{s}"])
                else:
                    dma("pool", f"d_ring{s}", ring[s][:].rearrange("p k n -> p (k n)"), wscr[n % NBLK], [f"scr{n % NBLK}"], [f"ring{s}"])
                rs["issued"] += 1

        def ring_acquire():
            n = rs["acq"]
            assert n < rs["issued"], "ring underflow"
            rs["acq"] += 1
            s = n % NRING
            return ring[s], f"ring{s}"

        def ring_release(k=1):
            for _ in range(k):
                n = rs["done"]
                s_ = n % NRING
                if USE_SCRATCH and n < NBLK and NPASS > 1:
                    dma("pool", f"s_scr{s_}", wscr[n], ring[s_][:].rearrange("p k n -> p (k n)"), [f"ring{s_}"], [f"scr{n}"])
                rs["done"] += 1
            ring_pump()

        def cdma(eng, key, out, in_, slow=False):
            return dma(eng, "d_c_" + key, out, in_, [], [key], slow=slow)

        def setup_early():
            cdma("pool", "identb", identb[:], cst["ident"])
            dma("act", "d_c_gvec", gvec[0:8, :], pre_g_d.rearrange("o (k p) -> (o k) p", p=128), [], ["gvec"])
            dma("act", "d_c_gvec", gvec[8:16, :], ple_in_g_d.rearrange("o (k p) -> (o k) p", p=128), [], ["gvec"])
            dma("act", "d_c_gvec", gvec[16:24, :], pool_scale_d.rearrange("o (k p) -> (o k) p", p=128), [], ["gvec"])
            cdma("act", "identf24", identf24[:], cst["ident"][0:24, 0:24])
            cdma("pool", "pcur", pcur_sb[:], cst["pcur"])
            cdma("pool", "pfh", pfh_sb[:], cst["pfirst_hi"])
            cdma("pool", "pfl", pfl_sb[:], cst["pfirst_lo"])
            cdma("pool", "pprev", pprev_sb[:], cst["pprev"])
            cdma("pool", "pss", pss_sb[:], cst["pss"])
            cdma("pool", "psx", psx_sb[:], cst["psx"])
            spf = sp_d.rearrange("b r c -> (b r) c")
            dma("pool", "d_c_st_bf", st_bf[:, 0, :], spf[0:128, :], [], ["st_bf"])
            dma("pool", "d_c_st_bf", st_bf[0:112, 1, :], spf[128:240, :], [], ["st_bf"])
            ring_pump()
            bk0, bk0k = ps_next()
            tr(bk0[:, 0:24], gvec[:, :], identf24[:, :], ["gvec", "identf24"], [bk0k], True)
            dve("tensor_copy", [bk0k], ["pre_gT", "ple_in_gT", "pscaleT"], out=gT3[:, :], in_=bk0[:, 0:24])
            for i in range(12):
                dve("memset", [], [f"hist{i}"], ap=hist[i][:], constant=1.0)
            dve("memset", [], ["lnst"], ap=lnst[:], constant=1.0)

        def setup_ln():
            cdma("sp", "ln_g", ln_g_t[:], ln_g_d.broadcast_to([128, D]))
            cdma("sp", "ln_b", ln_b_t[:], ln_b_d.broadcast_to([128, D]))

        def setup_late():
            cdma("sp", "ws00", ws00[:], bass.AP(w_s_d.tensor, 0, [[0, 16], [128 * 128, 8]]), slow=True)

        def setup_late2():
            cdma("pool", "wpool", wpool_sb[:], w_pool_d.rearrange("g (kk p) d -> p g kk d", p=128))

        def setup_late3():
            cdma("sp", "post_g", post_g_t[:], post_g_d.broadcast_to([128, D]))
            cdma("sp", "ple_out_g", ple_out_g_t[:], ple_out_g_d.broadcast_to([128, D]))
            cdma("pool", "wple", wple_sb[:], w_ple_d.rearrange("(kk p) d -> p kk d", p=128))

        def spool_copy():
            store_toks.append(dma("sp", "s_spool", spool_d[:, 0:POOL_BUF - 1, :], sp_d[:, 1:POOL_BUF, :], [], []))

        def setup_stage_dma():
            dma("sp", "s_stg0", tmp[0][:].rearrange("p (h s) -> p h s", h=8), w_s_d.rearrange("h t s -> t h s"), [], ["tmp0"])
            dma("sp", "s_stg1", tmp[1][:, 0:128], cst["trilmask"], [], ["tmp1"])
            dma("sp", "s_stg1", tmp[1][:, 128:256], cst["ident"], [], ["tmp1"])
            dma("sp", "s_stg2", tmp[2][0:2, :], b_s_d.broadcast_to([2, D]), [], ["tmp2"])

        def setup_compute():
            wsb = xn[0]
            dve("tensor_tensor", ["tmp0", "tmp1"], ["xn0"], out=wsb[:].rearrange("p (h s) -> p h s", h=8),
                in0=tmp[0][:].rearrange("p (h s) -> p h s", h=8), in1=tmp[1][:, 0:128].unsqueeze(1).broadcast_to([128, 8, 128]), op=ALU.mult)
            bk, bkk = ps_next()
            bkb = bk.bitcast(BF16)
            for h in range(8):
                tr(bkb[:, h * 128:(h + 1) * 128], wsb[:, h * 128:(h + 1) * 128], identb[:], ["xn0", "identb"], [bkk], h == 7)
            dve("tensor_copy", [bkk], ["trilWT"], out=trilWT[:].rearrange("p h t -> p (h t)"), in_=bkb[:, :])
            for h in range(8):
                dve("tensor_scalar", ["tmp1", "ws00"], ["wd"], out=wd_sb[:, h, :], in0=tmp[1][0:16, 128:144], scalar1=ws00[:, h:h + 1],
                    scalar2=None, op0=ALU.mult)
            bsf, bsh, bshf = tmp[2][0:2, :], xn[1][0:2, :], hres[0][0:2, :]
            dve("tensor_copy", ["tmp2"], ["xn1"], out=bsh, in_=bsf)
            dve("tensor_copy", ["xn1"], ["hres0"], out=bshf, in_=bsh)
            dve("tensor_tensor", ["tmp2", "hres0"], ["hres0"], out=bshf, in0=bsf, in1=bshf, op=ALU.subtract)
            dve("tensor_copy", ["hres0"], ["bs2"], out=bs2[:], in_=bshf)
            dve("tensor_copy", ["xn1", "bs2"], ["bs2"], out=bs2[0:1, :], in_=bsh[0:1, :])
            dve("memset", [], ["ones2"], ap=ones2[:], constant=1.0)

        store_toks = []

        def gk(G, nm, i=None):
            if nm == "sga":
                nm = "qT"
            return f"{G.name}:{nm}" if i is None else f"{G.name}:{nm}.{i}"

        def x_rows(G, pi, c):
            if G.name == "P":
                r0 = (pi * 4 + c) * 128
                return x_d[r0:r0 + 128, :]
            return xs_d[:, :]

        def p_rows(G, pi, c):
            if G.name == "P":
                r0 = (pi * 4 + c) * 128
                return pp_d[r0:r0 + 128, :]
            return psm_d[:, :]

        def y_rows(G, pi, c):
            if G.name == "P":
                r0 = (pi * 4 + c) * 128
                return y_d[r0:r0 + 128, :]
            return ys_d[:, :]

        xnl = {}
        ptl = {}

        def _stage_A1(G, pi):
            T = G.T
            for c in range(G.nch):
                xt, xk = pool_next("xin", xin)
                dma("sp", "d_" + xk, xt[:T, :], x_rows(G, pi, c), [], [xk])
                stt, sk = stat_next()
                act(junk[:T, :], xt[:T, :], AF.Square, [xk], ["m2t0", sk], accum=stt[:T, 0:1], scale=SQ_SCALE)
                r = rms_rstd(stt, sk, T, 1)
                xnt, xnk = pool_next("xn", xn)
                dve("tensor_scalar", [xk, sk], [xnk], out=xnt[:T, :], in0=xt[:T, :], scalar1=r, scalar2=None, op0=ALU.mult)
                xnl[(G.name, pi, c)] = (xnt, xnk)

        def _stage_A1_batched(groups, pi):
            items = [(G, c) for G in groups for c in range(G.nch)]
            assert len(items) <= len(hres)
            dve("memset", [], ["pro"], ap=pro[:], constant=1.0)
            for i, (G, c) in enumerate(items):
                T = G.T
                xt, xk = hres[i], f"hres{i}"
                dma("sp", "d_" + xk, xt[:T, :], x_rows(G, pi, c), [], [xk])
                act(junk[:T, :], xt[:T, :], AF.Square, [xk], ["m2t0", "pro"], accum=pro[:T, i:i + 1], scale=SQ_SCALE)
            nI = len(items)
            X, Y, HX, B1 = pro[:, 0:nI], pro[:, 8:8 + nI], pro[:, 16:16 + nI], pro[:, 24:24 + nI]
            ks = ["pro"]
            dve("tensor_scalar", ks, ks, out=Y.bitcast(I32), in0=X.bitcast(I32), scalar1=-0.5, scalar2=float(MAGIC), op0=ALU.mult, op1=ALU.add)
            dve("tensor_scalar", ks, ks, out=HX, in0=X, scalar1=EPS, scalar2=-0.5, op0=ALU.add, op1=ALU.mult)
            for _ in range(NEWTON_ITERS):
                dve("tensor_tensor", ks, ks, out=B1, in0=Y, in1=Y, op=ALU.mult)
                dve("tensor_tensor", ks, ks, out=B1, in0=B1, in1=HX, op=ALU.mult)
                dve("scalar_tensor_tensor", ks, ks, out=Y, in0=B1, scalar=1.5, in1=Y, op0=ALU.add, op1=ALU.mult)
            for i, (G, c) in enumerate(items):
                T = G.T
                xnt, xnk = pool_next("xn", xn)
                dve("tensor_scalar", [f"hres{i}", "pro"], [xnk], out=xnt[:T, :], in0=hres[i][:T, :], scalar1=pro[:T, 8 + i:9 + i], scalar2=None,
                    op0=ALU.mult)
                xnl[(G.name, pi, c)] = (xnt, xnk)

        def _stage_A2(G, pi, chunks=None, pspool=None):
            T = G.T
            for c in (range(G.nch) if chunks is None else chunks):
                xnt, xnk = xnl.pop((G.name, pi, c))
                bk, bkk = ps_next(pspool)
                bkb = bk.bitcast(BF16)
                for k in range(8):
                    tr(bkb[:, k * T:(k + 1) * T], xnt[:T, k * 128:(k + 1) * 128], identb[:T, :T], [xnk, "identb"], [bkk], k == 7)
                dve("tensor_tensor", [bkk, "pre_gT"], [gk(G, "xT", c)], out=G.xT[:, :, c * T:(c + 1) * T],
                    in0=bkb[:, 0:8 * T].rearrange("p (k t) -> p k t", k=8), in1=pre_gT.unsqueeze(2).broadcast_to([128, 8, T]),
                    op=ALU.mult)

        def _stage_P1(G, pi):
            T = G.T
            for c in range(G.nch):
                pb, pbk = pool_next("pbf", pbf)
                dma("pool", "d_" + pbk, pb[:T, :], p_rows(G, pi, c), [], [pbk])
                ptl[(G.name, pi, c)] = (pb, pbk)

        def _stage_P2(G, pi, chunks=None, pspool=None):
            T = G.T
            for c in (range(G.nch) if chunks is None else chunks):
                pb, pbk = ptl.pop((G.name, pi, c))
                bk2, bk2k = ps_next(pspool)
                bk2b = bk2.bitcast(BF16)
                for kk in range(2):
                    tr(bk2b[:, kk * T:(kk + 1) * T], pb[:T, kk * 128:(kk + 1) * 128], identb[:T, :T], [pbk, "identb"], [bk2k], kk == 1)
                ptt, ptk = pool_next("pT", pT)
                dve("tensor_copy", [bk2k], [ptk], out=ptt[:, :, :T], in_=bk2b[:, 0:2 * T].rearrange("p (k t) -> p k t", k=2))
                ptl[(G.name, pi, c, "T")] = (ptt, ptk)

        def is_out_chunk(G, pi, c, kind="v"):
            return G.name == "S" or (pi == NPASS - 1 and c == G.nch - 1)

        bvs = {}

        def bv_begin():
            bvs["s"] = (ring_acquire(), ring_acquire())

        def bv_chunk(G, c, pi, pspool=None):
            (s0, s0k), (s1, s1k) = bvs["s"]
            T = G.T
            if True:
                if True:
                    vf, vk = pool_next("tmp", tmp)
                    stt, sk = stat_next()
                    for half, (sl, slk) in enumerate(((s0, s0k), (s1, s1k))):
                        bk, bkk = ps_next(pspool)
                        for k in range(8):
                            mm(bk[:T, :], G.xT[:, k, c * T:(c + 1) * T], sl[:, k, :], k == 0, k == 7, [gk(G, "xT", c), slk], [bkk], k == 7)
                        act(vf[:T, half * 512:(half + 1) * 512], bk[:T, :], AF.Gelu_apprx_tanh, [bkk], [vk, sk], accum=stt[:T, half:half + 1])
                    act(junk[:T, :], vf[:T, :], AF.Square, [vk], ["m2t0", sk], accum=stt[:T, 2:3], scale=SQ_SCALE)
                    dve("tensor_tensor", [sk], [sk], out=stt[:T, 7:8], in0=stt[:T, 0:1], in1=stt[:T, 1:2], op=ALU.add)
                    dve("tensor_scalar", [sk], [sk], out=stt[:T, 8:9], in0=stt[:T, 7:8], scalar1=1.0 / D, scalar2=None, op0=ALU.mult)
                    dve("tensor_tensor", [sk], [sk], out=stt[:T, 9:10], in0=stt[:T, 8:9], in1=stt[:T, 8:9], op=ALU.mult)
                    dve("tensor_tensor", [sk], [sk], out=stt[:T, 3:4], in0=stt[:T, 2:3], in1=stt[:T, 9:10], op=ALU.subtract)
                    r = newton(stt, sk, T, 3)
                    dve("scalar_tensor_tensor", [vk, sk, "ln_g"], [vk], out=vf[:T, :], in0=vf[:T, :], scalar=stt[:T, 8:9], in1=ln_g_t[:T, :],
                        op0=ALU.subtract, op1=ALU.mult)
                    if is_out_chunk(G, pi, c):
                        dve("scalar_tensor_tensor", [vk, sk, "ln_b"], [vk], out=vf[:T, :], in0=vf[:T, :], scalar=r, in1=ln_b_t[:T, :],
                            op0=ALU.mult, op1=ALU.add)
                        act(G.vn[:T, c, :], vf[:T, :], AF.Copy, [vk], [gk(G, "vn", c)])
                        dst = scv_d[:, :] if G.name == "S" else pcv_d[:, :]
                        store_toks.append(dma("sp", "s_" + vk, dst, vf[:T, :], [vk], []))
                    else:
                        dve("scalar_tensor_tensor", [vk, sk, "ln_b"], [gk(G, "vn", c)], out=G.vn[:T, c, :], in0=vf[:T, :], scalar=r,
                            in1=ln_b_t[:T, :], op0=ALU.mult, op1=ALU.add)

        def bv_all(groups, pi):
            (s0, s0k), (s1, s1k) = ring_acquire(), ring_acquire()
            items = [(G, c) for G in groups for c in range(G.nch)]
            n = len(items)
            assert n <= len(tmp) and n <= 8
            A0, A1, Q, MEAN, MSQ, X, Y, HX, B1 = [lnst[:, 8 * j:8 * j + n] for j in range(9)]
            lk = ["lnst"]
            vfs = []
            for i, (G, c) in enumerate(items):
                T = G.T
                vf, vk = pool_next("tmp", tmp)
                vfs.append((vf, vk))
                for half, (sl, slk) in enumerate(((s0, s0k), (s1, s1k))):
                    bk, bkk = ps_next()
                    for k in range(8):
                        mm(bk[:T, :], G.xT[:, k, c * T:(c + 1) * T], sl[:, k, :], k == 0, k == 7, [gk(G, "xT", c), slk], [bkk], k == 7)
                    act(vf[:T, half * 512:(half + 1) * 512], bk[:T, :], AF.Gelu_apprx_tanh, [bkk], [vk] + lk,
                        accum=lnst[:T, 8 * half + i:8 * half + i + 1])
                act(junk[:T, :], vf[:T, :], AF.Square, [vk], ["m2t0"] + lk, accum=lnst[:T, 16 + i:17 + i], scale=SQ_SCALE)
            ring_release(2)
            dve("tensor_tensor", lk, lk, out=MEAN, in0=A0, in1=A1, op=ALU.add)
            dve("tensor_scalar", lk, lk, out=MEAN, in0=MEAN, scalar1=1.0 / D, scalar2=None, op0=ALU.mult)
            dve("tensor_tensor", lk, lk, out=MSQ, in0=MEAN, in1=MEAN, op=ALU.mult)
            dve("tensor_tensor", lk, lk, out=X, in0=Q, in1=MSQ, op=ALU.subtract)
            dve("tensor_scalar", lk, lk, out=Y.bitcast(I32), in0=X.bitcast(I32), scalar1=-0.5, scalar2=float(MAGIC), op0=ALU.mult, op1=ALU.add)
            dve("tensor_scalar", lk, lk, out=HX, in0=X, scalar1=EPS, scalar2=-0.5, op0=ALU.add, op1=ALU.mult)
            for _ in range(NEWTON_ITERS):
                dve("tensor_tensor", lk, lk, out=B1, in0=Y, in1=Y, op=ALU.mult)
                dve("tensor_tensor", lk, lk, out=B1, in0=B1, in1=HX, op=ALU.mult)
                dve("scalar_tensor_tensor", lk, lk, out=Y, in0=B1, scalar=1.5, in1=Y, op0=ALU.add, op1=ALU.mult)
            for i, (G, c) in enumerate(items):
                T = G.T
                vf, vk = vfs[i]
                mean_i = lnst[:T, 24 + i:25 + i]
                r_i = lnst[:T, 48 + i:49 + i]
                dve("scalar_tensor_tensor", [vk, "ln_g"] + lk, [vk], out=vf[:T, :], in0=vf[:T, :], scalar=mean_i, in1=ln_g_t[:T, :],
                    op0=ALU.subtract, op1=ALU.mult)
                if is_out_chunk(G, pi, c):
                    dve("scalar_tensor_tensor", [vk, "ln_b"] + lk, [vk], out=vf[:T, :], in0=vf[:T, :], scalar=r_i, in1=ln_b_t[:T, :],
                        op0=ALU.mult, op1=ALU.add)
                    dve("tensor_copy", [vk], [gk(G, "vn", c)], out=G.vn[:T, c, :], in_=vf[:T, :])
                    dst = scv_d[:, :] if G.name == "S" else pcv_d[:, :]
                    store_toks.append(dma("sp", "s_" + vk, dst, vf[:T, :], [vk], []))
                else:
                    dve("scalar_tensor_tensor", [vk, "ln_b"] + lk, [gk(G, "vn", c)], out=G.vn[:T, c, :], in0=vf[:T, :], scalar=r_i,
                        in1=ln_b_t[:T, :], op0=ALU.mult, op1=ALU.add)

        def bv_end():
            ring_release(2)

        bxs = {}

        def bx_begin(groups, pi):
            bxs["s"] = (ring_acquire(), ring_acquire())
            for G in groups:
                if G.name == "P" and pi > 0:
                    dve("tensor_copy", [gk(G, "xb", 4)], [gk(G, "xb", 0)], out=G.xb[:, 0, :], in_=G.xb[:, 4, :])

        def bx_chunk(G, c, pi, pspool=None):
            (s0, s0k), (s1, s1k) = bxs["s"]
            T = G.T
            outc = is_out_chunk(G, pi, c, "x")
            if outc:
                xf, xfk = pool_next("tmp", tmp)
            for half, (sl, slk) in enumerate(((s0, s0k), (s1, s1k))):
                bk, bkk = ps_next(pspool)
                for k in range(8):
                    mm(bk[:T, :], G.xT[:, k, c * T:(c + 1) * T], sl[:, k, :], k == 0, k == 7, [gk(G, "xT", c), slk], [bkk], k == 7)
                act(G.xb[:T, G.xo + c, half * 512:(half + 1) * 512], bk[:T, :], AF.Copy, [bkk], [gk(G, "xb", 1 + c)])
                if outc:
                    dve("tensor_copy", [bkk], [xfk], out=xf[:T, half * 512:(half + 1) * 512], in_=bk[:T, :])
            if outc:
                if G.name == "S":
                    store_toks.append(dma("sp", "s_" + xfk, spool_d[:, POOL_BUF - 1, :], xf[:T, :], [xfk], []))
                else:
                    store_toks.append(dma("sp", "s_" + xfk, ppool_d[:, :], xf[128 - POOL_BUF:128, :], [xfk], []))

        def bx_end():
            ring_release(2)

        def fm_proj(groups, func, dst_fn, post_fn=None):
            for half in range(2):
                sl, slk = ring_acquire()
                for j in range(4):
                    jj = half * 4 + j
                    for G in groups:
                        bk, bkk = ps_next()
                        for k in range(8):
                            mm(bk[:, :G.TT], sl[:, k, j * 128:(j + 1) * 128], G.xT[:, k, :], k == 0, k == 7,
                               [slk] + [gk(G, "xT", c) for c in range(G.nch)], [bkk], k == 7)
                        out, okeys = dst_fn(G, jj)
                        act(out, bk[:, :G.TT], func, [bkk], okeys)
                        if post_fn is not None:
                            post_fn(G, jj)
                ring_release(1)

        def _stage_C(groups, mid_hook=None, after_za=None, after_zb=None):
            fm_proj(groups, AF.Gelu_apprx_tanh, lambda G, jj: (G.u[:, jj, :], [gk(G, "u", jj)]))
            if mid_hook is not None:
                mid_hook()
            zt = {}

            def dst_za(G, jj):
                t, tk = pool_next("szt", szt)
                zt[(G.name, jj)] = (t, tk)
                return t[:, :G.TT], [tk]

            def post_za(G, jj):
                t, tk = zt[(G.name, jj)]
                dve("tensor_tensor", [tk, gk(G, "u", jj)], [gk(G, "u", jj)], out=G.u[:, jj, :], in0=G.u[:, jj, :], in1=t[:, :G.TT], op=ALU.mult)

            fm_proj(groups, AF.Silu, dst_za, post_za)
            if after_za is not None:
                after_za()
            fm_proj(groups, AF.Silu, lambda G, jj: (G.szb[:, jj, :], [gk(G, "szb", jj)]))
            if after_zb is not None:
                after_zb()

        def _stage_D(groups, pi):
            for G in groups:
                T = G.T
                for jj in range(8):
                    g = jj // 2
                    cs = slice(jj * 128, (jj + 1) * 128)
                    bk, bkk = ps_next()
                    if G.name == "P":
                        for c in range(4):
                            o = bk[:, c * 128:(c + 1) * 128]
                            last = (c == 3)
                            if pi == 0 and c == 0:
                                mm(o, G.xb[:, 1, cs], pfh_sb[:, g, :], True, False, [gk(G, "xb", 1), "pfh"], [bkk], False)
                                mm(o, G.xb[:, 1, cs], pfl_sb[:, g, :], False, True, [gk(G, "xb", 1), "pfl"], [bkk], last)
                            else:
                                mm(o, G.xb[:, 1 + c, cs], pcur_sb[:, g, :], True, False, [gk(G, "xb", 1 + c), "pcur"], [bkk], False)
                                mm(bk[:, c * 128:c * 128 + 16], G.xb[64:128, c, cs], pprev_sb[64:128, g, :], False, True,
                                   [gk(G, "xb", c), "pprev"], [bkk], last)
                    else:
                        o = bk[:, 0:16]
                        mm(o, st_bf[:, 0, cs], pss_sb[:, 0, g, :], True, False, ["st_bf", "pss"], [bkk], False)
                        mm(o, st_bf[0:112, 1, cs], pss_sb[0:112, 1, g, :], False, False, ["st_bf", "pss"], [bkk], False)
                        mm(o, G.xb[0:16, G.xo, cs], psx_sb[:, g, :], False, True, [gk(G, "xb", 1), "psx"], [bkk], True)
                    dve("tensor_copy", [bkk], [gk(G, "qT", jj)], out=G.qT[:, jj, :], in_=bk[:, :G.TT])

        def _stage_E(groups):
            for G in groups:
                T = G.T
                for h in range(8):
                    bk, bkk = ps_next()
                    if G.name == "P":
                        brhs = bass.AP(bs2, h * 128, [[1024, 2], [0, 4], [1, 128]])
                    else:
                        brhs = bass.AP(bs2, h * 128, [[1024, 2], [0, 16]])
                    mm(bk[:, :G.TT], ones2[:, :], brhs, True, False, ["ones2", "bs2"], [bkk], False)
                    for c in range(G.nch):
                        rhs = trilWT[:, h, :] if G.name == "P" else wd_sb[:, h, :]
                        mm(bk[:, c * T:(c + 1) * T], G.vn[:T, c, h * 128:(h + 1) * 128], rhs, False, c == G.nch - 1,
                           [gk(G, "vn", c), "trilWT", "wd"], [bkk], c == G.nch - 1)
                    dve("tensor_tensor", [bkk, gk(G, "u", h)], [gk(G, "u", h)], out=G.u[:, h, :], in0=bk[:, :G.TT], in1=G.u[:, h, :], op=ALU.mult)

        def _stage_F(groups):
            for G in groups:
                for jd in range(8):
                    g, dd = jd // 2, jd % 2
                    bk, bkk = ps_next()
                    for kk in range(2):
                        mm(bk[:, :G.TT], wpool_sb[:, g, kk, dd * 128:(dd + 1) * 128], G.qT[:, 2 * g + kk, :], kk == 0, kk == 1,
                           ["wpool", gk(G, "qT", 2 * g + kk)], [bkk], kk == 1)
                    dve("scalar_tensor_tensor", [bkk, "pscaleT", gk(G, "szb", jd)], [gk(G, "szb", jd)], out=G.szb[:, jd, :], in0=bk[:, :G.TT],
                        scalar=pscaleT[:, jd:jd + 1], in1=G.szb[:, jd, :], op0=ALU.mult, op1=ALU.mult)

        def fm_mix(groups, src_fn, evac_fn):
            for half in range(2):
                sl, slk = ring_acquire()
                for j in range(4):
                    jj = half * 4 + j
                    for G in groups:
                        bk, bkk = ps_next()
                        for k in range(8):
                            src, skey = src_fn(G, k)
                            mm(bk[:, :G.TT], sl[:, k, j * 128:(j + 1) * 128], src, k == 0, k == 7, [slk, skey], [bkk], k == 7)
                        evac_fn(G, jj, bk, bkk)
                ring_release(1)

        def _stage_G(groups):
            fm_proj(groups, AF.Sigmoid, lambda G, jj: (G.sga[:, jj, :], [gk(G, "sga", jj)]))

            def evac_a(G, jj, bk, bkk):
                dve("tensor_tensor", [bkk, gk(G, "sga", jj)], [gk(G, "sga", jj)], out=G.sga[:, jj, :], in0=bk[:, :G.TT], in1=G.sga[:, jj, :], op=ALU.mult)

            fm_mix(groups, lambda G, k: (G.u[:, k, :], gk(G, "u", k)), evac_a)
            fm_proj(groups, AF.Sigmoid, lambda G, jj: (G.sgb[:, jj, :], [gk(G, "sgb", jj)]))

            def evac_b(G, jj, bk, bkk):
                t, tk = pool_next("m2t", m2t)
                dve("tensor_tensor", [bkk, gk(G, "sgb", jj)], [tk], out=t[:, :G.TT], in0=bk[:, :G.TT], in1=G.sgb[:, jj, :], op=ALU.mult)
                ve("pool" if (POOL_ADDS and jj < 6) else "dve", "tensor_tensor", [tk, gk(G, "sga", jj)], [gk(G, "sgb", jj)], out=G.sgb[:, jj, :],
                   in0=t[:, :G.TT], in1=G.sga[:, jj, :], op=ALU.add)

            fm_mix(groups, lambda G, k: (G.szb[:, k, :], gk(G, "szb", k)), evac_b)

        def _stage_HI(groups, pi, fill_hook=None, drain_hook=None):
            (s0, s0k), (s1, s1k) = ring_acquire(), ring_acquire()
            (g0, g0k), (g1, g1k) = ring_acquire(), ring_acquire()
            items = [(G, c) for G in groups for c in range(G.nch)]
            n = len(items)
            S = [dict() for _ in range(n)]
            PS_P, PS_I = (4, 1), (5, 3)
            addeng = "pool" if POOL_ADDS else "dve"
            nsteps = n + 5
            HB = [(hist[i], f"hist{i}") for i in range(nsteps + 1)]

            def chain(t):
                bt, bk_ = HB[t]
                X, Y, HX, B1 = bt[:, 0:3], bt[:, 4:7], bt[:, 8:11], bt[:, 12:15]
                dve("tensor_scalar", [bk_], [bk_], out=Y.bitcast(I32), in0=X.bitcast(I32), scalar1=-0.5, scalar2=float(MAGIC),
                    op0=ALU.mult, op1=ALU.add)
                dve("tensor_scalar", [bk_], [bk_], out=HX, in0=X, scalar1=EPS, scalar2=-0.5, op0=ALU.add, op1=ALU.mult)
                for _ in range(NEWTON_ITERS):
                    dve("tensor_tensor", [bk_], [bk_], out=B1, in0=Y, in1=Y, op=ALU.mult)
                    dve("tensor_tensor", [bk_], [bk_], out=B1, in0=B1, in1=HX, op=ALU.mult)
                    dve("scalar_tensor_tensor", [bk_], [bk_], out=Y, in0=B1, scalar=1.5, in1=Y, op0=ALU.add, op1=ALU.mult)

            def h_pe(i, t):
                G, c = items[i]; T = G.T; st = S[i]
                d = nxt("psH", 2)
                pd = psd[d]
                keys = [f"ps{2 * d}", f"ps{2 * d + 1}"]
                for half, (sl, slk) in enumerate(((s0, s0k), (s1, s1k))):
                    for k in range(8):
                        mm(pd[:T, half * 512:(half + 1) * 512], G.sgb[:, k, c * T:(c + 1) * T], sl[:, k, :], k == 0, k == 7,
                           [gk(G, "sgb", k), slk], [keys[half]], k == 7)
                nb, nbk = HB[t + 1]
                act(junk[:T, :], pd[:T, :], AF.Square, keys, ["m2t0", nbk], accum=nb[:T, 0:1], scale=SQ_SCALE)
                st["pd"], st["pkeys"] = pd, keys

            def h_d1(i, t):
                G, c = items[i]; T = G.T; st = S[i]
                bt, btk = HB[t]
                ht, hk = pool_next("hres", hres)
                st["ht"], st["hk"] = ht, hk
                dma("sp", "d_" + hk, ht[:T, :], x_rows(G, pi, c), [], [hk])
                t1, t1k = pool_next("tmp", tmp)
                dve("scalar_tensor_tensor", st["pkeys"] + [btk, "post_g"], [t1k], out=t1[:T, :], in0=st["pd"][:T, :], scalar=bt[:T, 4:5],
                    in1=post_g_t[:T, :], op0=ALU.mult, op1=ALU.mult)
                dve("tensor_tensor", [t1k, hk], [hk], out=ht[:T, :], in0=ht[:T, :], in1=t1[:T, :], op=ALU.add)

            def h_d1b(i, t):
                G, c = items[i]; T = G.T; st = S[i]
                ht, hk = st["ht"], st["hk"]
                nb, nbk = HB[t + 1]
                act(junk[:T, :], ht[:T, :], AF.Square, [hk], ["m2t0", nbk], accum=nb[:T, 1:2], scale=SQ_SCALE)

            def h_d2(i, t):
                G, c = items[i]; T = G.T; st = S[i]
                bt, btk = HB[t]
                xnt, xnk = pool_next("xn", xn)
                act(xnt[:T, :], st["ht"][:T, :], AF.Copy, [st["hk"], btk], [xnk], scale=bt[:T, 5:6])
                st["xnt"], st["xnk"] = xnt, xnk

            def i_prep(i, t):
                G, c = items[i]; T = G.T; st = S[i]
                xnt, xnk = st["xnt"], st["xnk"]
                bk, bkk = ps_next(PS_P)
                bkb = bk.bitcast(BF16)
                for k in range(8):
                    tr(bkb[:, k * T:(k + 1) * T], xnt[:T, k * 128:(k + 1) * 128], identb[:T, :T], [xnk, "identb"], [bkk], k == 7)
                x2, x2k = pool_next("xn2T", xn2T)
                dve("tensor_tensor", [bkk, "ple_in_gT"], [x2k], out=x2[:, :, :T], in0=bkb[:, 0:8 * T].rearrange("p (k t) -> p k t", k=8),
                    in1=ple_in_gT.unsqueeze(2).broadcast_to([128, 8, T]), op=ALU.mult)
                st["x2"], st["x2k"] = x2, x2k

            def i_half(i, half):
                G, c = items[i]; T = G.T; st = S[i]
                x2, x2k, gt, gtk = st["x2"], st["x2k"], st["gt"], st["gtk"]
                ptt, ptk = st["ptt"], st["ptk"]
                sl, slk = ((g0, g0k), (g1, g1k))[half]
                hs = slice(half * 512, (half + 1) * 512)
                bg, bgk = ps_next(PS_I)
                for k in range(8):
                    mm(bg[:T, :], x2[:, k, :T], sl[:, k, :], k == 0, k == 7, [x2k, slk], [bgk], k == 7)
                act(gt[:T, hs], bg[:T, :], AF.Sigmoid, [bgk], [gtk])
                be, bek = ps_next(PS_I)
                for kk in range(2):
                    mm(be[:T, :], ptt[:, kk, :T], wple_sb[:, kk, hs], kk == 0, kk == 1, [ptk, "wple"], [bek], kk == 1)
                dve("tensor_tensor", [bek, gtk], [gtk], out=gt[:T, hs], in0=be[:T, :], in1=gt[:T, hs], op=ALU.mult)

            def i_pe_a(i, t):
                G, c = items[i]; st = S[i]
                st["ptt"], st["ptk"] = ptl.pop((G.name, pi, c, "T"))
                st["gt"], st["gtk"] = pool_next("tmp", tmp)
                i_half(i, 0)

            def i_pe_b(i, t):
                G, c = items[i]; T = G.T; st = S[i]
                i_half(i, 1)
                nb, nbk = HB[t + 1]
                act(junk[:T, :], st["gt"][:T, :], AF.Square, [st["gtk"]], ["m2t0", nbk], accum=nb[:T, 2:3], scale=SQ_SCALE)

            def i_fin(i, t):
                G, c = items[i]; T = G.T; st = S[i]
                bt, btk = HB[t]
                gt, gtk = st["gt"], st["gtk"]
                last = (pi == NPASS - 1 and i == n - 1)
                if last:
                    dve("scalar_tensor_tensor", [gtk, btk, "ple_out_g"], [gtk], out=gt[:T, :], in0=gt[:T, :], scalar=bt[:T, 6:7],
                        in1=ple_out_g_t[:T, :], op0=ALU.mult, op1=ALU.mult)
                    dve("tensor_tensor", [gtk, st["hk"]], [gtk], out=gt[:T, :], in0=gt[:T, :], in1=st["ht"][:T, :], op=ALU.add)
                elif POOL_ADDS:
                    act(gt[:T, :], gt[:T, :], AF.Copy, [gtk, btk], [gtk], scale=bt[:T, 6:7])
                    ve("pool", "tensor_tensor", [gtk, "ple_out_g"], [gtk], out=gt[:T, :], in0=gt[:T, :], in1=ple_out_g_t[:T, :], op=ALU.mult)
                else:
                    dve("scalar_tensor_tensor", [gtk, btk, "ple_out_g"], [gtk], out=gt[:T, :], in0=gt[:T, :], scalar=bt[:T, 6:7],
                        in1=ple_out_g_t[:T, :], op0=ALU.mult, op1=ALU.mult)
                if not last:
                    ve(addeng, "tensor_tensor", [gtk, st["hk"]], [gtk], out=gt[:T, :], in0=gt[:T, :], in1=st["ht"][:T, :], op=ALU.add)
                store_toks.append(dma("sp", "s_" + gtk, y_rows(G, pi, c), gt[:T, :], [gtk], []))

            seq = [(h_pe, 0), (None, None), (h_d1, 1), (h_d2, 2), (i_fin, 5), (i_prep, 3), (i_pe_a, 4), (h_d1b, 1), (i_pe_b, 4)]
            uses_chain = (1, 2, 5)
            for t in range(nsteps):
                for fn, lag in seq:
                    if fn is None:
                        if any(0 <= t - sj < n for sj in uses_chain):
                            chain(t)
                        continue
                    i = t - lag
                    if 0 <= i < n:
                        fn(i, t)
                if t < n and fill_hook is not None:
                    fill_hook(t)
                if t >= n and drain_hook is not None:
                    drain_hook(t - n)
                if t == n - 1:
                    ring_release(2)
                if t == n + 3:
                    ring_release(2)
            if drain_hook is not None:
                drain_hook(None)

        def _wrap(fn, label):
            def w(*a, **k):
                old = p.stage
                p.stage = label
                try:
                    return fn(*a, **k)
                finally:
                    p.stage = old
            return w

        stage_A1 = _wrap(_stage_A1, "A1")
        stage_A2 = _wrap(_stage_A2, "A2")
        stage_P1 = _wrap(_stage_P1, "P1")
        stage_P2 = _wrap(_stage_P2, "P2")
        stage_C = _wrap(_stage_C, "C")
        stage_D = _wrap(_stage_D, "D")
        stage_E = _wrap(_stage_E, "E")
        stage_F = _wrap(_stage_F, "F")
        stage_G = _wrap(_stage_G, "G")
        stage_HI = _wrap(_stage_HI, "HI")

        def groups_of(pi):
            return [GP] + ([GS] if pi == SAMPLE_PASS else [])

        def schedule():
            setup_early()
            p.stage = "A1"
            _stage_A1_batched(groups_of(0), 0)
            setup_ln()
            for G in groups_of(0):
                stage_A2(G, 0)
            p.stage = "B"
            bx_begin(groups_of(0), 0)
            for G in groups_of(0):
                for c in range(G.nch):
                    bx_chunk(G, c, 0)
            bx_end()
            setup_late()
            for pi in range(NPASS):
                groups = groups_of(pi)
                for G in groups:
                    stage_P1(G, pi)
                stage_D(groups, pi)
                p.stage = "Bv"
                bv_all(groups, pi)
                if pi == 0:
                    setup_stage_dma()

                def mid(pi=pi):
                    if pi == 0:
                        setup_late2()
                        setup_compute()

                def after_za(groups=groups, pi=pi):
                    stage_E(groups)
                    if pi + 1 < NPASS:
                        for G in groups_of(pi + 1):
                            stage_A1(G, pi + 1)

                def after_zb(groups=groups, pi=pi):
                    if pi == 0:
                        setup_late3()
                    stage_F(groups)

                stage_C(groups, mid, after_za=after_za, after_zb=after_zb)
                stage_G(groups)
                if pi == NPASS - 1:
                    spool_copy()
                nxtg = groups_of(pi + 1) if pi + 1 < NPASS else []
                items_next = [(G, c) for G in nxtg for c in range(G.nch)]
                hi_items = [(G, c) for G in groups for c in range(G.nch)]
                st = {"k": 0, "begun": False}

                def fill(t, pi=pi, hi_items=hi_items, items_next=items_next):
                    G, c = hi_items[t]
                    stage_P2(G, pi, [c], (5, 3))
                    if t < len(items_next):
                        Gn, cn = items_next[t]
                        stage_A2(Gn, pi + 1, [cn], (5, 3))

                def drain(j, pi=pi, nxtg=nxtg, items_next=items_next, st=st):
                    if not items_next:
                        return
                    old = p.stage
                    p.stage = "B"
                    if j is not None and j < 1:
                        p.stage = old
                        return
                    if not st["begun"]:
                        bx_begin(nxtg, pi + 1)
                        st["begun"] = True
                    todo = len(items_next) - st["k"] if j is None else min(1, len(items_next) - st["k"])
                    for _ in range(todo):
                        Gn, cn = items_next[st["k"]]
                        bx_chunk(Gn, cn, pi + 1, (0, 4))
                        st["k"] += 1
                    if j is None:
                        bx_end()
                    p.stage = old

                stage_HI(groups, pi, fill, drain)

        schedule()
        if stop is not None:
            for sname, cnt in list(p.dcount.items()):
                store_toks.append((sname, cnt))
        p.finish("sp", store_toks)
        import os
        if os.environ.get("KDUMP"):
            import json
            json.dump(p.pe_labels, open(os.environ["KDUMP"], "w"))
        p.emit()
    return nc


_NC_CACHE = {}


def kernel(x_prompt, x_sample, state_pool, p_prompt, p_sample, pre_g, w_in, ln_g, ln_b, w_s, b_s, w_pool, pool_scale,
           w_pa, w_pb, w_out, post_g, w_ple, w_pg, ple_in_g, ple_out_g):
    f = lambda a: np.ascontiguousarray(np.asarray(a), dtype=np.float32)
    if "nc" not in _NC_CACHE:
        _NC_CACHE["nc"] = build_program()
    nc = _NC_CACHE["nc"]
    consts = {"c_" + k: v for k, v in _constants().items()}
    shared = {
        "pre_g": f(pre_g[0]).reshape(1, D), "w_in": f(w_in[0]), "ln_g": f(ln_g[0]).reshape(1, D), "ln_b": f(ln_b[0]).reshape(1, D),
        "w_s": f(w_s[0]), "b_s": f(b_s[0]).reshape(1, D), "w_pool": f(w_pool[0]), "pool_scale": f(pool_scale[0]).reshape(1, D),
        "w_pa": f(w_pa[0]), "w_pb": f(w_pb[0]), "w_out": f(w_out[0]), "post_g": f(post_g[0]).reshape(1, D), "w_ple": f(w_ple[0]),
        "w_pg": f(w_pg[0]), "ple_in_g": f(ple_in_g[0]).reshape(1, D), "ple_out_g": f(ple_out_g[0]).reshape(1, D),
    }
    shared.update(consts)
    xp = f(x_prompt); xs = f(x_sample); spl = f(state_pool); pp = f(p_prompt); psm = f(p_sample)
    in_maps = []
    for i in range(NCORES):
        m = dict(shared)
        m["x"] = xp[i]
        m["xs"] = xs[i * NS_TOK:(i + 1) * NS_TOK, 0, :]
        m["sp"] = spl[0, i * NS_TOK:(i + 1) * NS_TOK]
        m["pp"] = pp[0, i]
        m["psm"] = psm[0, i * NS_TOK:(i + 1) * NS_TOK, 0, :]
        in_maps.append({k: np.ascontiguousarray(v) for k, v in m.items()})
    res = run_bass_kernel_spmd(nc, in_maps, core_ids=list(range(NCORES)))
    R = res.results
    y_prompt = np.stack([R[i]["y"] for i in range(NCORES)], axis=0)
    y_sample = np.concatenate([R[i]["ys"] for i in range(NCORES)], axis=0)[:, None, :]
    pcv = np.stack([R[i]["pcv"] for i in range(NCORES)], axis=0)[None]
    ppool = np.stack([R[i]["ppool"] for i in range(NCORES)], axis=0)[None]
    scv = np.concatenate([R[i]["scv"] for i in range(NCORES)], axis=0)[None, :, None, :]
    spool = np.concatenate([R[i]["spool"] for i in range(NCORES)], axis=0)[None]
    return (y_prompt.astype(np.float32), y_sample.astype(np.float32), pcv.astype(np.float32), ppool.astype(np.float32),
            scv.astype(np.float32), spool.astype(np.float32))
```

```python
import contextlib
import numpy as np
import concourse.bass as bass
import concourse.mybir as mybir
from concourse.bass_utils import run_bass_kernel_spmd

F32 = mybir.dt.float32
BF16 = mybir.dt.bfloat16
I32 = mybir.dt.int32
AF = mybir.ActivationFunctionType
ALU = mybir.AluOpType

NCORES = 8
D = 1024
SEQ = 2048
NS_TOK = 16
POOL_BUF = 15
WINDOWS = (2, 4, 8, 16)
EPS = 1e-6
MAGIC = 0x5F3759DF
NPASS = 4
SAMPLE_PASS = 0
NRING = 4
USE_SCRATCH = False
NEWTON_ITERS = 2
POOL_ADDS = True

ENGS = ("pe", "act", "dve", "pool", "sp")


class Prog:
    def __init__(self, nc):
        self.nc = nc
        self.ops = {e: [] for e in ENGS}
        self.cnt = {e: 0 for e in ENGS}
        self.buf = {}
        self.waited = {e: {} for e in ENGS}
        self.dcount = {}
        self.final_tokens = []
        self.stage = "setup"
        self.pe_labels = []

    def _deps(self, eng, reads, writes):
        need = {}

        def add(tok, raw):
            if tok is None:
                return
            s, v = tok
            if s == eng and eng == "pe":
                return
            if need.get(s, 0) < v:
                need[s] = v

        for k in reads:
            b = self.buf.get(k)
            if b is not None:
                add(b[0], True)
                if k.startswith("ps"):
                    for t in b[1]:
                        if t[0] != eng:
                            add(t, False)
        for k in writes:
            b = self.buf.get(k)
            if b is not None:
                add(b[0], False)
                for t in b[1]:
                    add(t, False)
        waits = []
        wd = self.waited[eng]
        for s, v in need.items():
            if wd.get(s, 0) >= v:
                continue
            wd[s] = v
            waits.append((s, v))
        return waits

    def _commit(self, tok, reads, writes):
        for k in reads:
            self.buf.setdefault(k, [None, []])[1].append(tok)
        for k in writes:
            self.buf[k] = [tok, []]

    def op(self, eng, fn, reads=(), writes=(), sig=True):
        waits = self._deps(eng, reads, writes)
        if sig:
            self.cnt[eng] += 1
            tok = (eng, self.cnt[eng])
        else:
            tok = (eng, self.cnt[eng] + 1)
        self.ops[eng].append((fn, waits, eng if sig else None, 1))
        if eng == "pe":
            self.pe_labels.append(self.stage)
        self._commit(tok, reads, writes)
        return tok

    def dma(self, eng, sem, fn, reads=(), writes=()):
        waits = self._deps(eng, reads, writes)
        self.dcount[sem] = self.dcount.get(sem, 0) + 16
        tok = (sem, self.dcount[sem])
        self.ops[eng].append((fn, waits, sem, 16))
        self._commit(tok, reads, writes)
        return tok

    def finish(self, eng, toks):
        self.final_tokens.append((eng, list(toks)))

    def emit(self):
        nc = self.nc
        names = list(ENGS) + sorted(self.dcount.keys())
        with contextlib.ExitStack() as st:
            sems = {n: st.enter_context(nc.semaphore("s_" + n)) for n in names}
            finals = {e: {} for e in ENGS}
            for e, toks in self.final_tokens:
                for (s, v) in toks:
                    finals[e][s] = max(finals[e].get(s, 0), v)

            def run(e):
                def body(engine):
                    for fn, waits, sg, inc in self.ops[e]:
                        for (s, v) in waits:
                            engine.wait_ge(sems[s], v)
                        ins = fn(engine)
                        if sg is not None:
                            ins.then_inc(sems[sg], inc)
                    for s, v in finals[e].items():
                        engine.wait_ge(sems[s], v)
                return body

            with nc.Block() as block:
                block.sync(run("sp"))
                block.scalar(run("act"))
                block.vector(run("dve"))
                block.gpsimd(run("pool"))
                block.tensor(run("pe"))


def _bf16_round(a):
    a = np.ascontiguousarray(a, dtype=np.float32)
    u = a.view(np.uint32).astype(np.uint64)
    r = ((u + 0x7FFF + ((u >> 16) & 1)) & 0xFFFF0000).astype(np.uint32)
    return r.view(np.float32)


def _constants():
    c = {}
    c["ident"] = np.eye(128, dtype=np.float32)
    t = np.arange(128)
    c["trilmask"] = (t[None, :] <= t[:, None]).astype(np.float32)
    pcur = np.zeros((128, 4, 128), np.float32)
    pfirst = np.zeros((128, 4, 128), np.float32)
    pprev = np.zeros((128, 4, 16), np.float32)
    for g, w in enumerate(WINDOWS):
        for tt in range(128):
            for j in range(w):
                s = tt - j
                if s >= 0:
                    pcur[s, g, tt] += 1.0 / w
                    pfirst[s, g, tt] += 1.0 / min(tt + 1, w)
                elif tt < 16:
                    pprev[128 + s, g, tt] += 1.0 / w
            pcur[tt, g, tt] -= 1.0
            pfirst[tt, g, tt] -= 1.0
    c["pcur"] = pcur
    hi = _bf16_round(pfirst)
    c["pfirst_hi"] = hi
    c["pfirst_lo"] = (pfirst - hi).astype(np.float32)
    c["pprev"] = pprev
    pss = np.zeros((128, 2, 4, 16), np.float32)
    psx = np.zeros((16, 4, 16), np.float32)
    for g, w in enumerate(WINDOWS):
        for b in range(16):
            psx[b, g, b] = 1.0 / w - 1.0
            for r in range(POOL_BUF):
                if r >= 16 - w:
                    row = b * POOL_BUF + r
                    pss[row % 128, row // 128, g, b] = 1.0 / w
    c["pss"] = pss
    c["psx"] = psx
    return c


CONST_SHAPES = {
    "ident": [128, 128], "trilmask": [128, 128], "pcur": [128, 4, 128], "pfirst_hi": [128, 4, 128],
    "pfirst_lo": [128, 4, 128], "pprev": [128, 4, 16], "pss": [128, 2, 4, 16], "psx": [16, 4, 16],
}


class Grp:
    pass


def build_program(stop=None):
    nc = bass.Bass("TRN2", target_bir_lowering=False)

    def din(name, shape):
        return nc.dram_tensor(name, list(shape), F32, kind="ExternalInput").ap()

    def dout(name, shape):
        return nc.dram_tensor(name, list(shape), F32, kind="ExternalOutput").ap()

    x_d = din("x", [SEQ, D]); xs_d = din("xs", [NS_TOK, D]); sp_d = din("sp", [NS_TOK, POOL_BUF, D])
    pp_d = din("pp", [SEQ, 256]); psm_d = din("psm", [NS_TOK, 256])
    pre_g_d = din("pre_g", [1, D]); w_in_d = din("w_in", [D, 7 * D]); ln_g_d = din("ln_g", [1, D]); ln_b_d = din("ln_b", [1, D])
    w_s_d = din("w_s", [8, 128, 128]); b_s_d = din("b_s", [1, D]); w_pool_d = din("w_pool", [4, 256, 256])
    pool_scale_d = din("pool_scale", [1, D]); w_pa_d = din("w_pa", [D, D]); w_pb_d = din("w_pb", [D, D])
    w_out_d = din("w_out", [D, D]); post_g_d = din("post_g", [1, D]); w_ple_d = din("w_ple", [256, D])
    w_pg_d = din("w_pg", [D, D]); ple_in_g_d = din("ple_in_g", [1, D]); ple_out_g_d = din("ple_out_g", [1, D])
    cst = {k: din("c_" + k, v) for k, v in CONST_SHAPES.items()}

    y_d = dout("y", [SEQ, D]); ys_d = dout("ys", [NS_TOK, D]); pcv_d = dout("pcv", [128, D]); ppool_d = dout("ppool", [POOL_BUF, D])
    scv_d = dout("scv", [NS_TOK, D]); spool_d = dout("spool", [NS_TOK, POOL_BUF, D])

    with contextlib.ExitStack() as stack:
        def sb(name, shape, dt):
            return stack.enter_context(nc.sbuf_tensor(name, list(shape), dt))

        ring = [sb(f"ring{i}", [128, 8, 512], BF16) for i in range(NRING)]
        wpool_sb = sb("wpool_sb", [128, 4, 2, 256], BF16)
        wple_sb = sb("wple_sb", [128, 2, 1024], BF16)
        trilWT = sb("trilWT", [128, 8, 128], BF16)
        identb = sb("identb", [128, 128], BF16)
        pcur_sb = sb("pcur_sb", [128, 4, 128], BF16)
        pfh_sb = sb("pfh_sb", [128, 4, 128], BF16)
        pfl_sb = sb("pfl_sb", [128, 4, 128], BF16)
        pprev_sb = sb("pprev_sb", [128, 4, 16], BF16)
        pss_sb = sb("pss_sb", [128, 2, 4, 16], BF16)
        psx_sb = sb("psx_sb", [16, 4, 16], BF16)
        wd_sb = sb("wd_sb", [16, 8, 16], BF16)
        ws00 = sb("ws00", [16, 8], F32)
        bs2 = sb("bs2", [2, 1024], BF16)
        ones2 = sb("ones2", [2, 128], BF16)
        ln_g_t = sb("ln_g_t", [128, D], F32); ln_b_t = sb("ln_b_t", [128, D], F32)
        post_g_t = sb("post_g_t", [128, D], F32); ple_out_g_t = sb("ple_out_g_t", [128, D], F32)
        gvec = sb("gvec", [24, 128], F32)
        identf24 = sb("identf24", [24, 24], F32)
        gT3 = sb("gT3", [128, 24], F32)
        pre_gT, ple_in_gT, pscaleT = gT3[:, 0:8], gT3[:, 8:16], gT3[:, 16:24]
        st_bf = sb("st_bf", [128, 2, 1024], BF16)
        NSTAT = 12
        stats = [sb(f"stat{i}", [128, 16], F32) for i in range(NSTAT)]
        xin = [sb(f"xin{i}", [128, D], F32) for i in range(2)]
        hres = [sb(f"hres{i}", [128, D], F32) for i in range(5)]
        tmp = [sb(f"tmp{i}", [128, D], F32) for i in range(5)]
        xn = [sb(f"xn{i}", [128, D], BF16) for i in range(5)]
        szt = [sb(f"szt{i}", [128, 512], BF16) for i in range(2)]
        m2t = [sb(f"m2t{i}", [128, 512], F32) for i in range(2)]
        junk = m2t[0].bitcast(BF16)
        xn2T = [sb(f"xn2T{i}", [128, 8, 128], BF16) for i in range(2)]
        pbf = [sb(f"pbf{i}", [128, 256], BF16) for i in range(5)]
        pT = [sb(f"pT{i}", [128, 2, 128], BF16) for i in range(5)]

        def mkgrp(name, T, nch):
            G = Grp()
            G.name, G.T, G.nch, G.TT = name, T, nch, T * nch
            G.xT = sb(name + "_xT", [128, 8, G.TT], BF16)
            G.u = sb(name + "_u", [128, 8, G.TT], BF16)
            G.szb = sb(name + "_szb", [128, 8, G.TT], BF16)
            G.sgb = sb(name + "_sgb", [128, 8, G.TT], BF16)
            G.qT = sb(name + "_qT", [128, 8, G.TT], BF16)
            G.sga = G.qT
            G.vn = sb(name + "_vn", [128, nch, D], BF16)
            G.xo = 1 if nch > 1 else 0
            G.xb = sb(name + "_xb", [128, nch + G.xo, D], BF16)
            return G

        GP = mkgrp("P", 128, 4)
        GS = mkgrp("S", NS_TOK, 1)

        psd = [stack.enter_context(nc.psum_tensor(f"psd{i}", [128, 1024], F32)) for i in range(4)]
        psb = [psd[i // 2][:, (i % 2) * 512:(i % 2 + 1) * 512] for i in range(8)]
        hist = [sb(f"hist{i}", [128, 16], F32) for i in range(12)]
        pro = sb("pro", [128, 32], F32)
        lnst = sb("lnst", [128, 72], F32)

        p = Prog(nc)
        rot = {}

        def nxt(name, n):
            i = rot.get(name, 0)
            rot[name] = i + 1
            return i % n

        def ps_next(pool=None):
            if pool is None:
                i = nxt("ps", 8)
            else:
                lo, n = pool
                i = lo + nxt(f"ps_{lo}_{n}", n)
            return psb[i], f"ps{i}"

        def stat_next():
            i = nxt("stat", NSTAT)
            return stats[i], f"stat{i}"

        def pool_next(name, lst):
            i = nxt(name, len(lst))
            return lst[i], f"{name}{i}"

        def mm(out, lhsT, rhs, start, stop, reads, writes, sig):
            p.op("pe", lambda e: e.matmul(out, lhsT=lhsT, rhs=rhs, start=start, stop=stop), reads, writes, sig)

        def tr(out, in_, ident, reads, writes, sig):
            p.op("pe", lambda e: e.transpose(out=out, in_=in_, identity=ident), reads, writes, sig)

        def act(out, in_, func, reads, writes, scale=None, accum=None):
            def fn(e):
                kw = {}
                if scale is not None:
                    kw["scale"] = scale
                if accum is not None:
                    kw["accum_out"] = accum
                return e.activation(out=out, in_=in_, func=func, **kw)
            p.op("act", fn, reads, writes)

        def ve(eng, name, reads, writes, **kw):
            p.op(eng, lambda e: getattr(e, name)(**kw), reads, writes)

        def dve(name, reads, writes, **kw):
            ve("dve", name, reads, writes, **kw)

        def dma(eng, semkey, out, in_, reads, writes, slow=False):
            if slow:
                return p.dma(eng, semkey, lambda e: e.dma_start(out=out, in_=in_, allow_slow_non_contiguous=True), reads, writes)
            return p.dma(eng, semkey, lambda e: e.dma_start(out=out, in_=in_), reads, writes)

        SQ_SCALE = 1.0 / 32.0

        def newton(stt, sk, T, xcol):
            X = stt[:T, xcol:xcol + 1]
            Y = stt[:T, 4:5]
            HX = stt[:T, 5:6]
            B = stt[:T, 6:7]
            dve("tensor_scalar", [sk], [sk], out=Y.bitcast(I32), in0=X.bitcast(I32), scalar1=-0.5, scalar2=float(MAGIC),
                op0=ALU.mult, op1=ALU.add)
            dve("tensor_scalar", [sk], [sk], out=HX, in0=X, scalar1=EPS, scalar2=-0.5, op0=ALU.add, op1=ALU.mult)
            for _ in range(NEWTON_ITERS):
                dve("scalar_tensor_tensor", [sk], [sk], out=B, in0=Y, scalar=HX, in1=Y, op0=ALU.mult, op1=ALU.mult)
                dve("scalar_tensor_tensor", [sk], [sk], out=Y, in0=B, scalar=1.5, in1=Y, op0=ALU.add, op1=ALU.mult)
            return Y

        def rms_rstd(stt, sk, T, nparts):
            assert nparts == 1
            return newton(stt, sk, T, 0)

        def wsrc(w_ap, c0):
            return w_ap.rearrange("(k p) n -> p k n", p=128)[:, :, c0:c0 + 512]

        pass_blocks = []
        for c0 in (3072, 3584, 1024, 1536, 0, 512, 2048, 2560, 4096, 4608, 5120, 5632):
            pass_blocks.append(wsrc(w_in_d, c0))
        pass_blocks += [wsrc(w_pa_d, 0), wsrc(w_pa_d, 512)]
        pass_blocks += [wsrc(w_in_d, 6144), wsrc(w_in_d, 6656)]
        pass_blocks += [wsrc(w_pb_d, 0), wsrc(w_pb_d, 512), wsrc(w_out_d, 0), wsrc(w_out_d, 512), wsrc(w_pg_d, 0), wsrc(w_pg_d, 512)]
        NBLK = len(pass_blocks)
        n_loads = NBLK * NPASS
        wscr = nc.dram_tensor("wscr", [NBLK, 128, 8 * 512], BF16).ap()
        rs = {"issued": 0, "acq": 0, "done": 0}

        def ring_pump():
            while rs["issued"] < n_loads and rs["issued"] - rs["done"] < NRING:
                n = rs["issued"]
                s = n % NRING
                if n < NBLK or not USE_SCRATCH:
                    dma("pool", f"d_ring{s}", ring[s][:], pass_blocks[n % NBLK], [], [f"ring{s}"])
                else:
                    dma("pool", f"d_ring{s}", ring[s][:].rearrange("p k n -> p (k n)"), wscr[n % NBLK], [f"scr{n % NBLK}"], [f"ring{s}"])
                rs["issued"] += 1

        def ring_acquire():
            n = rs["acq"]
            assert n < rs["issued"], "ring underflow"
            rs["acq"] += 1
            s = n % NRING
            return ring[s], f"ring{s}"

        def ring_release(k=1):
            for _ in range(k):
                n = rs["done"]
                s_ = n % NRING
                if USE_SCRATCH and n < NBLK and NPASS > 1:
                    dma("pool", f"s_scr{s_}", wscr[n], ring[s_][:].rearrange("p k n -> p (k n)"), [f"ring{s_}"], [f"scr{n}"])
                rs["done"] += 1
            ring_pump()

        def cdma(eng, key, out, in_, slow=False):
            return dma(eng, "d_c_" + key, out, in_, [], [key], slow=slow)

        def setup_early():
            cdma("pool", "identb", identb[:], cst["ident"])
            dma("act", "d_c_gvec", gvec[0:8, :], pre_g_d.rearrange("o (k p) -> (o k) p", p=128), [], ["gvec"])
            dma("act", "d_c_gvec", gvec[8:16, :], ple_in_g_d.rearrange("o (k p) -> (o k) p", p=128), [], ["gvec"])
            dma("act", "d_c_gvec", gvec[16:24, :], pool_scale_d.rearrange("o (k p) -> (o k) p", p=128), [], ["gvec"])
            cdma("act", "identf24", identf24[:], cst["ident"][0:24, 0:24])
            cdma("pool", "pcur", pcur_sb[:], cst["pcur"])
            cdma("pool", "pfh", pfh_sb[:], cst["pfirst_hi"])
            cdma("pool", "pfl", pfl_sb[:], cst["pfirst_lo"])
            cdma("pool", "pprev", pprev_sb[:], cst["pprev"])
            cdma("pool", "pss", pss_sb[:], cst["pss"])
            cdma("pool", "psx", psx_sb[:], cst["psx"])
            spf = sp_d.rearrange("b r c -> (b r) c")
            dma("pool", "d_c_st_bf", st_bf[:, 0, :], spf[0:128, :], [], ["st_bf"])
            dma("pool", "d_c_st_bf", st_bf[0:112, 1, :], spf[128:240, :], [], ["st_bf"])
            ring_pump()
            bk0, bk0k = ps_next()
            tr(bk0[:, 0:24], gvec[:, :], identf24[:, :], ["gvec", "identf24"], [bk0k], True)
            dve("tensor_copy", [bk0k], ["pre_gT", "ple_in_gT", "pscaleT"], out=gT3[:, :], in_=bk0[:, 0:24])
            for i in range(12):
                dve("memset", [], [f"hist{i}"], ap=hist[i][:], constant=1.0)
            dve("memset", [], ["lnst"], ap=lnst[:], constant=1.0)

        def setup_ln():
            cdma("sp", "ln_g", ln_g_t[:], ln_g_d.broadcast_to([128, D]))
            cdma("sp", "ln_b", ln_b_t[:], ln_b_d.broadcast_to([128, D]))

        def setup_late():
            cdma("sp", "ws00", ws00[:], bass.AP(w_s_d.tensor, 0, [[0, 16], [128 * 128, 8]]), slow=True)

        def setup_late2():
            cdma("pool", "wpool", wpool_sb[:], w_pool_d.rearrange("g (kk p) d -> p g kk d", p=128))

        def setup_late3():
            cdma("sp", "post_g", post_g_t[:], post_g_d.broadcast_to([128, D]))
            cdma("sp", "ple_out_g", ple_out_g_t[:], ple_out_g_d.broadcast_to([128, D]))
            cdma("pool", "wple", wple_sb[:], w_ple_d.rearrange("(kk p) d -> p kk d", p=128))

        def spool_copy():
            store_toks.append(dma("sp", "s_spool", spool_d[:, 0:POOL_BUF - 1, :], sp_d[:, 1:POOL_BUF, :], [], []))

        def setup_stage_dma():
            dma("sp", "s_stg0", hres[1][:].rearrange("p (h s) -> p h s", h=8), w_s_d.rearrange("h t s -> t h s"), [], ["hres1"])
            dma("sp", "s_stg1", hres[2][:, 0:128], cst["trilmask"], [], ["hres2"])
            dma("sp", "s_stg1", hres[2][:, 128:256], cst["ident"], [], ["hres2"])
            dma("sp", "s_stg2", hres[3][0:2, :], b_s_d.broadcast_to([2, D]), [], ["hres3"])

        def setup_compute():
            wsb = xn[0]
            dve("tensor_tensor", ["hres1", "hres2"], ["xn0"], out=wsb[:].rearrange("p (h s) -> p h s", h=8),
                in0=hres[1][:].rearrange("p (h s) -> p h s", h=8), in1=hres[2][:, 0:128].unsqueeze(1).broadcast_to([128, 8, 128]), op=ALU.mult)
            bk, bkk = ps_next()
            bkb = bk.bitcast(BF16)
            for h in range(8):
                tr(bkb[:, h * 128:(h + 1) * 128], wsb[:, h * 128:(h + 1) * 128], identb[:], ["xn0", "identb"], [bkk], h == 7)
            dve("tensor_copy", [bkk], ["trilWT"], out=trilWT[:].rearrange("p h t -> p (h t)"), in_=bkb[:, :])
            for h in range(8):
                dve("tensor_scalar", ["hres2", "ws00"], ["wd"], out=wd_sb[:, h, :], in0=hres[2][0:16, 128:144], scalar1=ws00[:, h:h + 1],
                    scalar2=None, op0=ALU.mult)
            bsf, bsh, bshf = hres[3][0:2, :], xn[1][0:2, :], hres[0][0:2, :]
            dve("tensor_copy", ["hres3"], ["xn1"], out=bsh, in_=bsf)
            dve("tensor_copy", ["xn1"], ["hres0"], out=bshf, in_=bsh)
            dve("tensor_tensor", ["hres3", "hres0"], ["hres0"], out=bshf, in0=bsf, in1=bshf, op=ALU.subtract)
            dve("tensor_copy", ["hres0"], ["bs2"], out=bs2[:], in_=bshf)
            dve("tensor_copy", ["xn1", "bs2"], ["bs2"], out=bs2[0:1, :], in_=bsh[0:1, :])
            dve("memset", [], ["ones2"], ap=ones2[:], constant=1.0)

        store_toks = []

        def gk(G, nm, i=None):
            if nm == "sga":
                nm = "qT"
            return f"{G.name}:{nm}" if i is None else f"{G.name}:{nm}.{i}"

        def x_rows(G, pi, c):
            if G.name == "P":
                r0 = (pi * 4 + c) * 128
                return x_d[r0:r0 + 128, :]
            return xs_d[:, :]

        def p_rows(G, pi, c):
            if G.name == "P":
                r0 = (pi * 4 + c) * 128
                return pp_d[r0:r0 + 128, :]
            return psm_d[:, :]

        def y_rows(G, pi, c):
            if G.name == "P":
                r0 = (pi * 4 + c) * 128
                return y_d[r0:r0 + 128, :]
            return ys_d[:, :]

        xnl = {}
        ptl = {}

        def _stage_A1(G, pi):
            T = G.T
            for c in range(G.nch):
                xt, xk = pool_next("xin", xin)
                dma("sp", "d_" + xk, xt[:T, :], x_rows(G, pi, c), [], [xk])
                stt, sk = stat_next()
                act(junk[:T, :], xt[:T, :], AF.Square, [xk], ["m2t0", sk], accum=stt[:T, 0:1], scale=SQ_SCALE)
                r = rms_rstd(stt, sk, T, 1)
                xnt, xnk = pool_next("xn", xn)
                dve("tensor_scalar", [xk, sk], [xnk], out=xnt[:T, :], in0=xt[:T, :], scalar1=r, scalar2=None, op0=ALU.mult)
                xnl[(G.name, pi, c)] = (xnt, xnk)

        def _stage_A1_batched(groups, pi):
            items = [(G, c) for G in groups for c in range(G.nch)]
            assert len(items) <= len(hres)
            dve("memset", [], ["pro"], ap=pro[:], constant=1.0)
            for i, (G, c) in enumerate(items):
                T = G.T
                xt, xk = hres[i], f"hres{i}"
                dma("sp", "d_" + xk, xt[:T, :], x_rows(G, pi, c), [], [xk])
                act(junk[:T, :], xt[:T, :], AF.Square, [xk], ["m2t0", "pro"], accum=pro[:T, i:i + 1], scale=SQ_SCALE)
            nI = len(items)
            X, Y, HX, B1 = pro[:, 0:nI], pro[:, 8:8 + nI], pro[:, 16:16 + nI], pro[:, 24:24 + nI]
            ks = ["pro"]
            dve("tensor_scalar", ks, ks, out=Y.bitcast(I32), in0=X.bitcast(I32), scalar1=-0.5, scalar2=float(MAGIC), op0=ALU.mult, op1=ALU.add)
            dve("tensor_scalar", ks, ks, out=HX, in0=X, scalar1=EPS, scalar2=-0.5, op0=ALU.add, op1=ALU.mult)
            for _ in range(NEWTON_ITERS):
                dve("tensor_tensor", ks, ks, out=B1, in0=Y, in1=Y, op=ALU.mult)
                dve("tensor_tensor", ks, ks, out=B1, in0=B1, in1=HX, op=ALU.mult)
                dve("scalar_tensor_tensor", ks, ks, out=Y, in0=B1, scalar=1.5, in1=Y, op0=ALU.add, op1=ALU.mult)
            for i, (G, c) in enumerate(items):
                T = G.T
                xnt, xnk = pool_next("xn", xn)
                dve("tensor_scalar", [f"hres{i}", "pro"], [xnk], out=xnt[:T, :], in0=hres[i][:T, :], scalar1=pro[:T, 8 + i:9 + i], scalar2=None,
                    op0=ALU.mult)
                xnl[(G.name, pi, c)] = (xnt, xnk)

        def _stage_A2(G, pi, chunks=None, pspool=None):
            T = G.T
            for c in (range(G.nch) if chunks is None else chunks):
                xnt, xnk = xnl.pop((G.name, pi, c))
                bk, bkk = ps_next(pspool)
                bkb = bk.bitcast(BF16)
                for k in range(8):
                    tr(bkb[:, k * T:(k + 1) * T], xnt[:T, k * 128:(k + 1) * 128], identb[:T, :T], [xnk, "identb"], [bkk], k == 7)
                dve("tensor_tensor", [bkk, "pre_gT"], [gk(G, "xT", c)], out=G.xT[:, :, c * T:(c + 1) * T],
                    in0=bkb[:, 0:8 * T].rearrange("p (k t) -> p k t", k=8), in1=pre_gT.unsqueeze(2).broadcast_to([128, 8, T]),
                    op=ALU.mult)

        def _stage_P1(G, pi):
            T = G.T
            for c in range(G.nch):
                pb, pbk = pool_next("pbf", pbf)
                dma("pool", "d_" + pbk, pb[:T, :], p_rows(G, pi, c), [], [pbk])
                ptl[(G.name, pi, c)] = (pb, pbk)

        def _stage_P2(G, pi, chunks=None, pspool=None):
            T = G.T
            for c in (range(G.nch) if chunks is None else chunks):
                pb, pbk = ptl.pop((G.name, pi, c))
                bk2, bk2k = ps_next(pspool)
                bk2b = bk2.bitcast(BF16)
                for kk in range(2):
                    tr(bk2b[:, kk * T:(kk + 1) * T], pb[:T, kk * 128:(kk + 1) * 128], identb[:T, :T], [pbk, "identb"], [bk2k], kk == 1)
                ptt, ptk = pool_next("pT", pT)
                dve("tensor_copy", [bk2k], [ptk], out=ptt[:, :, :T], in_=bk2b[:, 0:2 * T].rearrange("p (k t) -> p k t", k=2))
                ptl[(G.name, pi, c, "T")] = (ptt, ptk)

        def is_out_chunk(G, pi, c, kind="v"):
            return G.name == "S" or (pi == NPASS - 1 and c == G.nch - 1)

        bvs = {}

        def bv_begin():
            bvs["s"] = (ring_acquire(), ring_acquire())

        def bv_chunk(G, c, pi, pspool=None):
            (s0, s0k), (s1, s1k) = bvs["s"]
            T = G.T
            if True:
                if True:
                    vf, vk = pool_next("tmp", tmp)
                    stt, sk = stat_next()
                    for half, (sl, slk) in enumerate(((s0, s0k), (s1, s1k))):
                        bk, bkk = ps_next(pspool)
                        for k in range(8):
                            mm(bk[:T, :], G.xT[:, k, c * T:(c + 1) * T], sl[:, k, :], k == 0, k == 7, [gk(G, "xT", c), slk], [bkk], k == 7)
                        act(vf[:T, half * 512:(half + 1) * 512], bk[:T, :], AF.Gelu_apprx_tanh, [bkk], [vk, sk], accum=stt[:T, half:half + 1])
                    act(junk[:T, :], vf[:T, :], AF.Square, [vk], ["m2t0", sk], accum=stt[:T, 2:3], scale=SQ_SCALE)
                    dve("tensor_tensor", [sk], [sk], out=stt[:T, 7:8], in0=stt[:T, 0:1], in1=stt[:T, 1:2], op=ALU.add)
                    dve("tensor_scalar", [sk], [sk], out=stt[:T, 8:9], in0=stt[:T, 7:8], scalar1=1.0 / D, scalar2=None, op0=ALU.mult)
                    dve("tensor_tensor", [sk], [sk], out=stt[:T, 9:10], in0=stt[:T, 8:9], in1=stt[:T, 8:9], op=ALU.mult)
                    dve("tensor_tensor", [sk], [sk], out=stt[:T, 3:4], in0=stt[:T, 2:3], in1=stt[:T, 9:10], op=ALU.subtract)
                    r = newton(stt, sk, T, 3)
                    dve("scalar_tensor_tensor", [vk, sk, "ln_g"], [vk], out=vf[:T, :], in0=vf[:T, :], scalar=stt[:T, 8:9], in1=ln_g_t[:T, :],
                        op0=ALU.subtract, op1=ALU.mult)
                    if is_out_chunk(G, pi, c):
                        dve("scalar_tensor_tensor", [vk, sk, "ln_b"], [vk], out=vf[:T, :], in0=vf[:T, :], scalar=r, in1=ln_b_t[:T, :],
                            op0=ALU.mult, op1=ALU.add)
                        act(G.vn[:T, c, :], vf[:T, :], AF.Copy, [vk], [gk(G, "vn", c)])
                        dst = scv_d[:, :] if G.name == "S" else pcv_d[:, :]
                        store_toks.append(dma("sp", "s_" + vk, dst, vf[:T, :], [vk], []))
                    else:
                        dve("scalar_tensor_tensor", [vk, sk, "ln_b"], [gk(G, "vn", c)], out=G.vn[:T, c, :], in0=vf[:T, :], scalar=r,
                            in1=ln_b_t[:T, :], op0=ALU.mult, op1=ALU.add)

        def bv_all(groups, pi):
            (s0, s0k), (s1, s1k) = ring_acquire(), ring_acquire()
            items = [(G, c) for G in groups for c in range(G.nch)]
            n = len(items)
            assert n <= len(tmp) and n <= 8
            A0, A1, Q, MEAN, MSQ, X, Y, HX, B1 = [lnst[:, 8 * j:8 * j + n] for j in range(9)]
            lk = ["lnst"]
            vfs = []
            for i, (G, c) in enumerate(items):
                T = G.T
                vf, vk = pool_next("tmp", tmp)
                vfs.append((vf, vk))
                for half, (sl, slk) in enumerate(((s0, s0k), (s1, s1k))):
                    bk, bkk = ps_next()
                    for k in range(8):
                        mm(bk[:T, :], G.xT[:, k, c * T:(c + 1) * T], sl[:, k, :], k == 0, k == 7, [gk(G, "xT", c), slk], [bkk], k == 7)
                    act(vf[:T, half * 512:(half + 1) * 512], bk[:T, :], AF.Gelu_apprx_tanh, [bkk], [vk] + lk,
                        accum=lnst[:T, 8 * half + i:8 * half + i + 1])
                act(junk[:T, :], vf[:T, :], AF.Square, [vk], ["m2t0"] + lk, accum=lnst[:T, 16 + i:17 + i], scale=SQ_SCALE)
            ring_release(2)
            dve("tensor_tensor", lk, lk, out=MEAN, in0=A0, in1=A1, op=ALU.add)
            dve("tensor_scalar", lk, lk, out=MEAN, in0=MEAN, scalar1=1.0 / D, scalar2=None, op0=ALU.mult)
            dve("tensor_tensor", lk, lk, out=MSQ, in0=MEAN, in1=MEAN, op=ALU.mult)
            dve("tensor_tensor", lk, lk, out=X, in0=Q, in1=MSQ, op=ALU.subtract)
            dve("tensor_scalar", lk, lk, out=Y.bitcast(I32), in0=X.bitcast(I32), scalar1=-0.5, scalar2=float(MAGIC), op0=ALU.mult, op1=ALU.add)
            dve("tensor_scalar", lk, lk, out=HX, in0=X, scalar1=EPS, scalar2=-0.5, op0=ALU.add, op1=ALU.mult)
            for _ in range(NEWTON_ITERS):
                dve("tensor_tensor", lk, lk, out=B1, in0=Y, in1=Y, op=ALU.mult)
                dve("tensor_tensor", lk, lk, out=B1, in0=B1, in1=HX, op=ALU.mult)
                dve("scalar_tensor_tensor", lk, lk, out=Y, in0=B1, scalar=1.5, in1=Y, op0=ALU.add, op1=ALU.mult)
            for i, (G, c) in enumerate(items):
                T = G.T
                vf, vk = vfs[i]
                mean_i = lnst[:T, 24 + i:25 + i]
                r_i = lnst[:T, 48 + i:49 + i]
                dve("scalar_tensor_tensor", [vk, "ln_g"] + lk, [vk], out=vf[:T, :], in0=vf[:T, :], scalar=mean_i, in1=ln_g_t[:T, :],
                    op0=ALU.subtract, op1=ALU.mult)
                if is_out_chunk(G, pi, c):
                    dve("scalar_tensor_tensor", [vk, "ln_b"] + lk, [vk], out=vf[:T, :], in0=vf[:T, :], scalar=r_i, in1=ln_b_t[:T, :],
                        op0=ALU.mult, op1=ALU.add)
                    dve("tensor_copy", [vk], [gk(G, "vn", c)], out=G.vn[:T, c, :], in_=vf[:T, :])
                    dst = scv_d[:, :] if G.name == "S" else pcv_d[:, :]
                    store_toks.append(dma("sp", "s_" + vk, dst, vf[:T, :], [vk], []))
                else:
                    dve("scalar_tensor_tensor", [vk, "ln_b"] + lk, [gk(G, "vn", c)], out=G.vn[:T, c, :], in0=vf[:T, :], scalar=r_i,
                        in1=ln_b_t[:T, :], op0=ALU.mult, op1=ALU.add)

        def bv_end():
            ring_release(2)

        bxs = {}

        def bx_begin(groups, pi):
            bxs["s"] = (ring_acquire(), ring_acquire())
            for G in groups:
                if G.name == "P" and pi > 0:
                    dve("tensor_copy", [gk(G, "xb", 4)], [gk(G, "xb", 0)], out=G.xb[:, 0, :], in_=G.xb[:, 4, :])

        def bx_chunk(G, c, pi, pspool=None):
            (s0, s0k), (s1, s1k) = bxs["s"]
            T = G.T
            outc = is_out_chunk(G, pi, c, "x")
            if outc:
                xf, xfk = pool_next("tmp", tmp)
            for half, (sl, slk) in enumerate(((s0, s0k), (s1, s1k))):
                bk, bkk = ps_next(pspool)
                for k in range(8):
                    mm(bk[:T, :], G.xT[:, k, c * T:(c + 1) * T], sl[:, k, :], k == 0, k == 7, [gk(G, "xT", c), slk], [bkk], k == 7)
                act(G.xb[:T, G.xo + c, half * 512:(half + 1) * 512], bk[:T, :], AF.Copy, [bkk], [gk(G, "xb", 1 + c)])
                if outc:
                    dve("tensor_copy", [bkk], [xfk], out=xf[:T, half * 512:(half + 1) * 512], in_=bk[:T, :])
            if outc:
                if G.name == "S":
                    store_toks.append(dma("sp", "s_" + xfk, spool_d[:, POOL_BUF - 1, :], xf[:T, :], [xfk], []))
                else:
                    store_toks.append(dma("sp", "s_" + xfk, ppool_d[:, :], xf[128 - POOL_BUF:128, :], [xfk], []))

        def bx_end():
            ring_release(2)

        def fm_proj(groups, func, dst_fn, post_fn=None):
            for half in range(2):
                sl, slk = ring_acquire()
                for j in range(4):
                    jj = half * 4 + j
                    for G in groups:
                        bk, bkk = ps_next()
                        for k in range(8):
                            mm(bk[:, :G.TT], sl[:, k, j * 128:(j + 1) * 128], G.xT[:, k, :], k == 0, k == 7,
                               [slk] + [gk(G, "xT", c) for c in range(G.nch)], [bkk], k == 7)
                        out, okeys = dst_fn(G, jj)
                        act(out, bk[:, :G.TT], func, [bkk], okeys)
                        if post_fn is not None:
                            post_fn(G, jj)
                ring_release(1)

        def _stage_C(groups, mid_hook=None, after_za=None, after_zb=None):
            fm_proj(groups, AF.Gelu_apprx_tanh, lambda G, jj: (G.u[:, jj, :], [gk(G, "u", jj)]))
            if mid_hook is not None:
                mid_hook()
            zt = {}

            def dst_za(G, jj):
                t, tk = pool_next("szt", szt)
                zt[(G.name, jj)] = (t, tk)
                return t[:, :G.TT], [tk]

            def post_za(G, jj):
                t, tk = zt[(G.name, jj)]
                dve("tensor_tensor", [tk, gk(G, "u", jj)], [gk(G, "u", jj)], out=G.u[:, jj, :], in0=G.u[:, jj, :], in1=t[:, :G.TT], op=ALU.mult)

            fm_proj(groups, AF.Silu, dst_za, post_za)
            if after_za is not None:
                after_za()
            fm_proj(groups, AF.Silu, lambda G, jj: (G.szb[:, jj, :], [gk(G, "szb", jj)]))
            if after_zb is not None:
                after_zb()

        def _stage_D(groups, pi):
            for G in groups:
                T = G.T
                for jj in range(8):
                    g = jj // 2
                    cs = slice(jj * 128, (jj + 1) * 128)
                    bk, bkk = ps_next()
                    if G.name == "P":
                        for c in range(4):
                            o = bk[:, c * 128:(c + 1) * 128]
                            last = (c == 3)
                            if pi == 0 and c == 0:
                                mm(o, G.xb[:, 1, cs], pfh_sb[:, g, :], True, False, [gk(G, "xb", 1), "pfh"], [bkk], False)
                                mm(o, G.xb[:, 1, cs], pfl_sb[:, g, :], False, True, [gk(G, "xb", 1), "pfl"], [bkk], last)
                            else:
                                mm(o, G.xb[:, 1 + c, cs], pcur_sb[:, g, :], True, False, [gk(G, "xb", 1 + c), "pcur"], [bkk], False)
                                mm(bk[:, c * 128:c * 128 + 16], G.xb[64:128, c, cs], pprev_sb[64:128, g, :], False, True,
                                   [gk(G, "xb", c), "pprev"], [bkk], last)
                    else:
                        o = bk[:, 0:16]
                        mm(o, st_bf[:, 0, cs], pss_sb[:, 0, g, :], True, False, ["st_bf", "pss"], [bkk], False)
                        mm(o, st_bf[0:112, 1, cs], pss_sb[0:112, 1, g, :], False, False, ["st_bf", "pss"], [bkk], False)
                        mm(o, G.xb[0:16, G.xo, cs], psx_sb[:, g, :], False, True, [gk(G, "xb", 1), "psx"], [bkk], True)
                    dve("tensor_copy", [bkk], [gk(G, "qT", jj)], out=G.qT[:, jj, :], in_=bk[:, :G.TT])

        def _stage_E(groups):
            for G in groups:
                T = G.T
                for h in range(8):
                    bk, bkk = ps_next()
                    if G.name == "P":
                        brhs = bass.AP(bs2, h * 128, [[1024, 2], [0, 4], [1, 128]])
                    else:
                        brhs = bass.AP(bs2, h * 128, [[1024, 2], [0, 16]])
                    mm(bk[:, :G.TT], ones2[:, :], brhs, True, False, ["ones2", "bs2"], [bkk], False)
                    for c in range(G.nch):
                        rhs = trilWT[:, h, :] if G.name == "P" else wd_sb[:, h, :]
                        mm(bk[:, c * T:(c + 1) * T], G.vn[:T, c, h * 128:(h + 1) * 128], rhs, False, c == G.nch - 1,
                           [gk(G, "vn", c), "trilWT", "wd"], [bkk], c == G.nch - 1)
                    dve("tensor_tensor", [bkk, gk(G, "u", h)], [gk(G, "u", h)], out=G.u[:, h, :], in0=bk[:, :G.TT], in1=G.u[:, h, :], op=ALU.mult)

        def _stage_F(groups):
            for G in groups:
                for jd in range(8):
                    g, dd = jd // 2, jd % 2
                    bk, bkk = ps_next()
                    for kk in range(2):
                        mm(bk[:, :G.TT], wpool_sb[:, g, kk, dd * 128:(dd + 1) * 128], G.qT[:, 2 * g + kk, :], kk == 0, kk == 1,
                           ["wpool", gk(G, "qT", 2 * g + kk)], [bkk], kk == 1)
                    dve("scalar_tensor_tensor", [bkk, "pscaleT", gk(G, "szb", jd)], [gk(G, "szb", jd)], out=G.szb[:, jd, :], in0=bk[:, :G.TT],
                        scalar=pscaleT[:, jd:jd + 1], in1=G.szb[:, jd, :], op0=ALU.mult, op1=ALU.mult)

        def fm_mix(groups, src_fn, evac_fn):
            for half in range(2):
                sl, slk = ring_acquire()
                for j in range(4):
                    jj = half * 4 + j
                    for G in groups:
                        bk, bkk = ps_next()
                        for k in range(8):
                            src, skey = src_fn(G, k)
                            mm(bk[:, :G.TT], sl[:, k, j * 128:(j + 1) * 128], src, k == 0, k == 7, [slk, skey], [bkk], k == 7)
                        evac_fn(G, jj, bk, bkk)
                ring_release(1)

        def _stage_G(groups):
            fm_proj(groups, AF.Sigmoid, lambda G, jj: (G.sga[:, jj, :], [gk(G, "sga", jj)]))

            def evac_a(G, jj, bk, bkk):
                dve("tensor_tensor", [bkk, gk(G, "sga", jj)], [gk(G, "sga", jj)], out=G.sga[:, jj, :], in0=bk[:, :G.TT], in1=G.sga[:, jj, :], op=ALU.mult)

            fm_mix(groups, lambda G, k: (G.u[:, k, :], gk(G, "u", k)), evac_a)
            fm_proj(groups, AF.Sigmoid, lambda G, jj: (G.sgb[:, jj, :], [gk(G, "sgb", jj)]))

            def evac_b(G, jj, bk, bkk):
                t, tk = pool_next("m2t", m2t)
                dve("tensor_tensor", [bkk, gk(G, "sgb", jj)], [tk], out=t[:, :G.TT], in0=bk[:, :G.TT], in1=G.sgb[:, jj, :], op=ALU.mult)
                ve("pool" if (POOL_ADDS and jj < 6) else "dve", "tensor_tensor", [tk, gk(G, "sga", jj)], [gk(G, "sgb", jj)], out=G.sgb[:, jj, :],
                   in0=t[:, :G.TT], in1=G.sga[:, jj, :], op=ALU.add)

            fm_mix(groups, lambda G, k: (G.szb[:, k, :], gk(G, "szb", k)), evac_b)

        def _stage_HI(groups, pi, fill_hook=None, drain_hook=None):
            (s0, s0k), (s1, s1k) = ring_acquire(), ring_acquire()
            (g0, g0k), (g1, g1k) = ring_acquire(), ring_acquire()
            items = [(G, c) for G in groups for c in range(G.nch)]
            n = len(items)
            S = [dict() for _ in range(n)]
            PS_P, PS_I = (4, 1), (5, 3)
            addeng = "pool" if POOL_ADDS else "dve"
            nsteps = n + 5
            HB = [(hist[i], f"hist{i}") for i in range(nsteps + 1)]

            def chain(t):
                bt, bk_ = HB[t]
                X, Y, HX, B1 = bt[:, 0:3], bt[:, 4:7], bt[:, 8:11], bt[:, 12:15]
                dve("tensor_scalar", [bk_], [bk_], out=Y.bitcast(I32), in0=X.bitcast(I32), scalar1=-0.5, scalar2=float(MAGIC),
                    op0=ALU.mult, op1=ALU.add)
                dve("tensor_scalar", [bk_], [bk_], out=HX, in0=X, scalar1=EPS, scalar2=-0.5, op0=ALU.add, op1=ALU.mult)
                for _ in range(NEWTON_ITERS):
                    dve("tensor_tensor", [bk_], [bk_], out=B1, in0=Y, in1=Y, op=ALU.mult)
                    dve("tensor_tensor", [bk_], [bk_], out=B1, in0=B1, in1=HX, op=ALU.mult)
                    dve("scalar_tensor_tensor", [bk_], [bk_], out=Y, in0=B1, scalar=1.5, in1=Y, op0=ALU.add, op1=ALU.mult)

            def h_pe(i, t):
                G, c = items[i]; T = G.T; st = S[i]
                d = nxt("psH", 2)
                pd = psd[d]
                keys = [f"ps{2 * d}", f"ps{2 * d + 1}"]
                for half, (sl, slk) in enumerate(((s0, s0k), (s1, s1k))):
                    for k in range(8):
                        mm(pd[:T, half * 512:(half + 1) * 512], G.sgb[:, k, c * T:(c + 1) * T], sl[:, k, :], k == 0, k == 7,
                           [gk(G, "sgb", k), slk], [keys[half]], k == 7)
                nb, nbk = HB[t + 1]
                act(junk[:T, :], pd[:T, :], AF.Square, keys, ["m2t0", nbk], accum=nb[:T, 0:1], scale=SQ_SCALE)
                st["pd"], st["pkeys"] = pd, keys

            def h_d1(i, t):
                G, c = items[i]; T = G.T; st = S[i]
                bt, btk = HB[t]
                ht, hk = pool_next("hres", hres)
                st["ht"], st["hk"] = ht, hk
                dma("sp", "d_" + hk, ht[:T, :], x_rows(G, pi, c), [], [hk])
                t1, t1k = pool_next("tmp", tmp)
                dve("scalar_tensor_tensor", st["pkeys"] + [btk, "post_g"], [t1k], out=t1[:T, :], in0=st["pd"][:T, :], scalar=bt[:T, 4:5],
                    in1=post_g_t[:T, :], op0=ALU.mult, op1=ALU.mult)
                dve("tensor_tensor", [t1k, hk], [hk], out=ht[:T, :], in0=ht[:T, :], in1=t1[:T, :], op=ALU.add)

            def h_d1b(i, t):
                G, c = items[i]; T = G.T; st = S[i]
                ht, hk = st["ht"], st["hk"]
                nb, nbk = HB[t + 1]
                act(junk[:T, :], ht[:T, :], AF.Square, [hk], ["m2t0", nbk], accum=nb[:T, 1:2], scale=SQ_SCALE)

            def h_d2(i, t):
                G, c = items[i]; T = G.T; st = S[i]
                bt, btk = HB[t]
                xnt, xnk = pool_next("xn", xn)
                act(xnt[:T, :], st["ht"][:T, :], AF.Copy, [st["hk"], btk], [xnk], scale=bt[:T, 5:6])
                st["xnt"], st["xnk"] = xnt, xnk

            def i_prep(i, t):
                G, c = items[i]; T = G.T; st = S[i]
                xnt, xnk = st["xnt"], st["xnk"]
                bk, bkk = ps_next(PS_P)
                bkb = bk.bitcast(BF16)
                for k in range(8):
                    tr(bkb[:, k * T:(k + 1) * T], xnt[:T, k * 128:(k + 1) * 128], identb[:T, :T], [xnk, "identb"], [bkk], k == 7)
                x2, x2k = pool_next("xn2T", xn2T)
                dve("tensor_tensor", [bkk, "ple_in_gT"], [x2k], out=x2[:, :, :T], in0=bkb[:, 0:8 * T].rearrange("p (k t) -> p k t", k=8),
                    in1=ple_in_gT.unsqueeze(2).broadcast_to([128, 8, T]), op=ALU.mult)
                st["x2"], st["x2k"] = x2, x2k

            def i_half(i, half):
                G, c = items[i]; T = G.T; st = S[i]
                x2, x2k, gt, gtk = st["x2"], st["x2k"], st["gt"], st["gtk"]
                ptt, ptk = st["ptt"], st["ptk"]
                sl, slk = ((g0, g0k), (g1, g1k))[half]
                hs = slice(half * 512, (half + 1) * 512)
                bg, bgk = ps_next(PS_I)
                for k in range(8):
                    mm(bg[:T, :], x2[:, k, :T], sl[:, k, :], k == 0, k == 7, [x2k, slk], [bgk], k == 7)
                act(gt[:T, hs], bg[:T, :], AF.Sigmoid, [bgk], [gtk])
                be, bek = ps_next(PS_I)
                for kk in range(2):
                    mm(be[:T, :], ptt[:, kk, :T], wple_sb[:, kk, hs], kk == 0, kk == 1, [ptk, "wple"], [bek], kk == 1)
                dve("tensor_tensor", [bek, gtk], [gtk], out=gt[:T, hs], in0=be[:T, :], in1=gt[:T, hs], op=ALU.mult)

            def i_pe_a(i, t):
                G, c = items[i]; st = S[i]
                st["ptt"], st["ptk"] = ptl.pop((G.name, pi, c, "T"))
                st["gt"], st["gtk"] = pool_next("tmp", tmp)
                i_half(i, 0)

            def i_pe_b(i, t):
                G, c = items[i]; T = G.T; st = S[i]
                i_half(i, 1)
                nb, nbk = HB[t + 1]
                act(junk[:T, :], st["gt"][:T, :], AF.Square, [st["gtk"]], ["m2t0", nbk], accum=nb[:T, 2:3], scale=SQ_SCALE)

            def i_fin(i, t):
                G, c = items[i]; T = G.T; st = S[i]
                bt, btk = HB[t]
                gt, gtk = st["gt"], st["gtk"]
                last = (pi == NPASS - 1 and i == n - 1)
                if last:
                    dve("scalar_tensor_tensor", [gtk, btk, "ple_out_g"], [gtk], out=gt[:T, :], in0=gt[:T, :], scalar=bt[:T, 6:7],
                        in1=ple_out_g_t[:T, :], op0=ALU.mult, op1=ALU.mult)
                    dve("tensor_tensor", [gtk, st["hk"]], [gtk], out=gt[:T, :], in0=gt[:T, :], in1=st["ht"][:T, :], op=ALU.add)
                elif POOL_ADDS:
                    act(gt[:T, :], gt[:T, :], AF.Copy, [gtk, btk], [gtk], scale=bt[:T, 6:7])
                    ve("pool", "tensor_tensor", [gtk, "ple_out_g"], [gtk], out=gt[:T, :], in0=gt[:T, :], in1=ple_out_g_t[:T, :], op=ALU.mult)
                else:
                    dve("scalar_tensor_tensor", [gtk, btk, "ple_out_g"], [gtk], out=gt[:T, :], in0=gt[:T, :], scalar=bt[:T, 6:7],
                        in1=ple_out_g_t[:T, :], op0=ALU.mult, op1=ALU.mult)
                if not last:
                    ve(addeng, "tensor_tensor", [gtk, st["hk"]], [gtk], out=gt[:T, :], in0=gt[:T, :], in1=st["ht"][:T, :], op=ALU.add)
                store_toks.append(dma("sp", "s_" + gtk, y_rows(G, pi, c), gt[:T, :], [gtk], []))

            seq = [(h_pe, 0), (None, None), (h_d1, 1), (h_d2, 2), (i_fin, 5), (i_prep, 3), (i_pe_a, 4), (h_d1b, 1), (i_pe_b, 4)]
            uses_chain = (1, 2, 5)
            for t in range(nsteps):
                for fn, lag in seq:
                    if fn is None:
                        if any(0 <= t - sj < n for sj in uses_chain):
                            chain(t)
                        continue
                    i = t - lag
                    if 0 <= i < n:
                        fn(i, t)
                if t < n and fill_hook is not None:
                    fill_hook(t)
                if t >= n and drain_hook is not None:
                    drain_hook(t - n)
                if t == n - 1:
                    ring_release(2)
                if t == n + 3:
                    ring_release(2)
            if drain_hook is not None:
                drain_hook(None)

        def _wrap(fn, label):
            def w(*a, **k):
                old = p.stage
                p.stage = label
                try:
                    return fn(*a, **k)
                finally:
                    p.stage = old
            return w

        stage_A1 = _wrap(_stage_A1, "A1")
        stage_A2 = _wrap(_stage_A2, "A2")
        stage_P1 = _wrap(_stage_P1, "P1")
        stage_P2 = _wrap(_stage_P2, "P2")
        stage_C = _wrap(_stage_C, "C")
        stage_D = _wrap(_stage_D, "D")
        stage_E = _wrap(_stage_E, "E")
        stage_F = _wrap(_stage_F, "F")
        stage_G = _wrap(_stage_G, "G")
        stage_HI = _wrap(_stage_HI, "HI")

        def groups_of(pi):
            return [GP] + ([GS] if pi == SAMPLE_PASS else [])

        def schedule():
            setup_early()
            p.stage = "A1"
            _stage_A1_batched(groups_of(0), 0)
            setup_ln()
            for G in groups_of(0):
                stage_A2(G, 0)
            p.stage = "B"
            bx_begin(groups_of(0), 0)
            for G in groups_of(0):
                for c in range(G.nch):
                    bx_chunk(G, c, 0)
            bx_end()
            setup_late()
            setup_stage_dma()
            for pi in range(NPASS):
                groups = groups_of(pi)
                for G in groups:
                    stage_P1(G, pi)
                stage_D(groups, pi)
                p.stage = "Bv"
                bv_all(groups, pi)

                def mid(pi=pi):
                    if pi == 0:
                        setup_late2()
                        setup_compute()

                def after_za(groups=groups, pi=pi):
                    stage_E(groups)
                    if pi + 1 < NPASS:
                        for G in groups_of(pi + 1):
                            stage_A1(G, pi + 1)

                def after_zb(groups=groups, pi=pi):
                    if pi == 0:
                        setup_late3()
                    stage_F(groups)

                stage_C(groups, mid, after_za=after_za, after_zb=after_zb)
                stage_G(groups)
                if pi == NPASS - 1:
                    spool_copy()
                nxtg = groups_of(pi + 1) if pi + 1 < NPASS else []
                items_next = [(G, c) for G in nxtg for c in range(G.nch)]
                hi_items = [(G, c) for G in groups for c in range(G.nch)]
                st = {"k": 0, "begun": False}

                def fill(t, pi=pi, hi_items=hi_items, items_next=items_next):
                    G, c = hi_items[t]
                    stage_P2(G, pi, [c], (5, 3))
                    if t < len(items_next):
                        Gn, cn = items_next[t]
                        stage_A2(Gn, pi + 1, [cn], (5, 3))

                def drain(j, pi=pi, nxtg=nxtg, items_next=items_next, st=st):
                    if not items_next:
                        return
                    old = p.stage
                    p.stage = "B"
                    if j is not None and j < 1:
                        p.stage = old
                        return
                    if not st["begun"]:
                        bx_begin(nxtg, pi + 1)
                        st["begun"] = True
                    todo = len(items_next) - st["k"] if j is None else min(1, len(items_next) - st["k"])
                    for _ in range(todo):
                        Gn, cn = items_next[st["k"]]
                        bx_chunk(Gn, cn, pi + 1, (0, 4))
                        st["k"] += 1
                    if j is None:
                        bx_end()
                    p.stage = old

                stage_HI(groups, pi, fill, drain)

        schedule()
        if stop is not None:
            for sname, cnt in list(p.dcount.items()):
                store_toks.append((sname, cnt))
        p.finish("sp", store_toks)
        import os
        if os.environ.get("KDUMP"):
            import json
            json.dump(p.pe_labels, open(os.environ["KDUMP"], "w"))
        p.emit()
    return nc


_NC_CACHE = {}


def kernel(x_prompt, x_sample, state_pool, p_prompt, p_sample, pre_g, w_in, ln_g, ln_b, w_s, b_s, w_pool, pool_scale,
           w_pa, w_pb, w_out, post_g, w_ple, w_pg, ple_in_g, ple_out_g):
    f = lambda a: np.ascontiguousarray(np.asarray(a), dtype=np.float32)
    if "nc" not in _NC_CACHE:
        _NC_CACHE["nc"] = build_program()
    nc = _NC_CACHE["nc"]
    consts = {"c_" + k: v for k, v in _constants().items()}
    shared = {
        "pre_g": f(pre_g[0]).reshape(1, D), "w_in": f(w_in[0]), "ln_g": f(ln_g[0]).reshape(1, D), "ln_b": f(ln_b[0]).reshape(1, D),
        "w_s": f(w_s[0]), "b_s": f(b_s[0]).reshape(1, D), "w_pool": f(w_pool[0]), "pool_scale": f(pool_scale[0]).reshape(1, D),
        "w_pa": f(w_pa[0]), "w_pb": f(w_pb[0]), "w_out": f(w_out[0]), "post_g": f(post_g[0]).reshape(1, D), "w_ple": f(w_ple[0]),
        "w_pg": f(w_pg[0]), "ple_in_g": f(ple_in_g[0]).reshape(1, D), "ple_out_g": f(ple_out_g[0]).reshape(1, D),
    }
    shared.update(consts)
    xp = f(x_prompt); xs = f(x_sample); spl = f(state_pool); pp = f(p_prompt); psm = f(p_sample)
    in_maps = []
    for i in range(NCORES):
        m = dict(shared)
        m["x"] = xp[i]
        m["xs"] = xs[i * NS_TOK:(i + 1) * NS_TOK, 0, :]
        m["sp"] = spl[0, i * NS_TOK:(i + 1) * NS_TOK]
        m["pp"] = pp[0, i]
        m["psm"] = psm[0, i * NS_TOK:(i + 1) * NS_TOK, 0, :]
        in_maps.append({k: np.ascontiguousarray(v) for k, v in m.items()})
    res = run_bass_kernel_spmd(nc, in_maps, core_ids=list(range(NCORES)))
    R = res.results
    y_prompt = np.stack([R[i]["y"] for i in range(NCORES)], axis=0)
    y_sample = np.concatenate([R[i]["ys"] for i in range(NCORES)], axis=0)[:, None, :]
    pcv = np.stack([R[i]["pcv"] for i in range(NCORES)], axis=0)[None]
    ppool = np.stack([R[i]["ppool"] for i in range(NCORES)], axis=0)[None]
    scv = np.concatenate([R[i]["scv"] for i in range(NCORES)], axis=0)[None, :, None, :]
    spool = np.concatenate([R[i]["spool"] for i in range(NCORES)], axis=0)[None]
    return (y_prompt.astype(np.float32), y_sample.astype(np.float32), pcv.astype(np.float32), ppool.astype(np.float32),
            scv.astype(np.float32), spool.astype(np.float32))
```

```python
import contextlib
import numpy as np
import concourse.bass as bass
import concourse.mybir as mybir
from concourse.bass_utils import run_bass_kernel_spmd

F32 = mybir.dt.float32
BF16 = mybir.dt.bfloat16
I32 = mybir.dt.int32
AF = mybir.ActivationFunctionType
ALU = mybir.AluOpType

NCORES = 8
D = 1024
SEQ = 2048
NS_TOK = 16
POOL_BUF = 15
WINDOWS = (2, 4, 8, 16)
EPS = 1e-6
MAGIC = 0x5F3759DF
NPASS = 4
SAMPLE_PASS = 0
NRING = 4
USE_SCRATCH = False
NEWTON_ITERS = 2
POOL_ADDS = True

ENGS = ("pe", "act", "dve", "pool", "sp")


class Prog:
    def __init__(self, nc):
        self.nc = nc
        self.ops = {e: [] for e in ENGS}
        self.cnt = {e: 0 for e in ENGS}
        self.buf = {}
        self.waited = {e: {} for e in ENGS}
        self.dcount = {}
        self.final_tokens = []
        self.stage = "setup"
        self.pe_labels = []

    def _deps(self, eng, reads, writes):
        need = {}

        def add(tok, raw):
            if tok is None:
                return
            s, v = tok
            if s == eng and eng == "pe":
                return
            if need.get(s, 0) < v:
                need[s] = v

        for k in reads:
            b = self.buf.get(k)
            if b is not None:
                add(b[0], True)
                if k.startswith("ps"):
                    for t in b[1]:
                        if t[0] != eng:
                            add(t, False)
        for k in writes:
            b = self.buf.get(k)
            if b is not None:
                add(b[0], False)
                for t in b[1]:
                    add(t, False)
        waits = []
        wd = self.waited[eng]
        for s, v in need.items():
            if wd.get(s, 0) >= v:
                continue
            wd[s] = v
            waits.append((s, v))
        return waits

    def _commit(self, tok, reads, writes):
        for k in reads:
            self.buf.setdefault(k, [None, []])[1].append(tok)
        for k in writes:
            self.buf[k] = [tok, []]

    def op(self, eng, fn, reads=(), writes=(), sig=True):
        waits = self._deps(eng, reads, writes)
        if sig:
            self.cnt[eng] += 1
            tok = (eng, self.cnt[eng])
        else:
            tok = (eng, self.cnt[eng] + 1)
        self.ops[eng].append((fn, waits, eng if sig else None, 1))
        if eng == "pe":
            self.pe_labels.append(self.stage)
        self._commit(tok, reads, writes)
        return tok

    def dma(self, eng, sem, fn, reads=(), writes=()):
        waits = self._deps(eng, reads, writes)
        self.dcount[sem] = self.dcount.get(sem, 0) + 16
        tok = (sem, self.dcount[sem])
        self.ops[eng].append((fn, waits, sem, 16))
        self._commit(tok, reads, writes)
        return tok

    def finish(self, eng, toks):
        self.final_tokens.append((eng, list(toks)))

    def emit(self):
        nc = self.nc
        names = list(ENGS) + sorted(self.dcount.keys())
        with contextlib.ExitStack() as st:
            sems = {n: st.enter_context(nc.semaphore("s_" + n)) for n in names}
            finals = {e: {} for e in ENGS}
            for e, toks in self.final_tokens:
                for (s, v) in toks:
                    finals[e][s] = max(finals[e].get(s, 0), v)

            def run(e):
                def body(engine):
                    for fn, waits, sg, inc in self.ops[e]:
                        for (s, v) in waits:
                            engine.wait_ge(sems[s], v)
                        ins = fn(engine)
                        if sg is not None:
                            ins.then_inc(sems[sg], inc)
                    for s, v in finals[e].items():
                        engine.wait_ge(sems[s], v)
                return body

            with nc.Block() as block:
                block.sync(run("sp"))
                block.scalar(run("act"))
                block.vector(run("dve"))
                block.gpsimd(run("pool"))
                block.tensor(run("pe"))


def _bf16_round(a):
    a = np.ascontiguousarray(a, dtype=np.float32)
    u = a.view(np.uint32).astype(np.uint64)
    r = ((u + 0x7FFF + ((u >> 16) & 1)) & 0xFFFF0000).astype(np.uint32)
    return r.view(np.float32)


def _constants():
    c = {}
    c["ident"] = np.eye(128, dtype=np.float32)
    t = np.arange(128)
    c["trilmask"] = (t[None, :] <= t[:, None]).astype(np.float32)
    pcur = np.zeros((128, 4, 128), np.float32)
    pfirst = np.zeros((128, 4, 128), np.float32)
    pprev = np.zeros((128, 4, 16), np.float32)
    for g, w in enumerate(WINDOWS):
        for tt in range(128):
            for j in range(w):
                s = tt - j
                if s >= 0:
                    pcur[s, g, tt] += 1.0 / w
                    pfirst[s, g, tt] += 1.0 / min(tt + 1, w)
                elif tt < 16:
                    pprev[128 + s, g, tt] += 1.0 / w
            pcur[tt, g, tt] -= 1.0
            pfirst[tt, g, tt] -= 1.0
    c["pcur"] = pcur
    hi = _bf16_round(pfirst)
    c["pfirst_hi"] = hi
    c["pfirst_lo"] = (pfirst - hi).astype(np.float32)
    c["pprev"] = pprev
    pss = np.zeros((128, 2, 4, 16), np.float32)
    psx = np.zeros((16, 4, 16), np.float32)
    for g, w in enumerate(WINDOWS):
        for b in range(16):
            psx[b, g, b] = 1.0 / w - 1.0
            for r in range(POOL_BUF):
                if r >= 16 - w:
                    row = b * POOL_BUF + r
                    pss[row % 128, row // 128, g, b] = 1.0 / w
    c["pss"] = pss
    c["psx"] = psx
    return c


CONST_SHAPES = {
    "ident": [128, 128], "trilmask": [128, 128], "pcur": [128, 4, 128], "pfirst_hi": [128, 4, 128],
    "pfirst_lo": [128, 4, 128], "pprev": [128, 4, 16], "pss": [128, 2, 4, 16], "psx": [16, 4, 16],
}


class Grp:
    pass


def build_program(stop=None):
    nc = bass.Bass("TRN2", target_bir_lowering=False)

    def din(name, shape):
        return nc.dram_tensor(name, list(shape), F32, kind="ExternalInput").ap()

    def dout(name, shape):
        return nc.dram_tensor(name, list(shape), F32, kind="ExternalOutput").ap()

    x_d = din("x", [SEQ, D]); xs_d = din("xs", [NS_TOK, D]); sp_d = din("sp", [NS_TOK, POOL_BUF, D])
    pp_d = din("pp", [SEQ, 256]); psm_d = din("psm", [NS_TOK, 256])
    pre_g_d = din("pre_g", [1, D]); w_in_d = din("w_in", [D, 7 * D]); ln_g_d = din("ln_g", [1, D]); ln_b_d = din("ln_b", [1, D])
    w_s_d = din("w_s", [8, 128, 128]); b_s_d = din("b_s", [1, D]); w_pool_d = din("w_pool", [4, 256, 256])
    pool_scale_d = din("pool_scale", [1, D]); w_pa_d = din("w_pa", [D, D]); w_pb_d = din("w_pb", [D, D])
    w_out_d = din("w_out", [D, D]); post_g_d = din("post_g", [1, D]); w_ple_d = din("w_ple", [256, D])
    w_pg_d = din("w_pg", [D, D]); ple_in_g_d = din("ple_in_g", [1, D]); ple_out_g_d = din("ple_out_g", [1, D])
    cst = {k: din("c_" + k, v) for k, v in CONST_SHAPES.items()}

    y_d = dout("y", [SEQ, D]); ys_d = dout("ys", [NS_TOK, D]); pcv_d = dout("pcv", [128, D]); ppool_d = dout("ppool", [POOL_BUF, D])
    scv_d = dout("scv", [NS_TOK, D]); spool_d = dout("spool", [NS_TOK, POOL_BUF, D])

    with contextlib.ExitStack() as stack:
        def sb(name, shape, dt):
            return stack.enter_context(nc.sbuf_tensor(name, list(shape), dt))

        ring = [sb(f"ring{i}", [128, 8, 512], BF16) for i in range(NRING)]
        wpool_sb = sb("wpool_sb", [128, 4, 2, 256], BF16)
        wple_sb = sb("wple_sb", [128, 2, 1024], BF16)
        trilWT = sb("trilWT", [128, 8, 128], BF16)
        identb = sb("identb", [128, 128], BF16)
        pcur_sb = sb("pcur_sb", [128, 4, 128], BF16)
        pfh_sb = sb("pfh_sb", [128, 4, 128], BF16)
        pfl_sb = sb("pfl_sb", [128, 4, 128], BF16)
        pprev_sb = sb("pprev_sb", [128, 4, 16], BF16)
        pss_sb = sb("pss_sb", [128, 2, 4, 16], BF16)
        psx_sb = sb("psx_sb", [16, 4, 16], BF16)
        wd_sb = sb("wd_sb", [16, 8, 16], BF16)
        ws00 = sb("ws00", [16, 8], F32)
        bs2 = sb("bs2", [2, 1024], BF16)
        ones2 = sb("ones2", [2, 128], BF16)
        ln_g_t = sb("ln_g_t", [128, D], F32); ln_b_t = sb("ln_b_t", [128, D], F32)
        post_g_t = sb("post_g_t", [128, D], F32); ple_out_g_t = sb("ple_out_g_t", [128, D], F32)
        gvec = sb("gvec", [24, 128], F32)
        identf24 = sb("identf24", [24, 24], F32)
        gT3 = sb("gT3", [128, 24], F32)
        pre_gT, ple_in_gT, pscaleT = gT3[:, 0:8], gT3[:, 8:16], gT3[:, 16:24]
        st_bf = sb("st_bf", [128, 2, 1024], BF16)
        NSTAT = 12
        stats = [sb(f"stat{i}", [128, 16], F32) for i in range(NSTAT)]
        xin = [sb(f"xin{i}", [128, D], F32) for i in range(2)]
        hres = [sb(f"hres{i}", [128, D], F32) for i in range(5)]
        tmp = [sb(f"tmp{i}", [128, D], F32) for i in range(5)]
        xn = [sb(f"xn{i}", [128, D], BF16) for i in range(5)]
        szt = [sb(f"szt{i}", [128, 512], BF16) for i in range(2)]
        m2t = [sb(f"m2t{i}", [128, 512], F32) for i in range(2)]
        junk = m2t[0].bitcast(BF16)
        xn2T = [sb(f"xn2T{i}", [128, 8, 128], BF16) for i in range(2)]
        pbf = [sb(f"pbf{i}", [128, 256], BF16) for i in range(5)]
        pT = [sb(f"pT{i}", [128, 2, 128], BF16) for i in range(5)]

        def mkgrp(name, T, nch):
            G = Grp()
            G.name, G.T, G.nch, G.TT = name, T, nch, T * nch
            G.xT = sb(name + "_xT", [128, 8, G.TT], BF16)
            G.u = sb(name + "_u", [128, 8, G.TT], BF16)
            G.szb = sb(name + "_szb", [128, 8, G.TT], BF16)
            G.sgb = sb(name + "_sgb", [128, 8, G.TT], BF16)
            G.qT = sb(name + "_qT", [128, 8, G.TT], BF16)
            G.sga = G.qT
            G.vn = sb(name + "_vn", [128, nch, D], BF16)
            G.xo = 1 if nch > 1 else 0
            G.xb = sb(name + "_xb", [128, nch + G.xo, D], BF16)
            return G

        GP = mkgrp("P", 128, 4)
        GS = mkgrp("S", NS_TOK, 1)

        psd = [stack.enter_context(nc.psum_tensor(f"psd{i}", [128, 1024], F32)) for i in range(4)]
        psb = [psd[i // 2][:, (i % 2) * 512:(i % 2 + 1) * 512] for i in range(8)]
        hist = [sb(f"hist{i}", [128, 16], F32) for i in range(12)]
        pro = sb("pro", [128, 32], F32)
        lnst = sb("lnst", [128, 72], F32)

        p = Prog(nc)
        rot = {}

        def nxt(name, n):
            i = rot.get(name, 0)
            rot[name] = i + 1
            return i % n

        def ps_next(pool=None):
            if pool is None:
                i = nxt("ps", 8)
            else:
                lo, n = pool
                i = lo + nxt(f"ps_{lo}_{n}", n)
            return psb[i], f"ps{i}"

        def stat_next():
            i = nxt("stat", NSTAT)
            return stats[i], f"stat{i}"

        def pool_next(name, lst):
            i = nxt(name, len(lst))
            return lst[i], f"{name}{i}"

        def mm(out, lhsT, rhs, start, stop, reads, writes, sig):
            p.op("pe", lambda e: e.matmul(out, lhsT=lhsT, rhs=rhs, start=start, stop=stop), reads, writes, sig)

        def tr(out, in_, ident, reads, writes, sig):
            p.op("pe", lambda e: e.transpose(out=out, in_=in_, identity=ident), reads, writes, sig)

        def act(out, in_, func, reads, writes, scale=None, accum=None):
            def fn(e):
                kw = {}
                if scale is not None:
                    kw["scale"] = scale
                if accum is not None:
                    kw["accum_out"] = accum
                return e.activation(out=out, in_=in_, func=func, **kw)
            p.op("act", fn, reads, writes)

        def ve(eng, name, reads, writes, **kw):
            p.op(eng, lambda e: getattr(e, name)(**kw), reads, writes)

        def dve(name, reads, writes, **kw):
            ve("dve", name, reads, writes, **kw)

        def dma(eng, semkey, out, in_, reads, writes, slow=False):
            if slow:
                return p.dma(eng, semkey, lambda e: e.dma_start(out=out, in_=in_, allow_slow_non_contiguous=True), reads, writes)
            return p.dma(eng, semkey, lambda e: e.dma_start(out=out, in_=in_), reads, writes)

        SQ_SCALE = 1.0 / 32.0

        def newton(stt, sk, T, xcol):
            X = stt[:T, xcol:xcol + 1]
            Y = stt[:T, 4:5]
            HX = stt[:T, 5:6]
            B = stt[:T, 6:7]
            dve("tensor_scalar", [sk], [sk], out=Y.bitcast(I32), in0=X.bitcast(I32), scalar1=-0.5, scalar2=float(MAGIC),
                op0=ALU.mult, op1=ALU.add)
            dve("tensor_scalar", [sk], [sk], out=HX, in0=X, scalar1=EPS, scalar2=-0.5, op0=ALU.add, op1=ALU.mult)
            for _ in range(NEWTON_ITERS):
                dve("scalar_tensor_tensor", [sk], [sk], out=B, in0=Y, scalar=HX, in1=Y, op0=ALU.mult, op1=ALU.mult)
                dve("scalar_tensor_tensor", [sk], [sk], out=Y, in0=B, scalar=1.5, in1=Y, op0=ALU.add, op1=ALU.mult)
            return Y

        def rms_rstd(stt, sk, T, nparts):
            assert nparts == 1
            return newton(stt, sk, T, 0)

        def wsrc(w_ap, c0):
            return w_ap.rearrange("(k p) n -> p k n", p=128)[:, :, c0:c0 + 512]

        pass_blocks = []
        for c0 in (3072, 3584, 1024, 1536, 0, 512, 2048, 2560, 4096, 4608, 5120, 5632):
            pass_blocks.append(wsrc(w_in_d, c0))
        pass_blocks += [wsrc(w_pa_d, 0), wsrc(w_pa_d, 512)]
        pass_blocks += [wsrc(w_in_d, 6144), wsrc(w_in_d, 6656)]
        pass_blocks += [wsrc(w_pb_d, 0), wsrc(w_pb_d, 512), wsrc(w_out_d, 0), wsrc(w_out_d, 512), wsrc(w_pg_d, 0), wsrc(w_pg_d, 512)]
        NBLK = len(pass_blocks)
        n_loads = NBLK * NPASS
        wscr = nc.dram_tensor("wscr", [NBLK, 128, 8 * 512], BF16).ap()
        rs = {"issued": 0, "acq": 0, "done": 0}

        def ring_pump(extra_reads=()):
            while rs["issued"] < n_loads and rs["issued"] - rs["done"] < NRING:
                n = rs["issued"]
                s = n % NRING
                if n < NBLK or not USE_SCRATCH:
                    dma("pool", f"d_ring{s}", ring[s][:], pass_blocks[n % NBLK], list(extra_reads), [f"ring{s}"])
                    extra_reads = ()
                else:
                    dma("pool", f"d_ring{s}", ring[s][:].rearrange("p k n -> p (k n)"), wscr[n % NBLK], [f"scr{n % NBLK}"], [f"ring{s}"])
                rs["issued"] += 1

        def ring_acquire():
            n = rs["acq"]
            assert n < rs["issued"], "ring underflow"
            rs["acq"] += 1
            s = n % NRING
            return ring[s], f"ring{s}"

        def ring_release(k=1):
            for _ in range(k):
                n = rs["done"]
                s_ = n % NRING
                if USE_SCRATCH and n < NBLK and NPASS > 1:
                    dma("pool", f"s_scr{s_}", wscr[n], ring[s_][:].rearrange("p k n -> p (k n)"), [f"ring{s_}"], [f"scr{n}"])
                rs["done"] += 1
            ring_pump()

        def cdma(eng, key, out, in_, slow=False):
            return dma(eng, "d_c_" + key, out, in_, [], [key], slow=slow)

        def setup_early():
            cdma("pool", "identb", identb[:], cst["ident"])
            dma("act", "d_c_gvec", gvec[0:8, :], pre_g_d.rearrange("o (k p) -> (o k) p", p=128), [], ["gvec"])
            dma("act", "d_c_gvec", gvec[8:16, :], ple_in_g_d.rearrange("o (k p) -> (o k) p", p=128), [], ["gvec"])
            dma("act", "d_c_gvec", gvec[16:24, :], pool_scale_d.rearrange("o (k p) -> (o k) p", p=128), [], ["gvec"])
            cdma("act", "identf24", identf24[:], cst["ident"][0:24, 0:24])

        def setup_streams(xkeys):
            ring_pump(xkeys)
            cdma("pool", "pcur", pcur_sb[:], cst["pcur"])
            cdma("pool", "pfh", pfh_sb[:], cst["pfirst_hi"])
            cdma("pool", "pfl", pfl_sb[:], cst["pfirst_lo"])
            cdma("pool", "pprev", pprev_sb[:], cst["pprev"])
            cdma("pool", "pss", pss_sb[:], cst["pss"])
            cdma("pool", "psx", psx_sb[:], cst["psx"])
            spf = sp_d.rearrange("b r c -> (b r) c")
            dma("pool", "d_c_st_bf", st_bf[:, 0, :], spf[0:128, :], [], ["st_bf"])
            dma("pool", "d_c_st_bf", st_bf[0:112, 1, :], spf[128:240, :], [], ["st_bf"])

        def setup_early2():
            bk0, bk0k = ps_next()
            tr(bk0[:, 0:24], gvec[:, :], identf24[:, :], ["gvec", "identf24"], [bk0k], True)
            dve("tensor_copy", [bk0k], ["pre_gT", "ple_in_gT", "pscaleT"], out=gT3[:, :], in_=bk0[:, 0:24])
            for i in range(12):
                dve("memset", [], [f"hist{i}"], ap=hist[i][:], constant=1.0)
            dve("memset", [], ["lnst"], ap=lnst[:], constant=1.0)

        def setup_ln():
            cdma("sp", "ln_g", ln_g_t[:], ln_g_d.broadcast_to([128, D]))
            cdma("sp", "ln_b", ln_b_t[:], ln_b_d.broadcast_to([128, D]))

        def setup_late():
            cdma("sp", "ws00", ws00[:], bass.AP(w_s_d.tensor, 0, [[0, 16], [128 * 128, 8]]), slow=True)

        def setup_late2():
            cdma("pool", "wpool", wpool_sb[:], w_pool_d.rearrange("g (kk p) d -> p g kk d", p=128))

        def setup_late3():
            cdma("sp", "post_g", post_g_t[:], post_g_d.broadcast_to([128, D]))
            cdma("sp", "ple_out_g", ple_out_g_t[:], ple_out_g_d.broadcast_to([128, D]))
            cdma("pool", "wple", wple_sb[:], w_ple_d.rearrange("(kk p) d -> p kk d", p=128))

        def spool_copy():
            store_toks.append(dma("sp", "s_spool", spool_d[:, 0:POOL_BUF - 1, :], sp_d[:, 1:POOL_BUF, :], [], []))

        def setup_stage_dma():
            dma("sp", "s_stg0", hres[1][:].rearrange("p (h s) -> p h s", h=8), w_s_d.rearrange("h t s -> t h s"), [], ["hres1"])
            dma("sp", "s_stg1", hres[2][:, 0:128], cst["trilmask"], [], ["hres2"])
            dma("sp", "s_stg1", hres[2][:, 128:256], cst["ident"], [], ["hres2"])
            dma("sp", "s_stg2", hres[3][0:2, :], b_s_d.broadcast_to([2, D]), [], ["hres3"])

        def setup_compute():
            wsb = xn[0]
            dve("tensor_tensor", ["hres1", "hres2"], ["xn0"], out=wsb[:].rearrange("p (h s) -> p h s", h=8),
                in0=hres[1][:].rearrange("p (h s) -> p h s", h=8), in1=hres[2][:, 0:128].unsqueeze(1).broadcast_to([128, 8, 128]), op=ALU.mult)
            bk, bkk = ps_next()
            bkb = bk.bitcast(BF16)
            for h in range(8):
                tr(bkb[:, h * 128:(h + 1) * 128], wsb[:, h * 128:(h + 1) * 128], identb[:], ["xn0", "identb"], [bkk], h == 7)
            dve("tensor_copy", [bkk], ["trilWT"], out=trilWT[:].rearrange("p h t -> p (h t)"), in_=bkb[:, :])
            for h in range(8):
                dve("tensor_scalar", ["hres2", "ws00"], ["wd"], out=wd_sb[:, h, :], in0=hres[2][0:16, 128:144], scalar1=ws00[:, h:h + 1],
                    scalar2=None, op0=ALU.mult)
            bsf, bsh, bshf = hres[3][0:2, :], xn[1][0:2, :], hres[0][0:2, :]
            dve("tensor_copy", ["hres3"], ["xn1"], out=bsh, in_=bsf)
            dve("tensor_copy", ["xn1"], ["hres0"], out=bshf, in_=bsh)
            dve("tensor_tensor", ["hres3", "hres0"], ["hres0"], out=bshf, in0=bsf, in1=bshf, op=ALU.subtract)
            dve("tensor_copy", ["hres0"], ["bs2"], out=bs2[:], in_=bshf)
            dve("tensor_copy", ["xn1", "bs2"], ["bs2"], out=bs2[0:1, :], in_=bsh[0:1, :])
            dve("memset", [], ["ones2"], ap=ones2[:], constant=1.0)

        store_toks = []

        def gk(G, nm, i=None):
            if nm == "sga":
                nm = "qT"
            return f"{G.name}:{nm}" if i is None else f"{G.name}:{nm}.{i}"

        def x_rows(G, pi, c):
            if G.name == "P":
                r0 = (pi * 4 + c) * 128
                return x_d[r0:r0 + 128, :]
            return xs_d[:, :]

        def p_rows(G, pi, c):
            if G.name == "P":
                r0 = (pi * 4 + c) * 128
                return pp_d[r0:r0 + 128, :]
            return psm_d[:, :]

        def y_rows(G, pi, c):
            if G.name == "P":
                r0 = (pi * 4 + c) * 128
                return y_d[r0:r0 + 128, :]
            return ys_d[:, :]

        xnl = {}
        ptl = {}

        def _stage_A1(G, pi):
            T = G.T
            for c in range(G.nch):
                xt, xk = pool_next("xin", xin)
                dma("sp", "d_" + xk, xt[:T, :], x_rows(G, pi, c), [], [xk])
                stt, sk = stat_next()
                act(junk[:T, :], xt[:T, :], AF.Square, [xk], ["m2t0", sk], accum=stt[:T, 0:1], scale=SQ_SCALE)
                r = rms_rstd(stt, sk, T, 1)
                xnt, xnk = pool_next("xn", xn)
                dve("tensor_scalar", [xk, sk], [xnk], out=xnt[:T, :], in0=xt[:T, :], scalar1=r, scalar2=None, op0=ALU.mult)
                xnl[(G.name, pi, c)] = (xnt, xnk)

        def _stage_A1_batched(groups, pi):
            items = [(G, c) for G in groups for c in range(G.nch)]
            assert len(items) <= len(hres)
            dve("memset", [], ["pro"], ap=pro[:], constant=1.0)
            for i, (G, c) in enumerate(items):
                T = G.T
                xt, xk = hres[i], f"hres{i}"
                dma("sp", "d_" + xk, xt[:T, :], x_rows(G, pi, c), [], [xk])
                act(junk[:T, :], xt[:T, :], AF.Square, [xk], ["m2t0", "pro"], accum=pro[:T, i:i + 1], scale=SQ_SCALE)
            nI = len(items)
            X, Y, HX, B1 = pro[:, 0:nI], pro[:, 8:8 + nI], pro[:, 16:16 + nI], pro[:, 24:24 + nI]
            ks = ["pro"]
            dve("tensor_scalar", ks, ks, out=Y.bitcast(I32), in0=X.bitcast(I32), scalar1=-0.5, scalar2=float(MAGIC), op0=ALU.mult, op1=ALU.add)
            dve("tensor_scalar", ks, ks, out=HX, in0=X, scalar1=EPS, scalar2=-0.5, op0=ALU.add, op1=ALU.mult)
            for _ in range(NEWTON_ITERS):
                dve("tensor_tensor", ks, ks, out=B1, in0=Y, in1=Y, op=ALU.mult)
                dve("tensor_tensor", ks, ks, out=B1, in0=B1, in1=HX, op=ALU.mult)
                dve("scalar_tensor_tensor", ks, ks, out=Y, in0=B1, scalar=1.5, in1=Y, op0=ALU.add, op1=ALU.mult)
            for i, (G, c) in enumerate(items):
                T = G.T
                xnt, xnk = pool_next("xn", xn)
                dve("tensor_scalar", [f"hres{i}", "pro"], [xnk], out=xnt[:T, :], in0=hres[i][:T, :], scalar1=pro[:T, 8 + i:9 + i], scalar2=None,
                    op0=ALU.mult)
                xnl[(G.name, pi, c)] = (xnt, xnk)

        def _stage_A2(G, pi, chunks=None, pspool=None):
            T = G.T
            for c in (range(G.nch) if chunks is None else chunks):
                xnt, xnk = xnl.pop((G.name, pi, c))
                bk, bkk = ps_next(pspool)
                bkb = bk.bitcast(BF16)
                for k in range(8):
                    tr(bkb[:, k * T:(k + 1) * T], xnt[:T, k * 128:(k + 1) * 128], identb[:T, :T], [xnk, "identb"], [bkk], k == 7)
                dve("tensor_tensor", [bkk, "pre_gT"], [gk(G, "xT", c)], out=G.xT[:, :, c * T:(c + 1) * T],
                    in0=bkb[:, 0:8 * T].rearrange("p (k t) -> p k t", k=8), in1=pre_gT.unsqueeze(2).broadcast_to([128, 8, T]),
                    op=ALU.mult)

        def _stage_P1(G, pi):
            T = G.T
            for c in range(G.nch):
                pb, pbk = pool_next("pbf", pbf)
                dma("pool", "d_" + pbk, pb[:T, :], p_rows(G, pi, c), [], [pbk])
                ptl[(G.name, pi, c)] = (pb, pbk)

        def _stage_P2(G, pi, chunks=None, pspool=None):
            T = G.T
            for c in (range(G.nch) if chunks is None else chunks):
                pb, pbk = ptl.pop((G.name, pi, c))
                bk2, bk2k = ps_next(pspool)
                bk2b = bk2.bitcast(BF16)
                for kk in range(2):
                    tr(bk2b[:, kk * T:(kk + 1) * T], pb[:T, kk * 128:(kk + 1) * 128], identb[:T, :T], [pbk, "identb"], [bk2k], kk == 1)
                ptt, ptk = pool_next("pT", pT)
                dve("tensor_copy", [bk2k], [ptk], out=ptt[:, :, :T], in_=bk2b[:, 0:2 * T].rearrange("p (k t) -> p k t", k=2))
                ptl[(G.name, pi, c, "T")] = (ptt, ptk)

        def is_out_chunk(G, pi, c, kind="v"):
            return G.name == "S" or (pi == NPASS - 1 and c == G.nch - 1)

        bvs = {}

        def bv_begin():
            bvs["s"] = (ring_acquire(), ring_acquire())

        def bv_chunk(G, c, pi, pspool=None):
            (s0, s0k), (s1, s1k) = bvs["s"]
            T = G.T
            if True:
                if True:
                    vf, vk = pool_next("tmp", tmp)
                    stt, sk = stat_next()
                    for half, (sl, slk) in enumerate(((s0, s0k), (s1, s1k))):
                        bk, bkk = ps_next(pspool)
                        for k in range(8):
                            mm(bk[:T, :], G.xT[:, k, c * T:(c + 1) * T], sl[:, k, :], k == 0, k == 7, [gk(G, "xT", c), slk], [bkk], k == 7)
                        act(vf[:T, half * 512:(half + 1) * 512], bk[:T, :], AF.Gelu_apprx_tanh, [bkk], [vk, sk], accum=stt[:T, half:half + 1])
                    act(junk[:T, :], vf[:T, :], AF.Square, [vk], ["m2t0", sk], accum=stt[:T, 2:3], scale=SQ_SCALE)
                    dve("tensor_tensor", [sk], [sk], out=stt[:T, 7:8], in0=stt[:T, 0:1], in1=stt[:T, 1:2], op=ALU.add)
                    dve("tensor_scalar", [sk], [sk], out=stt[:T, 8:9], in0=stt[:T, 7:8], scalar1=1.0 / D, scalar2=None, op0=ALU.mult)
                    dve("tensor_tensor", [sk], [sk], out=stt[:T, 9:10], in0=stt[:T, 8:9], in1=stt[:T, 8:9], op=ALU.mult)
                    dve("tensor_tensor", [sk], [sk], out=stt[:T, 3:4], in0=stt[:T, 2:3], in1=stt[:T, 9:10], op=ALU.subtract)
                    r = newton(stt, sk, T, 3)
                    dve("scalar_tensor_tensor", [vk, sk, "ln_g"], [vk], out=vf[:T, :], in0=vf[:T, :], scalar=stt[:T, 8:9], in1=ln_g_t[:T, :],
                        op0=ALU.subtract, op1=ALU.mult)
                    if is_out_chunk(G, pi, c):
                        dve("scalar_tensor_tensor", [vk, sk, "ln_b"], [vk], out=vf[:T, :], in0=vf[:T, :], scalar=r, in1=ln_b_t[:T, :],
                            op0=ALU.mult, op1=ALU.add)
                        act(G.vn[:T, c, :], vf[:T, :], AF.Copy, [vk], [gk(G, "vn", c)])
                        dst = scv_d[:, :] if G.name == "S" else pcv_d[:, :]
                        store_toks.append(dma("sp", "s_" + vk, dst, vf[:T, :], [vk], []))
                    else:
                        dve("scalar_tensor_tensor", [vk, sk, "ln_b"], [gk(G, "vn", c)], out=G.vn[:T, c, :], in0=vf[:T, :], scalar=r,
                            in1=ln_b_t[:T, :], op0=ALU.mult, op1=ALU.add)

        def bv_all(groups, pi):
            (s0, s0k), (s1, s1k) = ring_acquire(), ring_acquire()
            items = [(G, c) for G in groups for c in range(G.nch)]
            n = len(items)
            assert n <= len(tmp) and n <= 8
            A0, A1, Q, MEAN, MSQ, X, Y, HX, B1 = [lnst[:, 8 * j:8 * j + n] for j in range(9)]
            lk = ["lnst"]
            vfs = []
            for i, (G, c) in enumerate(items):
                T = G.T
                vf, vk = pool_next("tmp", tmp)
                vfs.append((vf, vk))
                for half, (sl, slk) in enumerate(((s0, s0k), (s1, s1k))):
                    bk, bkk = ps_next()
                    for k in range(8):
                        mm(bk[:T, :], G.xT[:, k, c * T:(c + 1) * T], sl[:, k, :], k == 0, k == 7, [gk(G, "xT", c), slk], [bkk], k == 7)
                    act(vf[:T, half * 512:(half + 1) * 512], bk[:T, :], AF.Gelu_apprx_tanh, [bkk], [vk] + lk,
                        accum=lnst[:T, 8 * half + i:8 * half + i + 1])
                act(junk[:T, :], vf[:T, :], AF.Square, [vk], ["m2t0"] + lk, accum=lnst[:T, 16 + i:17 + i], scale=SQ_SCALE)
            ring_release(2)
            dve("tensor_tensor", lk, lk, out=MEAN, in0=A0, in1=A1, op=ALU.add)
            dve("tensor_scalar", lk, lk, out=MEAN, in0=MEAN, scalar1=1.0 / D, scalar2=None, op0=ALU.mult)
            dve("tensor_tensor", lk, lk, out=MSQ, in0=MEAN, in1=MEAN, op=ALU.mult)
            dve("tensor_tensor", lk, lk, out=X, in0=Q, in1=MSQ, op=ALU.subtract)
            dve("tensor_scalar", lk, lk, out=Y.bitcast(I32), in0=X.bitcast(I32), scalar1=-0.5, scalar2=float(MAGIC), op0=ALU.mult, op1=ALU.add)
            dve("tensor_scalar", lk, lk, out=HX, in0=X, scalar1=EPS, scalar2=-0.5, op0=ALU.add, op1=ALU.mult)
            for _ in range(NEWTON_ITERS):
                dve("tensor_tensor", lk, lk, out=B1, in0=Y, in1=Y, op=ALU.mult)
                dve("tensor_tensor", lk, lk, out=B1, in0=B1, in1=HX, op=ALU.mult)
                dve("scalar_tensor_tensor", lk, lk, out=Y, in0=B1, scalar=1.5, in1=Y, op0=ALU.add, op1=ALU.mult)
            for i, (G, c) in enumerate(items):
                T = G.T
                vf, vk = vfs[i]
                mean_i = lnst[:T, 24 + i:25 + i]
                r_i = lnst[:T, 48 + i:49 + i]
                dve("scalar_tensor_tensor", [vk, "ln_g"] + lk, [vk], out=vf[:T, :], in0=vf[:T, :], scalar=mean_i, in1=ln_g_t[:T, :],
                    op0=ALU.subtract, op1=ALU.mult)
                if is_out_chunk(G, pi, c):
                    dve("scalar_tensor_tensor", [vk, "ln_b"] + lk, [vk], out=vf[:T, :], in0=vf[:T, :], scalar=r_i, in1=ln_b_t[:T, :],
                        op0=ALU.mult, op1=ALU.add)
                    dve("tensor_copy", [vk], [gk(G, "vn", c)], out=G.vn[:T, c, :], in_=vf[:T, :])
                    dst = scv_d[:, :] if G.name == "S" else pcv_d[:, :]
                    store_toks.append(dma("sp", "s_" + vk, dst, vf[:T, :], [vk], []))
                else:
                    dve("scalar_tensor_tensor", [vk, "ln_b"] + lk, [gk(G, "vn", c)], out=G.vn[:T, c, :], in0=vf[:T, :], scalar=r_i,
                        in1=ln_b_t[:T, :], op0=ALU.mult, op1=ALU.add)

        def bv_end():
            ring_release(2)

        bxs = {}

        def bx_begin(groups, pi):
            bxs["s"] = (ring_acquire(), ring_acquire())
            for G in groups:
                if G.name == "P" and pi > 0:
                    dve("tensor_copy", [gk(G, "xb", 4)], [gk(G, "xb", 0)], out=G.xb[:, 0, :], in_=G.xb[:, 4, :])

        def bx_chunk(G, c, pi, pspool=None):
            (s0, s0k), (s1, s1k) = bxs["s"]
            T = G.T
            outc = is_out_chunk(G, pi, c, "x")
            if outc:
                xf, xfk = pool_next("tmp", tmp)
            for half, (sl, slk) in enumerate(((s0, s0k), (s1, s1k))):
                bk, bkk = ps_next(pspool)
                for k in range(8):
                    mm(bk[:T, :], G.xT[:, k, c * T:(c + 1) * T], sl[:, k, :], k == 0, k == 7, [gk(G, "xT", c), slk], [bkk], k == 7)
                act(G.xb[:T, G.xo + c, half * 512:(half + 1) * 512], bk[:T, :], AF.Copy, [bkk], [gk(G, "xb", 1 + c)])
                if outc:
                    dve("tensor_copy", [bkk], [xfk], out=xf[:T, half * 512:(half + 1) * 512], in_=bk[:T, :])
            if outc:
                if G.name == "S":
                    store_toks.append(dma("sp", "s_" + xfk, spool_d[:, POOL_BUF - 1, :], xf[:T, :], [xfk], []))
                else:
                    store_toks.append(dma("sp", "s_" + xfk, ppool_d[:, :], xf[128 - POOL_BUF:128, :], [xfk], []))

        def bx_end():
            ring_release(2)

        def fm_proj(groups, func, dst_fn, post_fn=None):
            for half in range(2):
                sl, slk = ring_acquire()
                for j in range(4):
                    jj = half * 4 + j
                    for G in groups:
                        bk, bkk = ps_next()
                        for k in range(8):
                            mm(bk[:, :G.TT], sl[:, k, j * 128:(j + 1) * 128], G.xT[:, k, :], k == 0, k == 7,
                               [slk] + [gk(G, "xT", c) for c in range(G.nch)], [bkk], k == 7)
                        out, okeys = dst_fn(G, jj)
                        act(out, bk[:, :G.TT], func, [bkk], okeys)
                        if post_fn is not None:
                            post_fn(G, jj)
                ring_release(1)

        def _stage_C(groups, mid_hook=None, after_za=None, after_zb=None):
            fm_proj(groups, AF.Gelu_apprx_tanh, lambda G, jj: (G.u[:, jj, :], [gk(G, "u", jj)]))
            if mid_hook is not None:
                mid_hook()
            zt = {}

            def dst_za(G, jj):
                t, tk = pool_next("szt", szt)
                zt[(G.name, jj)] = (t, tk)
                return t[:, :G.TT], [tk]

            def post_za(G, jj):
                t, tk = zt[(G.name, jj)]
                dve("tensor_tensor", [tk, gk(G, "u", jj)], [gk(G, "u", jj)], out=G.u[:, jj, :], in0=G.u[:, jj, :], in1=t[:, :G.TT], op=ALU.mult)

            fm_proj(groups, AF.Silu, dst_za, post_za)
            if after_za is not None:
                after_za()
            fm_proj(groups, AF.Silu, lambda G, jj: (G.szb[:, jj, :], [gk(G, "szb", jj)]))
            if after_zb is not None:
                after_zb()

        def _stage_D(groups, pi):
            for G in groups:
                T = G.T
                for jj in range(8):
                    g = jj // 2
                    cs = slice(jj * 128, (jj + 1) * 128)
                    bk, bkk = ps_next()
                    if G.name == "P":
                        for c in range(4):
                            o = bk[:, c * 128:(c + 1) * 128]
                            last = (c == 3)
                            if pi == 0 and c == 0:
                                mm(o, G.xb[:, 1, cs], pfh_sb[:, g, :], True, False, [gk(G, "xb", 1), "pfh"], [bkk], False)
                                mm(o, G.xb[:, 1, cs], pfl_sb[:, g, :], False, True, [gk(G, "xb", 1), "pfl"], [bkk], last)
                            else:
                                mm(o, G.xb[:, 1 + c, cs], pcur_sb[:, g, :], True, False, [gk(G, "xb", 1 + c), "pcur"], [bkk], False)
                                mm(bk[:, c * 128:c * 128 + 16], G.xb[64:128, c, cs], pprev_sb[64:128, g, :], False, True,
                                   [gk(G, "xb", c), "pprev"], [bkk], last)
                    else:
                        o = bk[:, 0:16]
                        mm(o, st_bf[:, 0, cs], pss_sb[:, 0, g, :], True, False, ["st_bf", "pss"], [bkk], False)
                        mm(o, st_bf[0:112, 1, cs], pss_sb[0:112, 1, g, :], False, False, ["st_bf", "pss"], [bkk], False)
                        mm(o, G.xb[0:16, G.xo, cs], psx_sb[:, g, :], False, True, [gk(G, "xb", 1), "psx"], [bkk], True)
                    dve("tensor_copy", [bkk], [gk(G, "qT", jj)], out=G.qT[:, jj, :], in_=bk[:, :G.TT])

        def _stage_E(groups):
            for G in groups:
                T = G.T
                for h in range(8):
                    bk, bkk = ps_next()
                    if G.name == "P":
                        brhs = bass.AP(bs2, h * 128, [[1024, 2], [0, 4], [1, 128]])
                    else:
                        brhs = bass.AP(bs2, h * 128, [[1024, 2], [0, 16]])
                    mm(bk[:, :G.TT], ones2[:, :], brhs, True, False, ["ones2", "bs2"], [bkk], False)
                    for c in range(G.nch):
                        rhs = trilWT[:, h, :] if G.name == "P" else wd_sb[:, h, :]
                        mm(bk[:, c * T:(c + 1) * T], G.vn[:T, c, h * 128:(h + 1) * 128], rhs, False, c == G.nch - 1,
                           [gk(G, "vn", c), "trilWT", "wd"], [bkk], c == G.nch - 1)
                    dve("tensor_tensor", [bkk, gk(G, "u", h)], [gk(G, "u", h)], out=G.u[:, h, :], in0=bk[:, :G.TT], in1=G.u[:, h, :], op=ALU.mult)

        def _stage_F(groups):
            for G in groups:
                for jd in range(8):
                    g, dd = jd // 2, jd % 2
                    bk, bkk = ps_next()
                    for kk in range(2):
                        mm(bk[:, :G.TT], wpool_sb[:, g, kk, dd * 128:(dd + 1) * 128], G.qT[:, 2 * g + kk, :], kk == 0, kk == 1,
                           ["wpool", gk(G, "qT", 2 * g + kk)], [bkk], kk == 1)
                    dve("scalar_tensor_tensor", [bkk, "pscaleT", gk(G, "szb", jd)], [gk(G, "szb", jd)], out=G.szb[:, jd, :], in0=bk[:, :G.TT],
                        scalar=pscaleT[:, jd:jd + 1], in1=G.szb[:, jd, :], op0=ALU.mult, op1=ALU.mult)

        def fm_mix(groups, src_fn, evac_fn):
            for half in range(2):
                sl, slk = ring_acquire()
                for j in range(4):
                    jj = half * 4 + j
                    for G in groups:
                        bk, bkk = ps_next()
                        for k in range(8):
                            src, skey = src_fn(G, k)
                            mm(bk[:, :G.TT], sl[:, k, j * 128:(j + 1) * 128], src, k == 0, k == 7, [slk, skey], [bkk], k == 7)
                        evac_fn(G, jj, bk, bkk)
                ring_release(1)

        def _stage_G(groups):
            fm_proj(groups, AF.Sigmoid, lambda G, jj: (G.sga[:, jj, :], [gk(G, "sga", jj)]))

            def evac_a(G, jj, bk, bkk):
                dve("tensor_tensor", [bkk, gk(G, "sga", jj)], [gk(G, "sga", jj)], out=G.sga[:, jj, :], in0=bk[:, :G.TT], in1=G.sga[:, jj, :], op=ALU.mult)

            fm_mix(groups, lambda G, k: (G.u[:, k, :], gk(G, "u", k)), evac_a)
            fm_proj(groups, AF.Sigmoid, lambda G, jj: (G.sgb[:, jj, :], [gk(G, "sgb", jj)]))

            def evac_b(G, jj, bk, bkk):
                t, tk = pool_next("m2t", m2t)
                dve("tensor_tensor", [bkk, gk(G, "sgb", jj)], [tk], out=t[:, :G.TT], in0=bk[:, :G.TT], in1=G.sgb[:, jj, :], op=ALU.mult)
                ve("pool" if (POOL_ADDS and jj < 6) else "dve", "tensor_tensor", [tk, gk(G, "sga", jj)], [gk(G, "sgb", jj)], out=G.sgb[:, jj, :],
                   in0=t[:, :G.TT], in1=G.sga[:, jj, :], op=ALU.add)

            fm_mix(groups, lambda G, k: (G.szb[:, k, :], gk(G, "szb", k)), evac_b)

        def _stage_HI(groups, pi, fill_hook=None, drain_hook=None):
            (s0, s0k), (s1, s1k) = ring_acquire(), ring_acquire()
            (g0, g0k), (g1, g1k) = ring_acquire(), ring_acquire()
            items = [(G, c) for G in groups for c in range(G.nch)]
            n = len(items)
            S = [dict() for _ in range(n)]
            PS_P, PS_I = (4, 1), (5, 3)
            addeng = "pool" if POOL_ADDS else "dve"
            nsteps = n + 5
            HB = [(hist[i], f"hist{i}") for i in range(nsteps + 1)]

            def chain(t):
                bt, bk_ = HB[t]
                X, Y, HX, B1 = bt[:, 0:3], bt[:, 4:7], bt[:, 8:11], bt[:, 12:15]
                dve("tensor_scalar", [bk_], [bk_], out=Y.bitcast(I32), in0=X.bitcast(I32), scalar1=-0.5, scalar2=float(MAGIC),
                    op0=ALU.mult, op1=ALU.add)
                dve("tensor_scalar", [bk_], [bk_], out=HX, in0=X, scalar1=EPS, scalar2=-0.5, op0=ALU.add, op1=ALU.mult)
                for _ in range(NEWTON_ITERS):
                    dve("tensor_tensor", [bk_], [bk_], out=B1, in0=Y, in1=Y, op=ALU.mult)
                    dve("tensor_tensor", [bk_], [bk_], out=B1, in0=B1, in1=HX, op=ALU.mult)
                    dve("scalar_tensor_tensor", [bk_], [bk_], out=Y, in0=B1, scalar=1.5, in1=Y, op0=ALU.add, op1=ALU.mult)

            def h_pe(i, t):
                G, c = items[i]; T = G.T; st = S[i]
                d = nxt("psH", 2)
                pd = psd[d]
                keys = [f"ps{2 * d}", f"ps{2 * d + 1}"]
                for half, (sl, slk) in enumerate(((s0, s0k), (s1, s1k))):
                    for k in range(8):
                        mm(pd[:T, half * 512:(half + 1) * 512], G.sgb[:, k, c * T:(c + 1) * T], sl[:, k, :], k == 0, k == 7,
                           [gk(G, "sgb", k), slk], [keys[half]], k == 7)
                nb, nbk = HB[t + 1]
                act(junk[:T, :], pd[:T, :], AF.Square, keys, ["m2t0", nbk], accum=nb[:T, 0:1], scale=SQ_SCALE)
                st["pd"], st["pkeys"] = pd, keys

            def h_d1(i, t):
                G, c = items[i]; T = G.T; st = S[i]
                bt, btk = HB[t]
                ht, hk = pool_next("hres", hres)
                st["ht"], st["hk"] = ht, hk
                dma("sp", "d_" + hk, ht[:T, :], x_rows(G, pi, c), [], [hk])
                t1, t1k = pool_next("tmp", tmp)
                dve("scalar_tensor_tensor", st["pkeys"] + [btk, "post_g"], [t1k], out=t1[:T, :], in0=st["pd"][:T, :], scalar=bt[:T, 4:5],
                    in1=post_g_t[:T, :], op0=ALU.mult, op1=ALU.mult)
                dve("tensor_tensor", [t1k, hk], [hk], out=ht[:T, :], in0=ht[:T, :], in1=t1[:T, :], op=ALU.add)

            def h_d1b(i, t):
                G, c = items[i]; T = G.T; st = S[i]
                ht, hk = st["ht"], st["hk"]
                nb, nbk = HB[t + 1]
                act(junk[:T, :], ht[:T, :], AF.Square, [hk], ["m2t0", nbk], accum=nb[:T, 1:2], scale=SQ_SCALE)

            def h_d2(i, t):
                G, c = items[i]; T = G.T; st = S[i]
                bt, btk = HB[t]
                xnt, xnk = pool_next("xn", xn)
                act(xnt[:T, :], st["ht"][:T, :], AF.Copy, [st["hk"], btk], [xnk], scale=bt[:T, 5:6])
                st["xnt"], st["xnk"] = xnt, xnk

            def i_prep(i, t):
                G, c = items[i]; T = G.T; st = S[i]
                xnt, xnk = st["xnt"], st["xnk"]
                bk, bkk = ps_next(PS_P)
                bkb = bk.bitcast(BF16)
                for k in range(8):
                    tr(bkb[:, k * T:(k + 1) * T], xnt[:T, k * 128:(k + 1) * 128], identb[:T, :T], [xnk, "identb"], [bkk], k == 7)
                x2, x2k = pool_next("xn2T", xn2T)
                dve("tensor_tensor", [bkk, "ple_in_gT"], [x2k], out=x2[:, :, :T], in0=bkb[:, 0:8 * T].rearrange("p (k t) -> p k t", k=8),
                    in1=ple_in_gT.unsqueeze(2).broadcast_to([128, 8, T]), op=ALU.mult)
                st["x2"], st["x2k"] = x2, x2k

            def i_half(i, half):
                G, c = items[i]; T = G.T; st = S[i]
                x2, x2k, gt, gtk = st["x2"], st["x2k"], st["gt"], st["gtk"]
                ptt, ptk = st["ptt"], st["ptk"]
                sl, slk = ((g0, g0k), (g1, g1k))[half]
                hs = slice(half * 512, (half + 1) * 512)
                bg, bgk = ps_next(PS_I)
                for k in range(8):
                    mm(bg[:T, :], x2[:, k, :T], sl[:, k, :], k == 0, k == 7, [x2k, slk], [bgk], k == 7)
                act(gt[:T, hs], bg[:T, :], AF.Sigmoid, [bgk], [gtk])
                be, bek = ps_next(PS_I)
                for kk in range(2):
                    mm(be[:T, :], ptt[:, kk, :T], wple_sb[:, kk, hs], kk == 0, kk == 1, [ptk, "wple"], [bek], kk == 1)
                dve("tensor_tensor", [bek, gtk], [gtk], out=gt[:T, hs], in0=be[:T, :], in1=gt[:T, hs], op=ALU.mult)

            def i_pe_a(i, t):
                G, c = items[i]; st = S[i]
                st["ptt"], st["ptk"] = ptl.pop((G.name, pi, c, "T"))
                st["gt"], st["gtk"] = pool_next("tmp", tmp)
                i_half(i, 0)

            def i_pe_b(i, t):
                G, c = items[i]; T = G.T; st = S[i]
                i_half(i, 1)
                nb, nbk = HB[t + 1]
                act(junk[:T, :], st["gt"][:T, :], AF.Square, [st["gtk"]], ["m2t0", nbk], accum=nb[:T, 2:3], scale=SQ_SCALE)

            def i_fin(i, t):
                G, c = items[i]; T = G.T; st = S[i]
                bt, btk = HB[t]
                gt, gtk = st["gt"], st["gtk"]
                last = (pi == NPASS - 1 and i == n - 1)
                if last:
                    dve("scalar_tensor_tensor", [gtk, btk, "ple_out_g"], [gtk], out=gt[:T, :], in0=gt[:T, :], scalar=bt[:T, 6:7],
                        in1=ple_out_g_t[:T, :], op0=ALU.mult, op1=ALU.mult)
                    dve("tensor_tensor", [gtk, st["hk"]], [gtk], out=gt[:T, :], in0=gt[:T, :], in1=st["ht"][:T, :], op=ALU.add)
                elif POOL_ADDS:
                    act(gt[:T, :], gt[:T, :], AF.Copy, [gtk, btk], [gtk], scale=bt[:T, 6:7])
                    ve("pool", "tensor_tensor", [gtk, "ple_out_g"], [gtk], out=gt[:T, :], in0=gt[:T, :], in1=ple_out_g_t[:T, :], op=ALU.mult)
                else:
                    dve("scalar_tensor_tensor", [gtk, btk, "ple_out_g"], [gtk], out=gt[:T, :], in0=gt[:T, :], scalar=bt[:T, 6:7],
                        in1=ple_out_g_t[:T, :], op0=ALU.mult, op1=ALU.mult)
                if not last:
                    ve(addeng, "tensor_tensor", [gtk, st["hk"]], [gtk], out=gt[:T, :], in0=gt[:T, :], in1=st["ht"][:T, :], op=ALU.add)
                store_toks.append(dma("sp", "s_" + gtk, y_rows(G, pi, c), gt[:T, :], [gtk], []))

            seq = [(h_pe, 0), (None, None), (h_d1, 1), (h_d2, 2), (i_fin, 5), (i_prep, 3), (i_pe_a, 4), (h_d1b, 1), (i_pe_b, 4)]
            uses_chain = (1, 2, 5)
            for t in range(nsteps):
                for fn, lag in seq:
                    if fn is None:
                        if any(0 <= t - sj < n for sj in uses_chain):
                            chain(t)
                        continue
                    i = t - lag
                    if 0 <= i < n:
                        fn(i, t)
                if t < n and fill_hook is not None:
                    fill_hook(t)
                if t >= n and drain_hook is not None:
                    drain_hook(t - n)
                if t == n - 1:
                    ring_release(2)
                if t == n + 3:
                    ring_release(2)
            if drain_hook is not None:
                drain_hook(None)

        def _wrap(fn, label):
            def w(*a, **k):
                old = p.stage
                p.stage = label
                try:
                    return fn(*a, **k)
                finally:
                    p.stage = old
            return w

        stage_A1 = _wrap(_stage_A1, "A1")
        stage_A2 = _wrap(_stage_A2, "A2")
        stage_P1 = _wrap(_stage_P1, "P1")
        stage_P2 = _wrap(_stage_P2, "P2")
        stage_C = _wrap(_stage_C, "C")
        stage_D = _wrap(_stage_D, "D")
        stage_E = _wrap(_stage_E, "E")
        stage_F = _wrap(_stage_F, "F")
        stage_G = _wrap(_stage_G, "G")
        stage_HI = _wrap(_stage_HI, "HI")

        def groups_of(pi):
            return [GP] + ([GS] if pi == SAMPLE_PASS else [])

        def schedule():
            setup_early()
            setup_early2()
            p.stage = "A1"
            _stage_A1_batched(groups_of(0), 0)
            setup_streams([f"hres{i}" for i in range(sum(G.nch for G in groups_of(0)))])
            setup_ln()
            for G in groups_of(0):
                stage_A2(G, 0)
            p.stage = "B"
            bx_begin(groups_of(0), 0)
            for G in groups_of(0):
                for c in range(G.nch):
                    bx_chunk(G, c, 0)
            bx_end()
            setup_late()
            setup_stage_dma()
            for pi in range(NPASS):
                groups = groups_of(pi)
                for G in groups:
                    stage_P1(G, pi)
                stage_D(groups, pi)
                p.stage = "Bv"
                bv_all(groups, pi)

                def mid(pi=pi):
                    if pi == 0:
                        setup_late2()
                        setup_compute()

                def after_za(groups=groups, pi=pi):
                    stage_E(groups)
                    if pi + 1 < NPASS:
                        for G in groups_of(pi + 1):
                            stage_A1(G, pi + 1)

                def after_zb(groups=groups, pi=pi):
                    if pi == 0:
                        setup_late3()
                    stage_F(groups)

                stage_C(groups, mid, after_za=after_za, after_zb=after_zb)
                stage_G(groups)
                if pi == NPASS - 1:
                    spool_copy()
                nxtg = groups_of(pi + 1) if pi + 1 < NPASS else []
                items_next = [(G, c) for G in nxtg for c in range(G.nch)]
                hi_items = [(G, c) for G in groups for c in range(G.nch)]
                st = {"k": 0, "begun": False}

                def fill(t, pi=pi, hi_items=hi_items, items_next=items_next):
                    G, c = hi_items[t]
                    stage_P2(G, pi, [c], (5, 3))
                    if t < len(items_next):
                        Gn, cn = items_next[t]
                        stage_A2(Gn, pi + 1, [cn], (5, 3))

                def drain(j, pi=pi, nxtg=nxtg, items_next=items_next, st=st):
                    if not items_next:
                        return
                    old = p.stage
                    p.stage = "B"
                    if j is not None and j < 1:
                        p.stage = old
                        return
                    if not st["begun"]:
                        bx_begin(nxtg, pi + 1)
                        st["begun"] = True
                    todo = len(items_next) - st["k"] if j is None else min(1, len(items_next) - st["k"])
                    for _ in range(todo):
                        Gn, cn = items_next[st["k"]]
                        bx_chunk(Gn, cn, pi + 1, (0, 4))
                        st["k"] += 1
                    if j is None:
                        bx_end()
                    p.stage = old

                stage_HI(groups, pi, fill, drain)

        schedule()
        if stop is not None:
            for sname, cnt in list(p.dcount.items()):
                store_toks.append((sname, cnt))
        p.finish("sp", store_toks)
        import os
        if os.environ.get("KDUMP"):
            import json
            json.dump(p.pe_labels, open(os.environ["KDUMP"], "w"))
        p.emit()
    return nc


_NC_CACHE = {}


def kernel(x_prompt, x_sample, state_pool, p_prompt, p_sample, pre_g, w_in, ln_g, ln_b, w_s, b_s, w_pool, pool_scale,
           w_pa, w_pb, w_out, post_g, w_ple, w_pg, ple_in_g, ple_out_g):
    f = lambda a: np.ascontiguousarray(np.asarray(a), dtype=np.float32)
    if "nc" not in _NC_CACHE:
        _NC_CACHE["nc"] = build_program()
    nc = _NC_CACHE["nc"]
    consts = {"c_" + k: v for k, v in _constants().items()}
    shared = {
        "pre_g": f(pre_g[0]).reshape(1, D), "w_in": f(w_in[0]), "ln_g": f(ln_g[0]).reshape(1, D), "ln_b": f(ln_b[0]).reshape(1, D),
        "w_s": f(w_s[0]), "b_s": f(b_s[0]).reshape(1, D), "w_pool": f(w_pool[0]), "pool_scale": f(pool_scale[0]).reshape(1, D),
        "w_pa": f(w_pa[0]), "w_pb": f(w_pb[0]), "w_out": f(w_out[0]), "post_g": f(post_g[0]).reshape(1, D), "w_ple": f(w_ple[0]),
        "w_pg": f(w_pg[0]), "ple_in_g": f(ple_in_g[0]).reshape(1, D), "ple_out_g": f(ple_out_g[0]).reshape(1, D),
    }
    shared.update(consts)
    xp = f(x_prompt); xs = f(x_sample); spl = f(state_pool); pp = f(p_prompt); psm = f(p_sample)
    in_maps = []
    for i in range(NCORES):
        m = dict(shared)
        m["x"] = xp[i]
        m["xs"] = xs[i * NS_TOK:(i + 1) * NS_TOK, 0, :]
        m["sp"] = spl[0, i * NS_TOK:(i + 1) * NS_TOK]
        m["pp"] = pp[0, i]
        m["psm"] = psm[0, i * NS_TOK:(i + 1) * NS_TOK, 0, :]
        in_maps.append({k: np.ascontiguousarray(v) for k, v in m.items()})
    res = run_bass_kernel_spmd(nc, in_maps, core_ids=list(range(NCORES)))
    R = res.results
    y_prompt = np.stack([R[i]["y"] for i in range(NCORES)], axis=0)
    y_sample = np.concatenate([R[i]["ys"] for i in range(NCORES)], axis=0)[:, None, :]
    pcv = np.stack([R[i]["pcv"] for i in range(NCORES)], axis=0)[None]
    ppool = np.stack([R[i]["ppool"] for i in range(NCORES)], axis=0)[None]
    scv = np.concatenate([R[i]["scv"] for i in range(NCORES)], axis=0)[None, :, None, :]
    spool = np.concatenate([R[i]["spool"] for i in range(NCORES)], axis=0)[None]
    return (y_prompt.astype(np.float32), y_sample.astype(np.float32), pcv.astype(np.float32), ppool.astype(np.float32),
            scv.astype(np.float32), spool.astype(np.float32))
```
